# Optimizing a Trainium2 kernel written in Bass

```python
import math
import jax, jax.numpy as jnp
from jax import lax
import numpy as np

D_MODEL = 2048
BATCH = 8
SEQ = 2048
DEPTH = 1
DEC_BATCH = 8
DEC_SEQ = 16
PAST_LEN = 4096

CHUNK = 64
WINDOW = 128
N_HEADS = 16
N_KV_HEADS = 2
HEAD_DIM = 64
GQA_GROUP = N_HEADS // N_KV_HEADS
D_ATTN = N_HEADS * HEAD_DIM
D_KV = N_KV_HEADS * HEAD_DIM
N_BUCKETS = 32
MAX_DISTANCE = 128
SGU_BLOCK = 128
SGU_GROUPS = 8
SGU_GROUP_DIM = 128
D_SGU = SGU_GROUPS * SGU_GROUP_DIM
PEER_HEADS = 8
PEER_N_KEYS = 128
PEER_N_EXPERTS = PEER_N_KEYS * PEER_N_KEYS
PEER_QUERY_DIM = 256
PEER_HALF = PEER_QUERY_DIM // 2
PEER_TOPK = 16
PEER_TOKEN_BLOCK = 128
D_IN = D_ATTN + 2 * D_KV + 2 * D_SGU + 2 * D_MODEL
SPLIT_POINTS = tuple(int(s) for s in np.cumsum([D_ATTN, D_KV, D_KV, D_SGU, D_SGU, D_MODEL]))
EPS = 1e-6
NEG_INF = -1e30

kernel_name = 'hybrid_swa_sgu_peer_stream_step'


def rmsnorm(x, g):
    xf = x.astype(jnp.float32)
    r = lax.rsqrt(jnp.mean(xf * xf, axis=-1, keepdims=True) + EPS)
    return (xf * r * g.astype(jnp.float32)).astype(x.dtype)


def t5_bucket(rel):
    half = N_BUCKETS // 2
    max_exact = half // 2
    ret = jnp.where(rel > 0, half, 0)
    n = jnp.abs(rel)
    nf = jnp.maximum(n, 1).astype(jnp.float32)
    large = max_exact + (jnp.log(nf / max_exact) / math.log(MAX_DISTANCE / max_exact)
                         * (half - max_exact)).astype(jnp.int32)
    large = jnp.minimum(large, half - 1)
    return ret + jnp.where(n < max_exact, n, large)


def rel_bias(table, n_q, n_k):
    rel = jnp.arange(n_k, dtype=jnp.int32)[None, :] - WINDOW - jnp.arange(n_q, dtype=jnp.int32)[:, None]
    b = table[t5_bucket(rel)].astype(jnp.float32)
    return b.transpose(2, 0, 1).reshape(N_KV_HEADS, GQA_GROUP, n_q, n_k)


def attend(q, k, v, bias, mask, sinks):
    s = jnp.einsum('...qkgd,...skd->...kgqs', q, k, preferred_element_type=jnp.float32)
    s = s * (HEAD_DIM ** -0.5) + bias
    if mask is not None:
        s = jnp.where(mask, s, NEG_INF)
    sink = jnp.broadcast_to(sinks.astype(jnp.float32).reshape(N_KV_HEADS, GQA_GROUP, 1, 1),
                            s.shape[:-1] + (1,))
    p = jax.nn.softmax(jnp.concatenate([s, sink], axis=-1), axis=-1)[..., :-1]
    return jnp.einsum('...kgqs,...skd->...qkgd', p.astype(v.dtype), v)


def swa_prompt(q, k, v, bias, sinks):
    B, S = q.shape[:2]
    n_c = S // CHUNK
    n_back = WINDOW // CHUNK
    qb = q.reshape(B, n_c, CHUNK, N_KV_HEADS, GQA_GROUP, HEAD_DIM)
    pad = ((0, 0), (WINDOW, 0), (0, 0), (0, 0))
    kp = jnp.pad(k, pad).reshape(B, n_c + n_back, CHUNK, N_KV_HEADS, HEAD_DIM)
    vp = jnp.pad(v, pad).reshape(B, n_c + n_back, CHUNK, N_KV_HEADS, HEAD_DIM)
    kb = jnp.concatenate([kp[:, i:i + n_c] for i in range(n_back + 1)], axis=2)
    vb = jnp.concatenate([vp[:, i:i + n_c] for i in range(n_back + 1)], axis=2)
    key_pos = (jnp.arange(n_c)[:, None] * CHUNK - WINDOW + jnp.arange(WINDOW + CHUNK)[None, :])
    mask = (key_pos >= 0)[None, :, None, None, None, :]
    o = attend(qb, kb, vb, bias, mask, sinks)
    return o.reshape(B, S, D_ATTN)


def sgu(u, vn, w_s_masked, b_s):
    L = u.shape[2]
    mixed = jnp.einsum('gts,bnsgc->bntgc', w_s_masked[:, :L, :L], vn)
    return u * (mixed + b_s[:, :L].T[:, :, None].astype(mixed.dtype))


def split_in(z, sgu_g):
    lead = z.shape[:-1]
    q, k, v, u, vs, ga, gb = jnp.split(z, SPLIT_POINTS, axis=-1)
    q = q.reshape(*lead, N_KV_HEADS, GQA_GROUP, HEAD_DIM)
    k = k.reshape(*lead, N_KV_HEADS, HEAD_DIM)
    v = v.reshape(*lead, N_KV_HEADS, HEAD_DIM)
    u = jax.nn.gelu(u)
    vs = rmsnorm(jax.nn.gelu(vs), sgu_g)
    return q, k, v, u, vs, ga, gb


def merge(o_attn, s_sgu, ga, gb, w_pa, w_pb, w_out):
    h = jax.nn.sigmoid(ga) * (o_attn @ w_pa) + jax.nn.sigmoid(gb) * (s_sgu @ w_pb)
    return h @ w_out


def peer_tokens(xt, w_query, sub_keys, expert_u, expert_v):
    T = xt.shape[0]
    q = (xt @ w_query).reshape(T, PEER_HEADS, 2, PEER_HALF)
    s = jnp.einsum('thpd,hpnd->thpn', q, sub_keys, preferred_element_type=jnp.float32)
    sv, si = lax.top_k(s, PEER_TOPK)
    cand = sv[:, :, 0, :, None] + sv[:, :, 1, None, :]
    cand_idx = si[:, :, 0, :, None] * PEER_N_KEYS + si[:, :, 1, None, :]
    cv, ci = lax.top_k(cand.reshape(T, PEER_HEADS, PEER_TOPK * PEER_TOPK), PEER_TOPK)
    eidx = jnp.take_along_axis(cand_idx.reshape(T, PEER_HEADS, PEER_TOPK * PEER_TOPK), ci, axis=-1)
    g = jax.nn.softmax(cv, axis=-1)
    ue = expert_u[eidx]
    a = jax.nn.gelu(jnp.einsum('thkd,td->thk', ue, xt, preferred_element_type=jnp.float32))
    h = (g * a).astype(xt.dtype)
    ve = expert_v[eidx]
    return jnp.einsum('thk,thkd->td', h, ve)


def setup_inputs(seed: int = 0) -> dict:
    key = jax.random.key(seed)
    ks = jax.random.split(key, 20)
    nrm = lambda k, shape, scale: jax.random.normal(k, shape, jnp.float32) * scale
    return {
        'x_prompt': nrm(ks[0], (BATCH, SEQ, D_MODEL), 1.0),
        'x_sample': nrm(ks[1], (DEC_BATCH, DEC_SEQ, D_MODEL), 1.0),
        'cache_k_swa': nrm(ks[2], (DEPTH, DEC_BATCH, WINDOW, N_KV_HEADS, HEAD_DIM), 1.0),
        'cache_v_swa': nrm(ks[3], (DEPTH, DEC_BATCH, WINDOW, N_KV_HEADS, HEAD_DIM), 1.0),
        'norm_mix_g': 1.0 + nrm(ks[4], (DEPTH, D_MODEL), 0.02),
        'w_in': nrm(ks[5], (DEPTH, D_MODEL, D_IN), D_MODEL ** -0.5),
        'sgu_norm_g': 1.0 + nrm(ks[6], (DEPTH, D_SGU), 0.02),
        'sgu_w_s': nrm(ks[7], (DEPTH, SGU_GROUPS, SGU_BLOCK, SGU_BLOCK), 0.5 * SGU_BLOCK ** -0.5),
        'sgu_b_s': 1.0 + nrm(ks[8], (DEPTH, SGU_GROUPS, SGU_BLOCK), 0.01),
        'attn_sinks': nrm(ks[9], (DEPTH, N_HEADS), 0.5),
        'rel_bias_table': nrm(ks[10], (N_BUCKETS, N_HEADS), 0.1),
        'w_branch_attn': nrm(ks[11], (DEPTH, D_ATTN, D_MODEL), D_ATTN ** -0.5),
        'w_branch_sgu': nrm(ks[12], (DEPTH, D_SGU, D_MODEL), D_SGU ** -0.5),
        'w_out': nrm(ks[13], (DEPTH, D_MODEL, D_MODEL), D_MODEL ** -0.5),
        'norm_ffn_g': 1.0 + nrm(ks[14], (DEPTH, D_MODEL), 0.02),
        'peer_w_query': nrm(ks[15], (DEPTH, D_MODEL, PEER_HEADS * PEER_QUERY_DIM), D_MODEL ** -0.5),
        'peer_sub_keys': nrm(ks[16], (DEPTH, PEER_HEADS, 2, PEER_N_KEYS, PEER_HALF), PEER_HALF ** -0.5),
        'peer_expert_u': nrm(ks[17], (DEPTH, PEER_N_EXPERTS, D_MODEL), D_MODEL ** -0.5),
        'peer_expert_v': nrm(ks[18], (DEPTH, PEER_N_EXPERTS, D_MODEL), 0.1),
        'norm_final_g': 1.0 + nrm(ks[19], (D_MODEL,), 0.02),
    }


def reference(x_prompt, x_sample, cache_k_swa, cache_v_swa, norm_mix_g, w_in, sgu_norm_g,
              sgu_w_s, sgu_b_s, attn_sinks, rel_bias_table, w_branch_attn, w_branch_sgu,
              w_out, norm_ffn_g, peer_w_query, peer_sub_keys, peer_expert_u, peer_expert_v,
              norm_final_g):
    B, S = x_prompt.shape[:2]
    Bd, T = x_sample.shape[:2]
    tril = jnp.tril(jnp.ones((SGU_BLOCK, SGU_BLOCK), dtype=bool))
    bias_p = rel_bias(rel_bias_table, CHUNK, WINDOW + CHUNK)
    bias_s = rel_bias(rel_bias_table, T, WINDOW + T)

    xp, xs = x_prompt, x_sample
    nk_p, nv_p, nk_s, nv_s, nsgu_s = [], [], [], [], []
    for l in range(DEPTH):
        w_s_masked = jnp.where(tril, sgu_w_s[l], 0.0).astype(sgu_w_s.dtype)
        zp = rmsnorm(xp, norm_mix_g[l]) @ w_in[l]
        qp, kp, vp, up, vsp, gap, gbp = split_in(zp, sgu_norm_g[l])
        op = swa_prompt(qp, kp, vp, bias_p, attn_sinks[l])
        n_blk = S // SGU_BLOCK
        sp = sgu(up.reshape(B, n_blk, SGU_BLOCK, SGU_GROUPS, SGU_GROUP_DIM),
                 vsp.reshape(B, n_blk, SGU_BLOCK, SGU_GROUPS, SGU_GROUP_DIM),
                 w_s_masked, sgu_b_s[l]).reshape(B, S, D_SGU)
        xp = xp + merge(op, sp, gap, gbp, w_branch_attn[l], w_branch_sgu[l], w_out[l])
        zs = rmsnorm(xs, norm_mix_g[l]) @ w_in[l]
        qs, kss, vss, us, vs_sgu, gas, gbs = split_in(zs, sgu_norm_g[l])
        k_all = jnp.concatenate([cache_k_swa[l].astype(kss.dtype), kss], axis=1)
        v_all = jnp.concatenate([cache_v_swa[l].astype(vss.dtype), vss], axis=1)
        os_ = attend(qs, k_all, v_all, bias_s, None, attn_sinks[l]).reshape(Bd, T, D_ATTN)
        ss = sgu(us.reshape(Bd, 1, T, SGU_GROUPS, SGU_GROUP_DIM),
                 vs_sgu.reshape(Bd, 1, T, SGU_GROUPS, SGU_GROUP_DIM),
                 w_s_masked, sgu_b_s[l]).reshape(Bd, T, D_SGU)
        xs = xs + merge(os_, ss, gas, gbs, w_branch_attn[l], w_branch_sgu[l], w_out[l])
        nk_p.append(kp[:, S - WINDOW:])
        nv_p.append(vp[:, S - WINDOW:])
        nk_s.append(kss)
        nv_s.append(vss)
        nsgu_s.append(vs_sgu)
        peer = lambda t: peer_tokens(t, peer_w_query[l], peer_sub_keys[l], peer_expert_u[l], peer_expert_v[l])
        xnp = rmsnorm(xp, norm_ffn_g[l]).reshape(-1, PEER_TOKEN_BLOCK, D_MODEL)
        xp = xp + lax.map(peer, xnp).reshape(B, S, D_MODEL)
        xns = rmsnorm(xs, norm_ffn_g[l]).reshape(Bd * T, D_MODEL)
        xs = xs + peer(xns).reshape(Bd, T, D_MODEL)

    y_prompt = rmsnorm(xp, norm_final_g)
    y_sample = rmsnorm(xs, norm_final_g)
    new_k_swa_prompt = jnp.stack(nk_p)
    new_v_swa_prompt = jnp.stack(nv_p)
    new_k_swa_sample = jnp.stack(nk_s)
    new_v_swa_sample = jnp.stack(nv_s)
    new_sgu_v_sample = jnp.stack(nsgu_s)
    return (y_prompt, y_sample, new_k_swa_prompt, new_v_swa_prompt, new_k_swa_sample, new_v_swa_sample, new_sgu_v_sample)
```

```python
import math
import numpy as np
from contextlib import ExitStack
import concourse.bass as bass
import concourse.mybir as mybir
from concourse.bass_utils import run_bass_kernel_spmd

F32 = mybir.dt.float32
F32R = mybir.dt.float32r
BF16 = mybir.dt.bfloat16
I32 = mybir.dt.int32
U32 = mybir.dt.uint32
AF = mybir.ActivationFunctionType
ALU = mybir.AluOpType
AX = mybir.AxisListType

D = 2048
DIN = 7424
SEQ = 2048
NT = SEQ // 128
G = 2
TMAX = G * 128
EPS = 1e-6
WCOL = 256


class Buf:
    __slots__ = ("name", "lw", "rd", "dsem", "dcnt", "excl")

    def __init__(self, name, excl=False):
        self.name = name
        self.excl = excl
        self.lw = None
        self.rd = {}
        self.dsem = {}
        self.dcnt = {}


class Prog:
    ENG = ("sp", "act", "dve", "pool", "pe")

    def __init__(self, nc, es):
        self.nc = nc
        self.es = es
        self.st = {e: [] for e in self.ENG}
        self.sem = {}
        self.cnt = {}
        self.waited = {e: {} for e in self.ENG}
        self.nsem = 0
        self.out_toks = []
        for e in self.ENG:
            self._new_sem(e)

    def _mk(self, name):
        self.nsem += 1
        return self.es.enter_context(self.nc.semaphore(f"{name}{self.nsem}"))

    def _new_sem(self, e):
        self.sem[e] = self._mk("e" + e)
        self.cnt[e] = 0

    def _wait(self, e, tok):
        sem, val, src = tok
        if src == e and e == "pe":
            return
        w = self.waited[e]
        if w.get(id(sem), -1) >= val:
            return
        w[id(sem)] = val
        self.st[e].append(("w", sem, val))

    def _deps(self, e, reads, writes):
        need = {}
        def add(tok):
            k = id(tok[0])
            if k not in need or need[k][1] < tok[1]:
                need[k] = tok
        for b in reads:
            if b.lw is not None:
                add(b.lw)
            if b.excl:
                for t in b.rd.values():
                    if t[2] != e:
                        add(t)
        for b in writes:
            if b.lw is not None:
                add(b.lw)
            for t in b.rd.values():
                add(t)
        for tok in need.values():
            self._wait(e, tok)

    def _commit(self, tok, reads, writes):
        for b in reads:
            k = id(tok[0])
            if k not in b.rd or b.rd[k][1] < tok[1]:
                b.rd[k] = tok
        for b in writes:
            b.lw = tok
            b.rd = {}

    def op(self, e, fn, reads=(), writes=()):
        self._deps(e, reads, writes)
        if self.cnt[e] >= 30000:
            self._new_sem(e)
        self.cnt[e] += 1
        tok = (self.sem[e], self.cnt[e], e)
        self.st[e].append(("o", fn, self.sem[e], 1))
        self._commit(tok, reads, writes)
        return tok

    def dma(self, q, fn, dbuf, reads=(), writes=(), is_out=False):
        self._deps(q, reads, writes)
        if q not in dbuf.dsem or dbuf.dcnt[q] >= 48000:
            dbuf.dsem[q] = self._mk("d")
            dbuf.dcnt[q] = 0
        dbuf.dcnt[q] += 16
        tok = (dbuf.dsem[q], dbuf.dcnt[q], "dma")
        self.st[q].append(("o", fn, dbuf.dsem[q], 16))
        self._commit(tok, reads, writes)
        if is_out:
            self.out_toks.append(tok)
        return tok

    def finish(self):
        for tok in self.out_toks:
            self._wait("sp", tok)

    def emit(self):
        blk = self.es.enter_context(self.nc.Block())

        def run(e):
            def f(eng):
                for it in self.st[e]:
                    if it[0] == "w":
                        eng.wait_ge(it[1], it[2])
                    else:
                        it[1](eng).then_inc(it[2], it[3])
            return f
        blk.sync(run("sp"))
        blk.scalar(run("act"))
        blk.vector(run("dve"))
        blk.gpsimd(run("pool"))
        blk.tensor(run("pe"))


def build_program(nt=NT, dbg=""):
    SEQ = nt * 128
    nc = bass.Bass("TRN2", target_bir_lowering=False)

    def din(name, shape, dt=F32):
        return nc.dram_tensor(name, list(shape), dt, kind="ExternalInput").ap()

    def dout(name, shape, dt=F32):
        return nc.dram_tensor(name, list(shape), dt, kind="ExternalOutput").ap()

    xp = din("xp", [SEQ, D]); xs = din("xs", [16, D])
    ck = din("ck", [128, 128]); cvv = din("cv", [128, 128])
    w_in = din("w_in", [D, DIN]); w_pa = din("w_pa", [1024, D]); w_pb = din("w_pb", [1024, D])
    w_out = din("w_out", [D, D]); w_q = din("w_q", [D, D])
    eu = din("eu", [16384, D]); ev = din("ev", [16384, D])
    gmixT_d = din("gmixT", [128, 16]); gffnT_d = din("gffnT", [128, 16])
    gffn_d = din("gffn_bc", [128, D]); gfin_d = din("gfin_bc", [128, D]); sgug_d = din("sgug_bc", [128, 1024])
    wsT_d = din("wsT", [128, 8, 128]); trilT_d = din("trilT", [128, 8, 128]); bsT_d = din("bsT", [128, 8])
    sinks_d = din("sinks_bc", [128, 16]); table_d = din("table", [32, 16]); oh_d = din("oh", [32, 32768])
    skT_d = din("skT", [128, 16, 128]); ident_d = din("ident", [128, 128]); iota_d = din("iota16", [128, 16])
    shc_d = din("shc", [128, 2], U32)
    y_p = dout("y_p", [SEQ, D]); y_s = dout("y_s", [16, D])
    nk_p = dout("nk_p", [128, 128]); nv_p = dout("nv_p", [128, 128])
    nk_s = dout("nk_s", [16, 128]); nv_s = dout("nv_s", [16, 128]); nsgu = dout("nsgu", [16, 1024])
    bscr = nc.dram_tensor("bscr", [16, 32768], F32, kind="Internal").ap()
    euvb = nc.dram_tensor("euvb", [16384, 2 * D], BF16, kind="Internal").ap()
    wscr = nc.dram_tensor("wscr", [64, 128, 2048], F32, kind="Internal").ap()

    w_in_v = w_in.rearrange("(kc p) n -> p kc n", p=128)
    w_pa_v = w_pa.rearrange("(kc p) n -> p kc n", p=128)
    w_pb_v = w_pb.rearrange("(kc p) n -> p kc n", p=128)
    w_out_v = w_out.rearrange("(kc p) n -> p kc n", p=128)
    w_q_v = w_q.rearrange("(kc p) n -> p kc n", p=128)

    with ExitStack() as es:
        P = Prog(nc, es)

        def sb(name, shape, dt=F32):
            return es.enter_context(nc.sbuf_tensor("s_" + name, list(shape), dt))

        biasT = sb("biasT", [128, 16, 2, 128]); B_bias = Buf("biasT")
        gffn = sb("gffn", [128, D]); gfin = sb("gfin", [128, D]); sgug = sb("sgug", [128, 1024])
        skT = sb("skT", [128, 16, 128]); wmT = sb("wmT", [128, 8, 128]); trilT = sb("trilT", [128, 8, 128])
        ident = sb("ident", [128, 128]); gmixT = sb("gmixT", [128, 16]); gffnT = sb("gffnT", [128, 16])
        bsT = sb("bsT", [128, 8]); esink = sb("esink", [128, 16]); iota16 = sb("iota16", [128, 16])
        shc = sb("shc", [128, 2], U32); tab = sb("tab", [32, 16])
        B_const = Buf("const")
        X = sb("X", [128, G, D]); B_X = [Buf(f"X{i}") for i in range(G)]
        R1 = sb("R1", [128, 16, TMAX], BF16); B_R1 = [Buf("R1a"), Buf("R1b")]
        QPT = sb("QPT", [128, 16, TMAX]); B_QPT = Buf("QPT")
        R2f = sb("R2", [128, 16 * TMAX]); B_R2 = [Buf("R2a"), Buf("R2b")]
        XNTb = R2f[:, :].bitcast(BF16)[:, 0:16 * TMAX].rearrange("p (k t) -> p k t", k=16)
        XN2T = R2f[:, :].rearrange("p (k t) -> p k t", k=16)
        R4 = sb("R4", [128, 16, TMAX], BF16); B_R4 = [Buf("R4a"), Buf("R4b")]
        R4f = R4[:, :, :].rearrange("p a b -> p (a b)")
        WBraw = [sb(f"WB{i}", [128, 2048]) for i in range(2)]
        WB = [w[:, :].bitcast(BF16).rearrange("p (k n) -> p k n", k=16) for w in WBraw]
        WBf32 = [w[:, :].rearrange("p (k n) -> p k n", k=16) for w in WBraw]
        B_WB = [[Buf(f"WB{i}")] for i in range(2)]
        VEB = sb("VEB", [128, 4, 2048], BF16); B_VE = [Buf(f"VE{i}") for i in range(4)]
        for h_ in range(2):
            vv = VEB[:, 2 * h_:2 * h_ + 2, :].rearrange("p a b -> p (a b)")
            WB.append(vv.rearrange("p (k n) -> p k n", k=16))
            WBf32.append(vv.bitcast(F32).rearrange("p (k n) -> p k n", k=16))
            WBraw.append(vv.bitcast(F32))
            B_WB.append([B_VE[2 * h_], B_VE[2 * h_ + 1]])
        NWB = 4
        R3 = sb("R3", [128, 3, 2048]); B_R3 = [[Buf(f"R3_{j}a"), Buf(f"R3_{j}b")] for j in range(3)]
        XG = sb("XG", [128, 2048]); B_XG = Buf("XG")
        KT = sb("KT", [64, 2, 128 + TMAX], BF16); B_KT = Buf("KT")
        VA = sb("VA", [128, 1 + G, 2, 65]); B_VA = Buf("VA")
        KTOK = sb("KTOK", [128, G, 128]); B_KTOK = Buf("KTOK")
        PT = [sb(f"PT{i}", [128, 2, 2, 128]) for i in range(2)]; B_PT = [Buf(f"PT{i}") for i in range(2)]
        DIAG = [sb(f"DIAG{i}", [128, 128], BF16) for i in range(4)]; B_DIAG = [Buf(f"DG{i}") for i in range(4)]
        small = sb("small", [128, 64]); B_small = Buf("small")
        DEN = sb("DEN", [128, 16]); RDEN = sb("RDEN", [128, 16]); B_DEN = Buf("DEN")
        SV = sb("SV", [128, 16, 16]); SI = sb("SI", [128, 16, 16], U32); SIF = sb("SIF", [128, 16, 16])
        CV = sb("CV", [128, 8, 16]); CI = sb("CI", [128, 8, 16], U32)
        IK = sb("IK", [128, 128], U32); JK = sb("JK", [128, 128], U32)
        IKF = sb("IKF", [128, 8, 16]); JKF = sb("JKF", [128, 8, 16])
        SEL0 = sb("SEL0", [128, 8, 16]); SEL1 = sb("SEL1", [128, 8, 16])
        EIF = sb("EIF", [128, 128]); EIDX = sb("EIDX", [128, 128], I32)
        EW = sb("EW", [128, 8, 16]); GW = sb("GW", [128, 8, 16]); SUMW = sb("SUMW", [128, 8]); RW = sb("RW", [128, 8])
        AA = sb("AA", [128, 128]); AGL = sb("AGL", [128, 128]); HW = sb("HW", [128, 128])
        B_SV = Buf("SV"); B_SI = Buf("SI"); B_SIF = Buf("SIF"); B_CV = Buf("CV"); B_CI = Buf("CI")
        B_IK = Buf("IK"); B_IKF = Buf("IKF"); B_SEL = Buf("SEL"); B_EIDX = Buf("EIDX"); B_GW = Buf("GW")
        B_AA = [Buf(f"AA{i}") for i in range(4)]; B_AGL = [Buf(f"AGL{i}") for i in range(4)]; B_HW = [Buf(f"HW{i}") for i in range(4)]
        PS = [es.enter_context(nc.psum_tensor(f"PS{i}", [128, 512], F32)) for i in range(8)]
        B_PS = [Buf(f"PS{i}", excl=True) for i in range(8)]
        psrr = [0]

        def nps(lo=0, hi=8):
            k = lo + psrr[0] % (hi - lo)
            psrr[0] += 1
            return k

        def scol(k):
            return small[:, k:k + 1]

        def rstd_from(ss_col, out_col, n, rows=128):
            P.op("dve", lambda e: e.tensor_scalar(small[0:rows, out_col:out_col + 1], small[0:rows, ss_col:ss_col + 1],
                                                  1.0 / n, EPS, ALU.mult, ALU.add), [B_small], [B_small])
            P.op("act", lambda e: e.activation(small[0:rows, out_col:out_col + 1], small[0:rows, out_col:out_col + 1], AF.Sqrt),
                 [B_small], [B_small])
            P.op("dve", lambda e: e.reciprocal(small[0:rows, out_col:out_col + 1], small[0:rows, out_col:out_col + 1]),
                 [B_small], [B_small])

        def ld(dst, src, buf=B_const):
            P.dma("sp", lambda e: e.dma_start(out=dst, in_=src), buf, writes=[buf])
        ld(gffn[:], gffn_d); ld(gfin[:], gfin_d); ld(sgug[:], sgug_d); ld(skT[:], skT_d)
        ld(wmT[:], wsT_d); ld(trilT[:], trilT_d); ld(ident[:], ident_d); ld(gmixT[:], gmixT_d); ld(gffnT[:], gffnT_d)
        ld(bsT[:], bsT_d); ld(esink[:], sinks_d); ld(iota16[:], iota_d); ld(shc[:], shc_d); ld(tab[:], table_d)
        P.op("dve", lambda e: e.tensor_tensor(wmT[:], wmT[:], trilT[:], ALU.mult), [B_const], [B_const])
        P.op("act", lambda e: e.activation(esink[:], esink[:], AF.Exp), [B_const], [B_const])
        P.op("dve", lambda e: e.memset(VA[:, :, :, 64:65], 1.0), [], [B_VA])
        for pc in range(8 if "nobias" not in dbg else 0):
            P.dma("sp", lambda e, pc=pc: e.dma_start(out=R3[0:32, 0, :], in_=oh_d[:, pc * 4096:pc * 4096 + 2048]), B_R3[0][0], writes=B_R3[0])
            P.dma("sp", lambda e, pc=pc: e.dma_start(out=R3[0:32, 1, :], in_=oh_d[:, pc * 4096 + 2048:(pc + 1) * 4096]), B_R3[1][0], writes=B_R3[1])
            for hf in range(2):
                for q4 in range(4):
                    k = nps()
                    P.op("pe", lambda e, k=k, hf=hf, q4=q4: e.matmul(PS[k][0:16, :], tab[:, :], R3[0:32, hf, q4 * 512:(q4 + 1) * 512],
                                                                    start=True, stop=True), [B_const] + B_R3[hf], [B_PS[k]])
                    P.op("act", lambda e, k=k, q4=q4: e.activation(XG[0:16, q4 * 512:(q4 + 1) * 512], PS[k][0:16, :], AF.Copy),
                         [B_PS[k]], [B_XG])
                P.dma("sp", lambda e, pc=pc, hf=hf: e.dma_start(out=bscr[:, pc * 4096 + hf * 2048: pc * 4096 + (hf + 1) * 2048], in_=XG[0:16, :]),
                      B_XG, reads=[B_XG], writes=[B_bias])
        if "nobias" not in dbg:
          P.dma("sp", lambda e: e.dma_start(out=biasT[:], in_=bscr.rearrange("h (kb kk qq) -> kk h kb qq", kb=2, kk=128)),
              B_bias, reads=[B_bias], writes=[B_bias])

        B_TUV = Buf("tblUV")
        RC = 2
        conv_chunks = []
        for t_, src in enumerate((eu, ev)):
            srcv = src.rearrange("(p r) d -> p r d", p=128)
            dstv = euvb.rearrange("(p r) (t d) -> p r t d", p=128, t=2)[:, :, t_, :]
            for r0 in range(0, 128, RC):
                def chunk(srcv=srcv, dstv=dstv, r0=r0):
                    b = len(conv_done) % 2
                    conv_done.append(1)
                    stg = VEB[:, 2 * b:2 * b + 2, :]
                    sB = [B_VE[2 * b], B_VE[2 * b + 1]]
                    P.dma("pool", lambda e: e.dma_start(out=stg, in_=srcv[:, r0:r0 + RC, :]), sB[0], writes=sB)
                    P.dma("sp", lambda e: e.dma_start(out=dstv[:, r0:r0 + RC, :], in_=stg), B_TUV, reads=sB, writes=[B_TUV])
                conv_chunks.append(chunk)
        conv_done = []

        def emit_conv(n):
            for _ in range(n):
                if len(conv_done) < len(conv_chunks):
                    conv_chunks[len(conv_done)]()

        wcount = [0]

        B_WSCR = Buf("wscr")
        first_group = [True]

        def load_block(specs, blk):
            nwb = 2 if first_group[0] else NWB
            b = wcount[0] % nwb
            wcount[0] += 1
            if first_group[0]:
                for (dst_fn, src) in specs:
                    P.dma("pool", lambda e, dst_fn=dst_fn, src=src, b=b: e.dma_start(out=dst_fn(b), in_=src),
                          B_WB[b][0], writes=B_WB[b])
                P.dma("sp", lambda e, b=b, blk=blk: e.dma_start(out=wscr[blk], in_=WBraw[b][:, :]), B_WSCR, reads=B_WB[b], writes=[B_WSCR])
            else:
                P.dma("sp", lambda e, b=b, blk=blk: e.dma_start(out=WBraw[b][:, :], in_=wscr[blk]), B_WB[b][0], reads=[B_WSCR], writes=B_WB[b])
            return b

        def run_items(items):
            widx = [k for k, it in enumerate(items) if it[0] is not None]
            bufs = {}
            depth = (2 if first_group[0] else NWB) - 1
            for j0 in range(min(depth, len(widx))):
                bufs[widx[j0]] = load_block(items[widx[j0]][0], j0)
            for k, (w, fn) in enumerate(items):
                if w is not None:
                    j = widx.index(k)
                    if j + depth < len(widx):
                        bufs[widx[j + depth]] = load_block(items[widx[j + depth]][0], j + depth)
                    fn(bufs[k])
                    if first_group[0]:
                        emit_conv(3)
                else:
                    fn(None)
            if first_group[0]:
                emit_conv(len(conv_chunks))

        def full(src_v, c0, ncol=WCOL):
            return [(lambda b: WB[b][:, :, 0:ncol], src_v[:, :, c0:c0 + ncol])]

        def do_group(tiles, sample):
            ng = len(tiles)
            T = ng * 128
            nr = 16 if sample else 128
            XNT = XNTb
            QT = R1

            for i, gt in enumerate(tiles):
                if sample:
                    P.op("dve", lambda e, i=i: e.memset(X[:, i, :], 0.0), [], [B_X[i]])
                    P.dma("sp", lambda e, i=i: e.dma_start(out=X[0:16, i, :], in_=xs), B_X[i], writes=[B_X[i]])
                else:
                    P.dma("sp", lambda e, i=i, gt=gt: e.dma_start(out=X[:, i, :], in_=xp[gt * 128:(gt + 1) * 128, :]), B_X[i], writes=[B_X[i]])
            if sample:
                P.dma("sp", lambda e: e.dma_start(out=KTOK[:, 0, :], in_=ck), B_KTOK, writes=[B_KTOK])
                k = nps()
                P.op("pe", lambda e, k=k: e.transpose(PS[k][:, 0:128], KTOK[:, 0, :], ident[:]), [B_KTOK, B_const], [B_PS[k]])
                P.op("act", lambda e, k=k: e.activation(KT[0:64, 0, 0:128], PS[k][0:64, 0:128], AF.Copy), [B_PS[k]], [B_KT])
                P.op("act", lambda e, k=k: e.activation(KT[0:64, 1, 0:128], PS[k][64:128, 0:128], AF.Copy), [B_PS[k]], [B_KT])
                P.dma("sp", lambda e: e.dma_start(out=VA[:, 0, :, 0:64], in_=cvv.rearrange("p (k d) -> p k d", k=2)), B_VA, writes=[B_VA])
            elif tiles[0] != 0:
                P.op("act", lambda e: e.activation(KT[0:64, :, 0:128], KT[0:64, :, TMAX:TMAX + 128], AF.Copy), [B_KT], [B_KT])
                P.op("dve", lambda e: e.tensor_copy(VA[:, 0, :, 0:64], VA[:, G, :, 0:64]), [B_VA], [B_VA])

            def norm_T(i, gT, dstR, dstB, col):
                XR = R3[:, 2, :]
                P.op("act", lambda e: e.activation(XR, X[:, i, :], AF.Square, accum_out=scol(col)), [B_X[i]], B_R3[2] + [B_small])
                rstd_from(col, col + 8, D)
                P.op("dve", lambda e: e.tensor_scalar(XR, X[:, i, :], scol(col + 8), None, ALU.mult), [B_X[i], B_small], B_R3[2])
                for k4 in range(4):
                    k = nps()
                    for j in range(4):
                        kc = k4 * 4 + j
                        P.op("pe", lambda e, k=k, j=j, kc=kc: e.transpose(PS[k][:, j * 128:(j + 1) * 128], XR[:, kc * 128:(kc + 1) * 128], ident[:]),
                             B_R3[2] + [B_const], [B_PS[k]])
                    P.op("dve", lambda e, k=k, k4=k4: e.tensor_tensor(dstR[:, k4 * 4:(k4 + 1) * 4, i * 128:(i + 1) * 128],
                                                                      PS[k][:, :].rearrange("p (a t) -> p a t", a=4),
                                                                      gT[:, k4 * 4:(k4 + 1) * 4].unsqueeze(2).broadcast_to([128, 4, 128]), ALU.mult),
                         [B_PS[k], B_const], dstB)
            for i in range(ng):
                norm_T(i, gmixT, XNT, B_R2, i)

            items = []

            def q_block(blk):
                def fn(b):
                    for j in range(2):
                        pj = blk * 2 + j
                        k = nps()
                        for kc in range(16):
                            P.op("pe", lambda e, k=k, kc=kc, j=j, b=b: e.matmul(PS[k][:, 0:T], WB[b][:, kc, j * 128:(j + 1) * 128], XNT[:, kc, 0:T],
                                                                               start=(kc == 0), stop=(kc == 15)), B_WB[b] + B_R2, [B_PS[k]])
                        P.op("act", lambda e, k=k, pj=pj: e.activation(QT[0:64, 2 * pj, 0:T], PS[k][0:64, 0:T], AF.Copy, scale=0.125), [B_PS[k]], B_R1)
                        P.op("act", lambda e, k=k, pj=pj: e.activation(QT[0:64, 2 * pj + 1, 0:T], PS[k][64:128, 0:T], AF.Copy, scale=0.125), [B_PS[k]], B_R1)
                return fn
            for blk in range(4):
                items.append((full(w_in_v, blk * 256), q_block(blk)))

            def kv_fn(b):
                for i in range(ng):
                    k = nps()
                    for kc in range(16):
                        P.op("pe", lambda e, k=k, kc=kc, i=i, b=b: e.matmul(PS[k][:, 0:256], XNT[:, kc, i * 128:(i + 1) * 128], WB[b][:, kc, :],
                                                                           start=(kc == 0), stop=(kc == 15)), B_WB[b] + B_R2, [B_PS[k]])
                    P.op("act", lambda e, k=k, i=i: e.activation(KTOK[:, i, :], PS[k][:, 0:128], AF.Copy), [B_PS[k]], [B_KTOK])
                    P.op("dve", lambda e, k=k, i=i: e.tensor_copy(VA[:, 1 + i, :, 0:64], PS[k][:, 128:256].rearrange("p (k d) -> p k d", k=2)),
                         [B_PS[k]], [B_VA])
                    k2 = nps()
                    P.op("pe", lambda e, k2=k2, i=i: e.transpose(PS[k2][:, 0:128], KTOK[:, i, :], ident[:]), [B_KTOK, B_const], [B_PS[k2]])
                    P.op("act", lambda e, k2=k2, i=i: e.activation(KT[0:64, 0, 128 + i * 128:256 + i * 128], PS[k2][0:64, 0:128], AF.Copy), [B_PS[k2]], [B_KT])
                    P.op("act", lambda e, k2=k2, i=i: e.activation(KT[0:64, 1, 128 + i * 128:256 + i * 128], PS[k2][64:128, 0:128], AF.Copy), [B_PS[k2]], [B_KT])
            items.append((full(w_in_v, 1024), kv_fn))

            def uv_block(slot, blk):
                def fn(b):
                    for i in range(ng):
                        k = nps()
                        for kc in range(16):
                            P.op("pe", lambda e, k=k, kc=kc, i=i, b=b: e.matmul(PS[k][:, 0:256], XNT[:, kc, i * 128:(i + 1) * 128], WB[b][:, kc, :],
                                                                               start=(kc == 0), stop=(kc == 15)), B_WB[b] + B_R2, [B_PS[k]])
                        P.op("act", lambda e, k=k, i=i: e.activation(R3[:, slot, i * 1024 + blk * 256: i * 1024 + (blk + 1) * 256], PS[k][:, 0:256],
                                                                     AF.Gelu_apprx_tanh), [B_PS[k]], B_R3[slot])
                return fn
            for blk in range(4):
                items.append((full(w_in_v, 1280 + blk * 256), uv_block(0, blk)))
            for blk in range(4):
                items.append((full(w_in_v, 2304 + blk * 256), uv_block(1, blk)))

            def mixers(_):
                for i in range(ng):
                    VNi = R3[:, 1, i * 1024:(i + 1) * 1024]
                    P.op("act", lambda e, i=i, VNi=VNi: e.activation(R4f[:, 0:1024], VNi, AF.Square, accum_out=scol(16 + i)),
                         B_R3[1], [B_R4[0], B_small])
                    rstd_from(16 + i, 24 + i, 1024)
                    P.op("dve", lambda e, i=i, VNi=VNi: e.scalar_tensor_tensor(out=VNi, in0=VNi, scalar=scol(24 + i), in1=sgug[:],
                                                                              op0=ALU.mult, op1=ALU.mult), B_R3[1] + [B_small, B_const], B_R3[1])
                for i, gt in enumerate(tiles):
                    has_prev = sample or gt != 0
                    ncur = 16 if sample else 128
                    pso = [5, 6, 7]
                    for hp in range(8):
                        s = hp % 2
                        k = nps(0, 5)
                        kbs = ([0] if has_prev else []) + [1]
                        for hh in range(2):
                            h = 2 * hp + hh
                            kv = h // 8
                            for kb in kbs:
                                nk = 128 if kb == 0 else ncur
                                c0 = i * 128 + kb * 128
                                P.op("pe", lambda e, k=k, hh=hh, kb=kb, nk=nk, c0=c0, kv=kv, h=h, i=i: e.matmul(
                                    PS[k][0:nk, (hh * 2 + kb) * 128:(hh * 2 + kb + 1) * 128], KT[0:64, kv, c0:c0 + nk],
                                    QT[0:64, h, i * 128:(i + 1) * 128], start=True, stop=True), [B_KT] + B_R1, [B_PS[k]])
                        for kb in kbs:
                            nk = 128 if kb == 0 else ncur
                            P.op("dve", lambda e, k=k, kb=kb, nk=nk, hp=hp, s=s: e.tensor_tensor(
                                PT[s][0:nk, :, kb, :], PS[k][0:nk, :].rearrange("p (a b q) -> p a b q", a=2, b=2)[:, :, kb, :],
                                biasT[0:nk, 2 * hp:2 * hp + 2, kb, :], ALU.add), [B_PS[k], B_bias], [B_PT[s]])
                            P.op("act", lambda e, kb=kb, nk=nk, s=s: e.activation(PT[s][0:nk, :, kb, :], PT[s][0:nk, :, kb, :], AF.Exp),
                                 [B_PT[s]], [B_PT[s]])
                        if not sample:
                            P.op("dve", lambda e, s=s: e.memset(PT[s][64:128, :, 1, 0:64], 0.0), [], [B_PT[s]])
                            if has_prev:
                                P.op("dve", lambda e, s=s: e.memset(PT[s][0:64, :, 0, 64:128], 0.0), [], [B_PT[s]])
                        for hh in range(2):
                            h = 2 * hp + hh
                            kv = h // 8
                            bk = pso[h // 6]
                            hl = h % 6
                            for n_, kb in enumerate(kbs):
                                nk = 128 if kb == 0 else ncur
                                slot = i + kb
                                P.op("pe", lambda e, bk=bk, hl=hl, s=s, hh=hh, kb=kb, nk=nk, slot=slot, kv=kv, n_=n_, kbs=kbs: e.matmul(
                                    PS[bk][:, hl * 65:(hl + 1) * 65], PT[s][0:nk, hh, kb, :], VA[0:nk, slot, kv, :],
                                    start=(n_ == 0), stop=(n_ == len(kbs) - 1)), [B_PT[s], B_VA], [B_PS[bk]])
                    for b3 in range(3):
                        nh = 6 if b3 < 2 else 4
                        h0 = b3 * 6
                        P.op("dve", lambda e, b3=b3, nh=nh, h0=h0: e.tensor_tensor(
                            DEN[:, h0:h0 + nh], PS[pso[b3]][:, 0:nh * 65].rearrange("p (h c) -> p h c", c=65)[:, :, 64],
                            esink[:, h0:h0 + nh], ALU.add), [B_PS[pso[b3]], B_const], [B_DEN])
                    P.op("dve", lambda e: e.reciprocal(RDEN[:], DEN[:]), [B_DEN], [B_DEN])
                    for h in range(16):
                        bk = pso[h // 6]
                        hl = h % 6
                        P.op("dve", lambda e, bk=bk, hl=hl, h=h, i=i: e.tensor_scalar(
                            R3[:, 2, i * 1024 + h * 64: i * 1024 + (h + 1) * 64], PS[bk][:, hl * 65:hl * 65 + 64], RDEN[:, h:h + 1], None, ALU.mult),
                            [B_PS[bk], B_DEN], B_R3[2])
                for i in range(ng):
                    ks = [nps(), nps()]
                    for g in range(8):
                        k = ks[g // 4]
                        P.op("pe", lambda e, k=k, g=g, i=i: e.matmul(PS[k][:, (g % 4) * 128:(g % 4 + 1) * 128], wmT[:, g, :],
                                                                    R3[:, 1, i * 1024 + g * 128: i * 1024 + (g + 1) * 128], start=True, stop=True),
                             [B_const] + B_R3[1], [B_PS[k]])
                    for g in range(8):
                        k = ks[g // 4]
                        Ug = R3[:, 0, i * 1024 + g * 128: i * 1024 + (g + 1) * 128]
                        P.op("dve", lambda e, k=k, g=g, Ug=Ug: e.scalar_tensor_tensor(out=Ug, in0=PS[k][:, (g % 4) * 128:(g % 4 + 1) * 128],
                                                                                    scalar=bsT[:, g:g + 1], in1=Ug, op0=ALU.add, op1=ALU.mult),
                             [B_PS[k], B_const] + B_R3[0], B_R3[0])
                if sample:
                    P.dma("sp", lambda e: e.dma_start(out=nk_s, in_=KTOK[0:16, 0, :]), B_KTOK, reads=[B_KTOK], is_out=True)
                    P.dma("sp", lambda e: e.dma_start(out=nv_s.rearrange("p (k d) -> p k d", k=2), in_=VA[0:16, 1, :, 0:64]), B_VA, reads=[B_VA], is_out=True)
                    P.dma("sp", lambda e: e.dma_start(out=nsgu, in_=R3[0:16, 1, 0:1024]), B_R3[1][0], reads=B_R3[1], is_out=True)
                elif tiles[-1] == nt - 1:
                    il = ng - 1
                    P.dma("sp", lambda e: e.dma_start(out=nk_p, in_=KTOK[:, il, :]), B_KTOK, reads=[B_KTOK], is_out=True)
                    P.dma("sp", lambda e: e.dma_start(out=nv_p.rearrange("p (k d) -> p k d", k=2), in_=VA[:, 1 + il, :, 0:64]), B_VA, reads=[B_VA], is_out=True)
                for i in range(ng):
                    for src_slot, dst0, dB in ((2, 0, B_R1[0]), (0, 8, B_R1[1])):
                        for f4 in range(2):
                            k = nps()
                            for j in range(4):
                                f = f4 * 4 + j
                                P.op("pe", lambda e, k=k, j=j, f=f, i=i, src_slot=src_slot: e.transpose(
                                    PS[k][:, j * 128:(j + 1) * 128], R3[:, src_slot, i * 1024 + f * 128: i * 1024 + (f + 1) * 128], ident[:]),
                                    B_R3[src_slot] + [B_const], [B_PS[k]])
                            P.op("act", lambda e, k=k, f4=f4, i=i, dst0=dst0: e.activation(
                                R1[:, dst0 + f4 * 4: dst0 + (f4 + 1) * 4, i * 128:(i + 1) * 128], PS[k][:, :].rearrange("p (a t) -> p a t", a=4), AF.Copy),
                                [B_PS[k]], [dB])
            items.append((None, mixers))

            SGA = XG[:, 0:1024]
            SGB = XG[:, 1024:2048]

            def gate_block(which, nb):
                def fn(b):
                    dst = SGA if which == 0 else SGB
                    for i in range(ng):
                        k = nps()
                        for kc in range(16):
                            P.op("pe", lambda e, k=k, kc=kc, i=i, b=b: e.matmul(PS[k][:, 0:256], XNT[:, kc, i * 128:(i + 1) * 128], WB[b][:, kc, :],
                                                                               start=(kc == 0), stop=(kc == 15)), B_WB[b] + B_R2, [B_PS[k]])
                        P.op("act", lambda e, k=k, i=i, dst=dst: e.activation(dst[:, i * 256:(i + 1) * 256], PS[k][:, 0:256], AF.Sigmoid),
                             [B_PS[k]], [B_XG])
                return fn

            def papb_block(nb):
                def fn(b):
                    for i in range(ng):
                        ka = nps()
                        kb_ = nps()
                        for kc in range(8):
                            P.op("pe", lambda e, ka=ka, kc=kc, i=i, b=b: e.matmul(PS[ka][:, 0:256], R1[:, kc, i * 128:(i + 1) * 128], WB[b][:, kc, :],
                                                                                 start=(kc == 0), stop=(kc == 7)), B_WB[b] + B_R1, [B_PS[ka]])
                        for kc in range(8):
                            P.op("pe", lambda e, kb_=kb_, kc=kc, i=i, b=b: e.matmul(PS[kb_][:, 0:256], R1[:, 8 + kc, i * 128:(i + 1) * 128], WB[b][:, 8 + kc, :],
                                                                                   start=(kc == 0), stop=(kc == 7)), B_WB[b] + B_R1, [B_PS[kb_]])
                        sa = SGA[:, i * 256:(i + 1) * 256]
                        sb_ = SGB[:, i * 256:(i + 1) * 256]
                        P.op("dve", lambda e, ka=ka, sa=sa: e.tensor_tensor(sa, sa, PS[ka][:, 0:256], ALU.mult), [B_PS[ka], B_XG], [B_XG])
                        P.op("dve", lambda e, kb_=kb_, sb_=sb_: e.tensor_tensor(sb_, sb_, PS[kb_][:, 0:256], ALU.mult), [B_PS[kb_], B_XG], [B_XG])
                        P.op("dve", lambda e, sa=sa, sb_=sb_: e.tensor_tensor(sa, sa, sb_, ALU.add), [B_XG], [B_XG])
                        k = nps()
                        for j in range(2):
                            P.op("pe", lambda e, k=k, j=j, sa=sa: e.transpose(PS[k][:, j * 128:(j + 1) * 128], sa[:, j * 128:(j + 1) * 128], ident[:]),
                                 [B_XG, B_const], [B_PS[k]])
                        HTv = R4
                        P.op("dve", lambda e, k=k, i=i, HTv=HTv: e.tensor_copy(HTv[:, nb * 2:nb * 2 + 2, i * 128:(i + 1) * 128],
                                                                              PS[k][:, 0:256].rearrange("p (a t) -> p a t", a=2)),
                             [B_PS[k]], B_R4)
                return fn
            for nb in range(8):
                items.append((full(w_in_v, 3328 + nb * 256), gate_block(0, nb)))
                items.append((full(w_in_v, 5376 + nb * 256), gate_block(1, nb)))
                items.append(([(lambda b: WB[b][:, 0:8, :], w_pa_v[:, :, nb * 256:(nb + 1) * 256]),
                               (lambda b: WB[b][:, 8:16, :], w_pb_v[:, :, nb * 256:(nb + 1) * 256])], papb_block(nb)))

            def wout_block(nb):
                def fn(b):
                    HTv = R4
                    for i in range(ng):
                        k = nps()
                        for kc in range(16):
                            P.op("pe", lambda e, k=k, kc=kc, i=i, b=b: e.matmul(PS[k][:, 0:256], HTv[:, kc, i * 128:(i + 1) * 128], WB[b][:, kc, :],
                                                                               start=(kc == 0), stop=(kc == 15)), B_WB[b] + B_R4, [B_PS[k]])
                        xs_ = X[:, i, nb * 256:(nb + 1) * 256]
                        P.op("dve", lambda e, k=k, xs_=xs_: e.tensor_tensor(xs_, xs_, PS[k][:, 0:256], ALU.add), [B_PS[k], B_X[i]], [B_X[i]])
                return fn
            for nb in range(8):
                items.append((full(w_out_v, nb * 256), wout_block(nb)))

            def peer_norm(_):
                for i in range(ng):
                    norm_T(i, gffnT, XNTb, B_R2, 32 + i)
            items.append((None, peer_norm))

            def wq_block(blk):
                def fn(b):
                    for j in range(2):
                        c = blk * 2 + j
                        k = nps()
                        for kc in range(16):
                            P.op("pe", lambda e, k=k, kc=kc, j=j, b=b: e.matmul(PS[k][:, 0:T], WB[b][:, kc, j * 128:(j + 1) * 128], XNTb[:, kc, 0:T],
                                                                               start=(kc == 0), stop=(kc == 15)), B_WB[b] + B_R2, [B_PS[k]])
                        P.op("act", lambda e, k=k, c=c: e.activation(QPT[:, c, 0:T], PS[k][:, 0:T], AF.Copy), [B_PS[k]], [B_QPT])
                return fn
            for blk in range(8):
                items.append((full(w_q_v, blk * 256), wq_block(blk)))

            if "it=" in dbg:
                items = items[:int(dbg.split("it=")[1].split(",")[0])]
            run_items(items)

            for i, gt in enumerate(tiles):
                if "nopeer" in dbg:
                    break
                SC = R3[:, 0, :].rearrange("p (a n) -> p a n", a=16)
                TMP = R3[:, 1, :].rearrange("p (a n) -> p a n", a=16)
                CAND = R3[:, 2, :]
                sks = [nps(0, 4) for _ in range(4)]
                for hp in range(16):
                    k = sks[hp // 4]
                    P.op("pe", lambda e, k=k, hp=hp, i=i: e.matmul(PS[k][:, (hp % 4) * 128:(hp % 4 + 1) * 128], QPT[:, hp, i * 128:(i + 1) * 128],
                                                                  skT[:, hp, :], start=True, stop=True), [B_QPT, B_const], [B_PS[k]])
                for q4 in range(4):
                    P.op("act", lambda e, q4=q4: e.activation(R3[:, 0, q4 * 512:(q4 + 1) * 512], PS[sks[q4]][:, :], AF.Copy), [B_PS[sks[q4]]], B_R3[0])
                for hp in range(16):
                    P.op("dve", lambda e, hp=hp: e.max(out=SV[:, hp, 0:8], in_=SC[:, hp, :]), B_R3[0], [B_SV])
                for hp in range(16):
                    P.op("dve", lambda e, hp=hp: e.match_replace(out=TMP[:, hp, :], in_to_replace=SV[:, hp, 0:8], in_values=SC[:, hp, :], imm_value=-1e30),
                         B_R3[0] + [B_SV], B_R3[1])
                for hp in range(16):
                    P.op("dve", lambda e, hp=hp: e.max(out=SV[:, hp, 8:16], in_=TMP[:, hp, :]), B_R3[1], [B_SV])
                for hp in range(16):
                    for o in (0, 8):
                        P.op("dve", lambda e, hp=hp, o=o: e.max_index(out=SI[:, hp, o:o + 8], in_max=SV[:, hp, o:o + 8], in_values=SC[:, hp, :]),
                             B_R3[0] + [B_SV], [B_SI])
                P.op("dve", lambda e: e.tensor_copy(SIF[:], SI[:]), [B_SI], [B_SIF])
                sv4 = SV[:, :, :].rearrange("p (h two) k -> p h two k", two=2)
                sif4 = SIF[:, :, :].rearrange("p (h two) k -> p h two k", two=2)
                CAND4 = CAND.rearrange("p (h a b) -> p h a b", h=8, a=16)
                P.op("dve", lambda e: e.tensor_tensor(CAND4, sv4[:, :, 0, :].unsqueeze(3).broadcast_to([128, 8, 16, 16]),
                                                      sv4[:, :, 1, :].unsqueeze(2).broadcast_to([128, 8, 16, 16]), ALU.add), [B_SV], B_R3[2])
                CAND2 = CAND.rearrange("p (h m) -> p h m", h=8)
                TMPC = R3[:, 1, :].rearrange("p (h m) -> p h m", h=8)
                for h in range(8):
                    P.op("dve", lambda e, h=h: e.max(out=CV[:, h, 0:8], in_=CAND2[:, h, :]), B_R3[2], [B_CV])
                for h in range(8):
                    P.op("dve", lambda e, h=h: e.match_replace(out=TMPC[:, h, :], in_to_replace=CV[:, h, 0:8], in_values=CAND2[:, h, :], imm_value=-1e30),
                         B_R3[2] + [B_CV], B_R3[1])
                for h in range(8):
                    P.op("dve", lambda e, h=h: e.max(out=CV[:, h, 8:16], in_=TMPC[:, h, :]), B_R3[1], [B_CV])
                for h in range(8):
                    for o in (0, 8):
                        P.op("dve", lambda e, h=h, o=o: e.max_index(out=CI[:, h, o:o + 8], in_max=CV[:, h, o:o + 8], in_values=CAND2[:, h, :]),
                             B_R3[2] + [B_CV], [B_CI])
                CIf = CI[:, :, :].rearrange("p h k -> p (h k)")
                P.op("dve", lambda e: e.tensor_scalar(IK[:], CIf, shc[:, 0:1], None, ALU.logical_shift_right), [B_CI, B_const], [B_IK])
                P.op("dve", lambda e: e.tensor_scalar(JK[:], CIf, shc[:, 1:2], None, ALU.bitwise_and), [B_CI, B_const], [B_IK])
                P.op("dve", lambda e: e.tensor_copy(IKF[:, :, :].rearrange("p h k -> p (h k)"), IK[:]), [B_IK], [B_IKF])
                P.op("dve", lambda e: e.tensor_copy(JKF[:, :, :].rearrange("p h k -> p (h k)"), JK[:]), [B_IK], [B_IKF])
                io4 = iota16[:, :].unsqueeze(1).unsqueeze(1).broadcast_to([128, 8, 16, 16])
                for w_, (KF, SEL) in enumerate(((IKF, SEL0), (JKF, SEL1))):
                    E4w = R2f[:, w_ * 2048:(w_ + 1) * 2048].rearrange("p (h a b) -> p h a b", h=8, a=16)
                    E4r = E4w
                    P.op("dve", lambda e, KF=KF, E4w=E4w: e.tensor_tensor(E4w, KF[:, :, :].unsqueeze(3).broadcast_to([128, 8, 16, 16]), io4, ALU.is_equal),
                         [B_IKF, B_const], [B_R2[w_]])
                    P.op("dve", lambda e, E4w=E4w, E4r=E4r, w_=w_: e.tensor_tensor(E4w, E4r, sif4[:, :, w_, :].unsqueeze(2).broadcast_to([128, 8, 16, 16]), ALU.mult),
                         [B_R2[w_], B_SIF], [B_R2[w_]])
                    P.op("dve", lambda e, SEL=SEL, E4r=E4r: e.tensor_reduce(SEL[:], E4r, AX.X, ALU.add), [B_R2[w_]], [B_SEL])
                P.op("dve", lambda e: e.scalar_tensor_tensor(out=EIF[:], in0=SEL0[:, :, :].rearrange("p h k -> p (h k)"), scalar=128.0,
                                                             in1=SEL1[:, :, :].rearrange("p h k -> p (h k)"), op0=ALU.mult, op1=ALU.add), [B_SEL], [B_SEL])
                P.op("dve", lambda e: e.tensor_copy(EIDX[:], EIF[:]), [B_SEL], [B_EIDX])
                P.op("dve", lambda e: e.tensor_tensor(EW[:], CV[:], CV[:, :, 0:1].broadcast_to([128, 8, 16]), ALU.subtract), [B_CV], [B_GW])
                P.op("act", lambda e: e.activation(EW[:], EW[:], AF.Exp), [B_GW], [B_GW])
                P.op("dve", lambda e: e.tensor_reduce(SUMW[:], EW[:], AX.X, ALU.add), [B_GW], [B_GW])
                P.op("dve", lambda e: e.reciprocal(RW[:], SUMW[:]), [B_GW], [B_GW])
                P.op("dve", lambda e: e.tensor_tensor(GW[:], EW[:], RW[:, :].unsqueeze(2).broadcast_to([128, 8, 16]), ALU.mult), [B_GW], [B_GW])
                GWf = GW[:, :, :].rearrange("p h k -> p (h k)")
                P.op("dve", lambda e, i=i: e.scalar_tensor_tensor(out=XG[:], in0=X[:, i, :], scalar=scol(40 + i), in1=gffn[:], op0=ALU.mult, op1=ALU.mult),
                     [B_X[i], B_small, B_const], [B_XG])
                acc = [4, 5, 6, 7]
                NB = 9
                JUNK = R4f[0:nr, 0:2048]

                def UV(s):
                    if s < 3:
                        return R3[:, s, :].bitcast(BF16)[0:nr, :]
                    if s < 5:
                        return VEB[:, 2 * (s - 3):2 * (s - 3) + 2, :].rearrange("p a b -> p (a b)")[0:nr, :]
                    if s < 7:
                        return R2f[:, (s - 5) * 2048:(s - 4) * 2048].bitcast(BF16)[0:nr, :]
                    return WBraw[s - 7][:, :].bitcast(BF16)[0:nr, :]

                def BUV(s):
                    if s < 3:
                        return B_R3[s]
                    if s < 5:
                        return [B_VE[2 * (s - 3)], B_VE[2 * (s - 3) + 1]]
                    if s < 7:
                        return [B_R2[s - 5]]
                    return B_WB[s - 7]
                LA = NB - 2

                def gather(cg):
                    sg_ = cg % NB
                    P.dma("pool", lambda e, cg=cg, sg_=sg_: e.indirect_dma_start(out=UV(sg_), out_offset=None, in_=euvb,
                                                                                 in_offset=bass.IndirectOffsetOnAxis(ap=EIDX[0:nr, cg:cg + 1], axis=0)),
                          BUV(sg_)[0], reads=[B_EIDX, B_TUV], writes=BUV(sg_))
                if "nogather" not in dbg:
                    for cg in range(LA):
                        gather(cg)
                for c in range(129 if "nogather" not in dbg else 0):
                    if c + LA < 128:
                        gather(c + LA)
                    if c < 128:
                        sb_ = c % NB
                        p4 = c % 4
                        P.op("dve", lambda e, c=c, sb_=sb_: e.scalar_tensor_tensor(out=JUNK, in0=UV(sb_)[:, 0:2048], scalar=1.0, in1=XG[0:nr, :],
                                                                                   op0=ALU.mult, op1=ALU.mult, accum_out=AA[0:nr, c:c + 1]),
                             BUV(sb_) + [B_XG], [B_R4[0], B_AA[p4]])
                        P.op("act", lambda e, c=c: e.activation(AGL[0:nr, c:c + 1], AA[0:nr, c:c + 1], AF.Gelu_apprx_tanh), [B_AA[p4]], [B_AGL[p4]])
                    if c >= 1:
                        c1 = c - 1
                        sb_ = c1 % NB
                        p4 = c1 % 4
                        P.op("act", lambda e, c1=c1: e.activation(HW[0:nr, c1:c1 + 1], AGL[0:nr, c1:c1 + 1], AF.Copy, scale=GWf[0:nr, c1:c1 + 1]),
                             [B_AGL[p4], B_GW], [B_HW[p4]])
                        P.op("act", lambda e, c1=c1, p4=p4: e.activation(DIAG[p4][0:nr, :], ident[0:nr, :], AF.Copy, scale=HW[0:nr, c1:c1 + 1]),
                             [B_HW[p4], B_const], [B_DIAG[p4]])
                        for j in range(4):
                            P.op("pe", lambda e, j=j, sb_=sb_, p4=p4, c1=c1: e.matmul(PS[acc[j]][:, :], DIAG[p4][0:nr, :], UV(sb_)[:, 2048 + j * 512:2048 + (j + 1) * 512],
                                                                                  start=(c1 == 0), stop=(c1 == 127)), [B_DIAG[p4]] + BUV(sb_), [B_PS[acc[j]]])
                for j in range(4):
                    xs_ = X[0:nr, i, j * 512:(j + 1) * 512]
                    P.op("dve", lambda e, j=j, xs_=xs_: e.tensor_tensor(xs_, xs_, PS[acc[j]][0:nr, :], ALU.add), [B_PS[acc[j]], B_X[i]], [B_X[i]])
                P.op("act", lambda e, i=i: e.activation(R4f[0:nr, 2048:4096], X[0:nr, i, :], AF.Square, accum_out=small[0:nr, 48 + i:49 + i]),
                     [B_X[i]], [B_R4[1], B_small])
                rstd_from(48 + i, 56 + i, D, rows=nr)
                P.op("dve", lambda e, i=i: e.scalar_tensor_tensor(out=X[0:nr, i, :], in0=X[0:nr, i, :], scalar=small[0:nr, 56 + i:57 + i], in1=gfin[0:nr, :],
                                                                  op0=ALU.mult, op1=ALU.mult), [B_X[i], B_small, B_const], [B_X[i]])
                if sample:
                    P.dma("sp", lambda e, i=i: e.dma_start(out=y_s, in_=X[0:16, i, :]), B_X[i], reads=[B_X[i]], is_out=True)
                else:
                    P.dma("sp", lambda e, i=i, gt=gt: e.dma_start(out=y_p[gt * 128:(gt + 1) * 128, :], in_=X[:, i, :]), B_X[i], reads=[B_X[i]], is_out=True)

        for g0 in range(0, nt, G):
            do_group(list(range(g0, g0 + G)), False)
            first_group[0] = False
        if "nosample" not in dbg:
            do_group([0], True)
        P.finish()
        P.emit()
    return nc


def _t5_bucket_np(rel):
    try:
        import jax
        import jax.numpy as jnp
        with jax.default_device(jax.devices("cpu")[0]):
            r = jnp.asarray(rel, dtype=jnp.int32)
            half = 16
            max_exact = 8
            ret = jnp.where(r > 0, half, 0)
            n = jnp.abs(r)
            nf = jnp.maximum(n, 1).astype(jnp.float32)
            large = max_exact + (jnp.log(nf / max_exact) / math.log(128 / max_exact) * (half - max_exact)).astype(jnp.int32)
            large = jnp.minimum(large, half - 1)
            return np.asarray(ret + jnp.where(n < max_exact, n, large))
    except Exception:
        r = np.asarray(rel, dtype=np.int32)
        ret = np.where(r > 0, 16, 0)
        n = np.abs(r)
        nf = np.maximum(n, 1).astype(np.float32)
        large = 8 + (np.log(nf / np.float32(8)) / np.float32(math.log(16.0)) * np.float32(8)).astype(np.int32)
        large = np.minimum(large, 15)
        return ret + np.where(n < 8, n, large)


_NC_CACHE = {}


def kernel(x_prompt, x_sample, cache_k_swa, cache_v_swa, norm_mix_g, w_in, sgu_norm_g, sgu_w_s, sgu_b_s,
           attn_sinks, rel_bias_table, w_branch_attn, w_branch_sgu, w_out, norm_ffn_g, peer_w_query,
           peer_sub_keys, peer_expert_u, peer_expert_v, norm_final_g):
    f = lambda a: np.ascontiguousarray(np.asarray(a), dtype=np.float32)
    if "nc" not in _NC_CACHE:
        _NC_CACHE["nc"] = build_program()
    nc = _NC_CACHE["nc"]
    kb = np.arange(2)[:, None, None]; kk = np.arange(128)[None, :, None]; qq = np.arange(128)[None, None, :]
    rel = (kb - 1) * 128 + kk - qq
    bkt = _t5_bucket_np(rel).reshape(-1)
    oh = np.zeros((32, 32768), np.float32)
    oh[bkt, np.arange(32768)] = 1.0
    s_i = np.arange(128)[:, None, None]; t_i = np.arange(128)[None, None, :]
    trilT = np.ascontiguousarray(np.broadcast_to((t_i >= s_i), (128, 8, 128))).astype(np.float32)
    shared = dict(
        w_in=f(w_in[0]), w_pa=f(w_branch_attn[0]), w_pb=f(w_branch_sgu[0]), w_out=f(w_out[0]), w_q=f(peer_w_query[0]),
        eu=f(peer_expert_u[0]), ev=f(peer_expert_v[0]),
        gmixT=f(np.asarray(norm_mix_g[0]).reshape(16, 128).T), gffnT=f(np.asarray(norm_ffn_g[0]).reshape(16, 128).T),
        gffn_bc=f(np.broadcast_to(np.asarray(norm_ffn_g[0])[None, :], (128, D))),
        gfin_bc=f(np.broadcast_to(np.asarray(norm_final_g)[None, :], (128, D))),
        sgug_bc=f(np.broadcast_to(np.asarray(sgu_norm_g[0])[None, :], (128, 1024))),
        wsT=f(np.asarray(sgu_w_s[0]).transpose(2, 0, 1)), trilT=trilT, bsT=f(np.asarray(sgu_b_s[0]).T),
        sinks_bc=f(np.broadcast_to(np.asarray(attn_sinks[0])[None, :], (128, 16))), table=f(rel_bias_table), oh=oh,
        skT=f(np.asarray(peer_sub_keys[0]).reshape(16, 128, 128).transpose(2, 0, 1)),
        ident=np.eye(128, dtype=np.float32), iota16=f(np.broadcast_to(np.arange(16, dtype=np.float32)[None, :], (128, 16))),
        shc=np.ascontiguousarray(np.broadcast_to(np.array([[4, 15]], np.uint32), (128, 2))),
    )
    xpn = np.asarray(x_prompt); xsn = np.asarray(x_sample); ckn = np.asarray(cache_k_swa); cvn = np.asarray(cache_v_swa)
    in_maps = []
    for c in range(8):
        m = dict(shared)
        m["xp"] = f(xpn[c]); m["xs"] = f(xsn[c])
        m["ck"] = f(ckn[0, c].reshape(128, 128)); m["cv"] = f(cvn[0, c].reshape(128, 128))
        in_maps.append(m)
    res = run_bass_kernel_spmd(nc, in_maps, core_ids=list(range(8)))
    r = res.results
    y_prompt = np.stack([r[c]["y_p"] for c in range(8)]).astype(np.float32)
    y_sample = np.stack([r[c]["y_s"] for c in range(8)]).astype(np.float32)
    nk_p = np.stack([r[c]["nk_p"].reshape(128, 2, 64) for c in range(8)])[None].astype(np.float32)
    nv_p = np.stack([r[c]["nv_p"].reshape(128, 2, 64) for c in range(8)])[None].astype(np.float32)
    nk_s = np.stack([r[c]["nk_s"].reshape(16, 2, 64) for c in range(8)])[None].astype(np.float32)
    nv_s = np.stack([r[c]["nv_s"].reshape(16, 2, 64) for c in range(8)])[None].astype(np.float32)
    nsg = np.stack([r[c]["nsgu"] for c in range(8)])[None].astype(np.float32)
    return (y_prompt, y_sample, nk_p, nv_p, nk_s, nv_s, nsg)
```

```python
import math
import numpy as np
from contextlib import ExitStack
import concourse.bass as bass
import concourse.mybir as mybir
from concourse.bass_utils import run_bass_kernel_spmd

F32 = mybir.dt.float32
F32R = mybir.dt.float32r
BF16 = mybir.dt.bfloat16
I32 = mybir.dt.int32
U32 = mybir.dt.uint32
AF = mybir.ActivationFunctionType
ALU = mybir.AluOpType
AX = mybir.AxisListType

D = 2048
DIN = 7424
SEQ = 2048
NT = SEQ // 128
G = 2
TMAX = G * 128
EPS = 1e-6
WCOL = 256


class Buf:
    __slots__ = ("name", "lw", "rd", "dsem", "dcnt", "excl")

    def __init__(self, name, excl=False):
        self.name = name
        self.excl = excl
        self.lw = None
        self.rd = {}
        self.dsem = {}
        self.dcnt = {}


class Prog:
    ENG = ("sp", "act", "dve", "pool", "pe")

    def __init__(self, nc, es):
        self.nc = nc
        self.es = es
        self.st = {e: [] for e in self.ENG}
        self.sem = {}
        self.cnt = {}
        self.waited = {e: {} for e in self.ENG}
        self.nsem = 0
        self.out_toks = []
        for e in self.ENG:
            self._new_sem(e)

    def _mk(self, name):
        self.nsem += 1
        return self.es.enter_context(self.nc.semaphore(f"{name}{self.nsem}"))

    def _new_sem(self, e):
        self.sem[e] = self._mk("e" + e)
        self.cnt[e] = 0

    def _wait(self, e, tok):
        sem, val, src = tok
        if src == e and e == "pe":
            return
        w = self.waited[e]
        if w.get(id(sem), -1) >= val:
            return
        w[id(sem)] = val
        self.st[e].append(("w", sem, val))

    def _deps(self, e, reads, writes):
        need = {}
        def add(tok):
            k = id(tok[0])
            if k not in need or need[k][1] < tok[1]:
                need[k] = tok
        for b in reads:
            if b.lw is not None:
                add(b.lw)
            if b.excl:
                for t in b.rd.values():
                    if t[2] != e:
                        add(t)
        for b in writes:
            if b.lw is not None:
                add(b.lw)
            for t in b.rd.values():
                add(t)
        for tok in need.values():
            self._wait(e, tok)

    def _commit(self, tok, reads, writes):
        for b in reads:
            k = id(tok[0])
            if k not in b.rd or b.rd[k][1] < tok[1]:
                b.rd[k] = tok
        for b in writes:
            b.lw = tok
            b.rd = {}

    def op(self, e, fn, reads=(), writes=()):
        self._deps(e, reads, writes)
        if self.cnt[e] >= 30000:
            self._new_sem(e)
        self.cnt[e] += 1
        tok = (self.sem[e], self.cnt[e], e)
        self.st[e].append(("o", fn, self.sem[e], 1))
        self._commit(tok, reads, writes)
        return tok

    def dma(self, q, fn, dbuf, reads=(), writes=(), is_out=False):
        self._deps(q, reads, writes)
        if q not in dbuf.dsem or dbuf.dcnt[q] >= 48000:
            dbuf.dsem[q] = self._mk("d")
            dbuf.dcnt[q] = 0
        dbuf.dcnt[q] += 16
        tok = (dbuf.dsem[q], dbuf.dcnt[q], "dma")
        self.st[q].append(("o", fn, dbuf.dsem[q], 16))
        self._commit(tok, reads, writes)
        if is_out:
            self.out_toks.append(tok)
        return tok

    def finish(self):
        for tok in self.out_toks:
            self._wait("sp", tok)

    def emit(self):
        blk = self.es.enter_context(self.nc.Block())

        def run(e):
            def f(eng):
                for it in self.st[e]:
                    if it[0] == "w":
                        eng.wait_ge(it[1], it[2])
                    else:
                        it[1](eng).then_inc(it[2], it[3])
            return f
        blk.sync(run("sp"))
        blk.scalar(run("act"))
        blk.vector(run("dve"))
        blk.gpsimd(run("pool"))
        blk.tensor(run("pe"))


def build_program(nt=NT, dbg=""):
    SEQ = nt * 128
    nc = bass.Bass("TRN2", target_bir_lowering=False)

    def din(name, shape, dt=F32):
        return nc.dram_tensor(name, list(shape), dt, kind="ExternalInput").ap()

    def dout(name, shape, dt=F32):
        return nc.dram_tensor(name, list(shape), dt, kind="ExternalOutput").ap()

    xp = din("xp", [SEQ, D]); xs = din("xs", [16, D])
    ck = din("ck", [128, 128]); cvv = din("cv", [128, 128])
    w_in = din("w_in", [D, DIN]); w_pa = din("w_pa", [1024, D]); w_pb = din("w_pb", [1024, D])
    w_out = din("w_out", [D, D]); w_q = din("w_q", [D, D])
    eu = din("eu", [16384, D]); ev = din("ev", [16384, D])
    gmixT_d = din("gmixT", [128, 16]); gffnT_d = din("gffnT", [128, 16])
    gffn_d = din("gffn_bc", [128, D]); gfin_d = din("gfin_bc", [128, D]); sgug_d = din("sgug_bc", [128, 1024])
    wsT_d = din("wsT", [128, 8, 128]); trilT_d = din("trilT", [128, 8, 128]); bsT_d = din("bsT", [128, 8])
    sinks_d = din("sinks_bc", [128, 16]); table_d = din("table", [32, 16]); oh_d = din("oh", [32, 32768])
    skT_d = din("skT", [128, 16, 128]); ident_d = din("ident", [128, 128]); iota_d = din("iota16", [128, 16])
    shc_d = din("shc", [128, 2], U32)
    y_p = dout("y_p", [SEQ, D]); y_s = dout("y_s", [16, D])
    nk_p = dout("nk_p", [128, 128]); nv_p = dout("nv_p", [128, 128])
    nk_s = dout("nk_s", [16, 128]); nv_s = dout("nv_s", [16, 128]); nsgu = dout("nsgu", [16, 1024])
    bscr = nc.dram_tensor("bscr", [16, 32768], F32, kind="Internal").ap()
    euvb = nc.dram_tensor("euvb", [16384, 2 * D], BF16, kind="Internal").ap()
    wscr = nc.dram_tensor("wscr", [64, 128, 2048], F32, kind="Internal").ap()

    w_in_v = w_in.rearrange("(kc p) n -> p kc n", p=128)
    w_pa_v = w_pa.rearrange("(kc p) n -> p kc n", p=128)
    w_pb_v = w_pb.rearrange("(kc p) n -> p kc n", p=128)
    w_out_v = w_out.rearrange("(kc p) n -> p kc n", p=128)
    w_q_v = w_q.rearrange("(kc p) n -> p kc n", p=128)

    with ExitStack() as es:
        P = Prog(nc, es)

        def sb(name, shape, dt=F32):
            return es.enter_context(nc.sbuf_tensor("s_" + name, list(shape), dt))

        biasT = sb("biasT", [128, 16, 2, 128]); B_bias = Buf("biasT")
        gffn = sb("gffn", [128, D]); gfin = sb("gfin", [128, D]); sgug = sb("sgug", [128, 1024])
        skT = sb("skT", [128, 16, 128]); wmT = sb("wmT", [128, 8, 128]); trilT = sb("trilT", [128, 8, 128])
        ident = sb("ident", [128, 128]); gmixT = sb("gmixT", [128, 16]); gffnT = sb("gffnT", [128, 16])
        bsT = sb("bsT", [128, 8]); esink = sb("esink", [128, 16]); iota16 = sb("iota16", [128, 16])
        shc = sb("shc", [128, 2], U32); tab = sb("tab", [32, 16])
        B_const = Buf("const")
        X = sb("X", [128, G, D]); B_X = [Buf(f"X{i}") for i in range(G)]
        R1 = sb("R1", [128, 16, TMAX], BF16); B_R1 = [Buf("R1a"), Buf("R1b")]
        QPT = sb("QPT", [128, 16, TMAX]); B_QPT = Buf("QPT")
        R2f = sb("R2", [128, 16 * TMAX]); B_R2 = [Buf("R2a"), Buf("R2b")]
        XNTb = R2f[:, :].bitcast(BF16)[:, 0:16 * TMAX].rearrange("p (k t) -> p k t", k=16)
        XN2T = R2f[:, :].rearrange("p (k t) -> p k t", k=16)
        R4 = sb("R4", [128, 16, TMAX], BF16); B_R4 = [Buf("R4a"), Buf("R4b")]
        R4f = R4[:, :, :].rearrange("p a b -> p (a b)")
        WBraw = [sb(f"WB{i}", [128, 2048]) for i in range(2)]
        WB = [w[:, :].bitcast(BF16).rearrange("p (k n) -> p k n", k=16) for w in WBraw]
        WBf32 = [w[:, :].rearrange("p (k n) -> p k n", k=16) for w in WBraw]
        B_WB = [[Buf(f"WB{i}")] for i in range(2)]
        VEB = sb("VEB", [128, 4, 2048], BF16); B_VE = [Buf(f"VE{i}") for i in range(4)]
        for h_ in range(2):
            vv = VEB[:, 2 * h_:2 * h_ + 2, :].rearrange("p a b -> p (a b)")
            WB.append(vv.rearrange("p (k n) -> p k n", k=16))
            WBf32.append(vv.bitcast(F32).rearrange("p (k n) -> p k n", k=16))
            WBraw.append(vv.bitcast(F32))
            B_WB.append([B_VE[2 * h_], B_VE[2 * h_ + 1]])
        NWB = 4
        R3 = sb("R3", [128, 3, 2048]); B_R3 = [[Buf(f"R3_{j}a"), Buf(f"R3_{j}b")] for j in range(3)]
        XG = sb("XG", [128, 2048]); B_XG = Buf("XG")
        KT = sb("KT", [64, 2, 128 + TMAX], BF16); B_KT = Buf("KT")
        VA = sb("VA", [128, 1 + G, 2, 65]); B_VA = Buf("VA")
        KTOK = sb("KTOK", [128, G, 128]); B_KTOK = Buf("KTOK")
        PT = [sb(f"PT{i}", [128, 2, 2, 128]) for i in range(2)]; B_PT = [Buf("PT0"), Buf("PT1")]
        DIAG = [sb(f"DIAG{i}", [128, 128], BF16) for i in range(4)]; B_DIAG = [Buf(f"DG{i}") for i in range(4)]
        small = sb("small", [128, 64]); B_small = Buf("small")
        DEN = sb("DEN", [128, 16]); RDEN = sb("RDEN", [128, 16]); B_DEN = Buf("DEN")
        SV = sb("SV", [128, 16, 16]); SI = sb("SI", [128, 16, 16], U32); SIF = sb("SIF", [128, 16, 16])
        CV = sb("CV", [128, 8, 16]); CI = sb("CI", [128, 8, 16], U32)
        IK = sb("IK", [128, 128], U32); JK = sb("JK", [128, 128], U32)
        IKF = sb("IKF", [128, 8, 16]); JKF = sb("JKF", [128, 8, 16])
        SEL0 = sb("SEL0", [128, 8, 16]); SEL1 = sb("SEL1", [128, 8, 16])
        EIF = sb("EIF", [128, 128]); EIDX = sb("EIDX", [128, 128], I32)
        EW = sb("EW", [128, 8, 16]); GW = sb("GW", [128, 8, 16]); SUMW = sb("SUMW", [128, 8]); RW = sb("RW", [128, 8])
        AA = sb("AA", [128, 128]); AGL = sb("AGL", [128, 128]); HW = sb("HW", [128, 128])
        B_SV = Buf("SV"); B_SI = Buf("SI"); B_SIF = Buf("SIF"); B_CV = Buf("CV"); B_CI = Buf("CI")
        B_IK = Buf("IK"); B_IKF = Buf("IKF"); B_SEL = Buf("SEL"); B_EIDX = Buf("EIDX"); B_GW = Buf("GW")
        B_AA = [Buf(f"AA{i}") for i in range(4)]; B_AGL = [Buf(f"AGL{i}") for i in range(4)]; B_HW = [Buf(f"HW{i}") for i in range(4)]
        PS = [es.enter_context(nc.psum_tensor(f"PS{i}", [128, 512], F32)) for i in range(8)]
        B_PS = [Buf(f"PS{i}", excl=True) for i in range(8)]
        psrr = [0]

        def nps(lo=0, hi=8):
            k = lo + psrr[0] % (hi - lo)
            psrr[0] += 1
            return k

        def scol(k):
            return small[:, k:k + 1]

        def rstd_from(ss_col, out_col, n, rows=128):
            P.op("dve", lambda e: e.tensor_scalar(small[0:rows, out_col:out_col + 1], small[0:rows, ss_col:ss_col + 1],
                                                  1.0 / n, EPS, ALU.mult, ALU.add), [B_small], [B_small])
            P.op("act", lambda e: e.activation(small[0:rows, out_col:out_col + 1], small[0:rows, out_col:out_col + 1], AF.Sqrt),
                 [B_small], [B_small])
            P.op("dve", lambda e: e.reciprocal(small[0:rows, out_col:out_col + 1], small[0:rows, out_col:out_col + 1]),
                 [B_small], [B_small])

        def ld(dst, src, buf=B_const):
            P.dma("sp", lambda e: e.dma_start(out=dst, in_=src), buf, writes=[buf])
        ld(gffn[:], gffn_d); ld(gfin[:], gfin_d); ld(sgug[:], sgug_d); ld(skT[:], skT_d)
        ld(wmT[:], wsT_d); ld(trilT[:], trilT_d); ld(ident[:], ident_d); ld(gmixT[:], gmixT_d); ld(gffnT[:], gffnT_d)
        ld(bsT[:], bsT_d); ld(esink[:], sinks_d); ld(iota16[:], iota_d); ld(shc[:], shc_d); ld(tab[:], table_d)
        P.op("dve", lambda e: e.tensor_tensor(wmT[:], wmT[:], trilT[:], ALU.mult), [B_const], [B_const])
        P.op("act", lambda e: e.activation(esink[:], esink[:], AF.Exp), [B_const], [B_const])
        P.op("dve", lambda e: e.memset(VA[:, :, :, 64:65], 1.0), [], [B_VA])
        for pc in range(8 if "nobias" not in dbg else 0):
            P.dma("sp", lambda e, pc=pc: e.dma_start(out=R3[0:32, 0, :], in_=oh_d[:, pc * 4096:pc * 4096 + 2048]), B_R3[0][0], writes=B_R3[0])
            P.dma("sp", lambda e, pc=pc: e.dma_start(out=R3[0:32, 1, :], in_=oh_d[:, pc * 4096 + 2048:(pc + 1) * 4096]), B_R3[1][0], writes=B_R3[1])
            for hf in range(2):
                for q4 in range(4):
                    k = nps()
                    P.op("pe", lambda e, k=k, hf=hf, q4=q4: e.matmul(PS[k][0:16, :], tab[:, :], R3[0:32, hf, q4 * 512:(q4 + 1) * 512],
                                                                    start=True, stop=True), [B_const] + B_R3[hf], [B_PS[k]])
                    P.op("act", lambda e, k=k, q4=q4: e.activation(XG[0:16, q4 * 512:(q4 + 1) * 512], PS[k][0:16, :], AF.Copy),
                         [B_PS[k]], [B_XG])
                P.dma("sp", lambda e, pc=pc, hf=hf: e.dma_start(out=bscr[:, pc * 4096 + hf * 2048: pc * 4096 + (hf + 1) * 2048], in_=XG[0:16, :]),
                      B_XG, reads=[B_XG], writes=[B_bias])
        if "nobias" not in dbg:
          P.dma("sp", lambda e: e.dma_start(out=biasT[:], in_=bscr.rearrange("h (kb kk qq) -> kk h kb qq", kb=2, kk=128)),
              B_bias, reads=[B_bias], writes=[B_bias])

        B_TUV = Buf("tblUV")
        RC = 2
        cvt = 0
        for t_, src in enumerate((eu, ev)):
            Bt = B_TUV
            srcv = src.rearrange("(p r) d -> p r d", p=128)
            dstv = euvb.rearrange("(p r) (t d) -> p r t d", p=128, t=2)[:, :, t_, :]
            for r0 in range(0, 128, RC):
                b = cvt % 2
                cvt += 1
                stg = VEB[:, 2 * b:2 * b + 2, :]
                sB = [B_VE[2 * b], B_VE[2 * b + 1]]
                P.dma("pool", lambda e, stg=stg, srcv=srcv, r0=r0: e.dma_start(out=stg, in_=srcv[:, r0:r0 + RC, :]), sB[0], writes=sB)
                P.dma("sp", lambda e, stg=stg, dstv=dstv, r0=r0: e.dma_start(out=dstv[:, r0:r0 + RC, :], in_=stg), Bt, reads=sB, writes=[Bt])

        wcount = [0]

        B_WSCR = Buf("wscr")
        first_group = [True]

        def load_block(specs, blk):
            b = wcount[0] % NWB
            wcount[0] += 1
            if first_group[0]:
                for (dst_fn, src) in specs:
                    P.dma("pool", lambda e, dst_fn=dst_fn, src=src, b=b: e.dma_start(out=dst_fn(b), in_=src),
                          B_WB[b][0], writes=B_WB[b])
                P.dma("sp", lambda e, b=b, blk=blk: e.dma_start(out=wscr[blk], in_=WBraw[b][:, :]), B_WSCR, reads=B_WB[b], writes=[B_WSCR])
            else:
                P.dma("sp", lambda e, b=b, blk=blk: e.dma_start(out=WBraw[b][:, :], in_=wscr[blk]), B_WB[b][0], reads=[B_WSCR], writes=B_WB[b])
            return b

        def run_items(items):
            widx = [k for k, it in enumerate(items) if it[0] is not None]
            bufs = {}
            depth = NWB - 1
            for j0 in range(min(depth, len(widx))):
                bufs[widx[j0]] = load_block(items[widx[j0]][0], j0)
            for k, (w, fn) in enumerate(items):
                if w is not None:
                    j = widx.index(k)
                    if j + depth < len(widx):
                        bufs[widx[j + depth]] = load_block(items[widx[j + depth]][0], j + depth)
                    fn(bufs[k])
                else:
                    fn(None)

        def full(src_v, c0, ncol=WCOL):
            return [(lambda b: WB[b][:, :, 0:ncol], src_v[:, :, c0:c0 + ncol])]

        def do_group(tiles, sample):
            ng = len(tiles)
            T = ng * 128
            nr = 16 if sample else 128
            XNT = XNTb
            QT = R1

            for i, gt in enumerate(tiles):
                if sample:
                    P.op("dve", lambda e, i=i: e.memset(X[:, i, :], 0.0), [], [B_X[i]])
                    P.dma("sp", lambda e, i=i: e.dma_start(out=X[0:16, i, :], in_=xs), B_X[i], writes=[B_X[i]])
                else:
                    P.dma("sp", lambda e, i=i, gt=gt: e.dma_start(out=X[:, i, :], in_=xp[gt * 128:(gt + 1) * 128, :]), B_X[i], writes=[B_X[i]])
            if sample:
                P.dma("sp", lambda e: e.dma_start(out=KTOK[:, 0, :], in_=ck), B_KTOK, writes=[B_KTOK])
                k = nps()
                P.op("pe", lambda e, k=k: e.transpose(PS[k][:, 0:128], KTOK[:, 0, :], ident[:]), [B_KTOK, B_const], [B_PS[k]])
                P.op("act", lambda e, k=k: e.activation(KT[0:64, 0, 0:128], PS[k][0:64, 0:128], AF.Copy), [B_PS[k]], [B_KT])
                P.op("act", lambda e, k=k: e.activation(KT[0:64, 1, 0:128], PS[k][64:128, 0:128], AF.Copy), [B_PS[k]], [B_KT])
                P.dma("sp", lambda e: e.dma_start(out=VA[:, 0, :, 0:64], in_=cvv.rearrange("p (k d) -> p k d", k=2)), B_VA, writes=[B_VA])
            elif tiles[0] != 0:
                P.op("act", lambda e: e.activation(KT[0:64, :, 0:128], KT[0:64, :, TMAX:TMAX + 128], AF.Copy), [B_KT], [B_KT])
                P.op("dve", lambda e: e.tensor_copy(VA[:, 0, :, 0:64], VA[:, G, :, 0:64]), [B_VA], [B_VA])

            def norm_T(i, gT, dstR, dstB, col):
                XR = R3[:, 2, :]
                P.op("act", lambda e: e.activation(XR, X[:, i, :], AF.Square, accum_out=scol(col)), [B_X[i]], B_R3[2] + [B_small])
                rstd_from(col, col + 8, D)
                P.op("dve", lambda e: e.tensor_scalar(XR, X[:, i, :], scol(col + 8), None, ALU.mult), [B_X[i], B_small], B_R3[2])
                for k4 in range(4):
                    k = nps()
                    for j in range(4):
                        kc = k4 * 4 + j
                        P.op("pe", lambda e, k=k, j=j, kc=kc: e.transpose(PS[k][:, j * 128:(j + 1) * 128], XR[:, kc * 128:(kc + 1) * 128], ident[:]),
                             B_R3[2] + [B_const], [B_PS[k]])
                    P.op("dve", lambda e, k=k, k4=k4: e.tensor_tensor(dstR[:, k4 * 4:(k4 + 1) * 4, i * 128:(i + 1) * 128],
                                                                      PS[k][:, :].rearrange("p (a t) -> p a t", a=4),
                                                                      gT[:, k4 * 4:(k4 + 1) * 4].unsqueeze(2).broadcast_to([128, 4, 128]), ALU.mult),
                         [B_PS[k], B_const], dstB)
            for i in range(ng):
                norm_T(i, gmixT, XNT, B_R2, i)

            items = []

            def q_block(blk):
                def fn(b):
                    for j in range(2):
                        pj = blk * 2 + j
                        k = nps()
                        for kc in range(16):
                            P.op("pe", lambda e, k=k, kc=kc, j=j, b=b: e.matmul(PS[k][:, 0:T], WB[b][:, kc, j * 128:(j + 1) * 128], XNT[:, kc, 0:T],
                                                                               start=(kc == 0), stop=(kc == 15)), B_WB[b] + B_R2, [B_PS[k]])
                        P.op("act", lambda e, k=k, pj=pj: e.activation(QT[0:64, 2 * pj, 0:T], PS[k][0:64, 0:T], AF.Copy, scale=0.125), [B_PS[k]], B_R1)
                        P.op("act", lambda e, k=k, pj=pj: e.activation(QT[0:64, 2 * pj + 1, 0:T], PS[k][64:128, 0:T], AF.Copy, scale=0.125), [B_PS[k]], B_R1)
                return fn
            for blk in range(4):
                items.append((full(w_in_v, blk * 256), q_block(blk)))

            def kv_fn(b):
                for i in range(ng):
                    k = nps()
                    for kc in range(16):
                        P.op("pe", lambda e, k=k, kc=kc, i=i, b=b: e.matmul(PS[k][:, 0:256], XNT[:, kc, i * 128:(i + 1) * 128], WB[b][:, kc, :],
                                                                           start=(kc == 0), stop=(kc == 15)), B_WB[b] + B_R2, [B_PS[k]])
                    P.op("act", lambda e, k=k, i=i: e.activation(KTOK[:, i, :], PS[k][:, 0:128], AF.Copy), [B_PS[k]], [B_KTOK])
                    P.op("dve", lambda e, k=k, i=i: e.tensor_copy(VA[:, 1 + i, :, 0:64], PS[k][:, 128:256].rearrange("p (k d) -> p k d", k=2)),
                         [B_PS[k]], [B_VA])
                    k2 = nps()
                    P.op("pe", lambda e, k2=k2, i=i: e.transpose(PS[k2][:, 0:128], KTOK[:, i, :], ident[:]), [B_KTOK, B_const], [B_PS[k2]])
                    P.op("act", lambda e, k2=k2, i=i: e.activation(KT[0:64, 0, 128 + i * 128:256 + i * 128], PS[k2][0:64, 0:128], AF.Copy), [B_PS[k2]], [B_KT])
                    P.op("act", lambda e, k2=k2, i=i: e.activation(KT[0:64, 1, 128 + i * 128:256 + i * 128], PS[k2][64:128, 0:128], AF.Copy), [B_PS[k2]], [B_KT])
            items.append((full(w_in_v, 1024), kv_fn))

            def uv_block(slot, blk):
                def fn(b):
                    for i in range(ng):
                        k = nps()
                        for kc in range(16):
                            P.op("pe", lambda e, k=k, kc=kc, i=i, b=b: e.matmul(PS[k][:, 0:256], XNT[:, kc, i * 128:(i + 1) * 128], WB[b][:, kc, :],
                                                                               start=(kc == 0), stop=(kc == 15)), B_WB[b] + B_R2, [B_PS[k]])
                        P.op("act", lambda e, k=k, i=i: e.activation(R3[:, slot, i * 1024 + blk * 256: i * 1024 + (blk + 1) * 256], PS[k][:, 0:256],
                                                                     AF.Gelu_apprx_tanh), [B_PS[k]], B_R3[slot])
                return fn
            for blk in range(4):
                items.append((full(w_in_v, 1280 + blk * 256), uv_block(0, blk)))
            for blk in range(4):
                items.append((full(w_in_v, 2304 + blk * 256), uv_block(1, blk)))

            def mixers(_):
                for i in range(ng):
                    VNi = R3[:, 1, i * 1024:(i + 1) * 1024]
                    P.op("act", lambda e, i=i, VNi=VNi: e.activation(R4f[:, 0:1024], VNi, AF.Square, accum_out=scol(16 + i)),
                         B_R3[1], [B_R4[0], B_small])
                    rstd_from(16 + i, 24 + i, 1024)
                    P.op("dve", lambda e, i=i, VNi=VNi: e.scalar_tensor_tensor(out=VNi, in0=VNi, scalar=scol(24 + i), in1=sgug[:],
                                                                              op0=ALU.mult, op1=ALU.mult), B_R3[1] + [B_small, B_const], B_R3[1])
                for i, gt in enumerate(tiles):
                    has_prev = sample or gt != 0
                    ncur = 16 if sample else 128
                    pso = [5, 6, 7]
                    for hp in range(8):
                        s = hp % 2
                        k = nps(0, 5)
                        kbs = ([0] if has_prev else []) + [1]
                        for hh in range(2):
                            h = 2 * hp + hh
                            kv = h // 8
                            for kb in kbs:
                                nk = 128 if kb == 0 else ncur
                                c0 = i * 128 + kb * 128
                                P.op("pe", lambda e, k=k, hh=hh, kb=kb, nk=nk, c0=c0, kv=kv, h=h, i=i: e.matmul(
                                    PS[k][0:nk, (hh * 2 + kb) * 128:(hh * 2 + kb + 1) * 128], KT[0:64, kv, c0:c0 + nk],
                                    QT[0:64, h, i * 128:(i + 1) * 128], start=True, stop=True), [B_KT] + B_R1, [B_PS[k]])
                        for kb in kbs:
                            nk = 128 if kb == 0 else ncur
                            P.op("dve", lambda e, k=k, kb=kb, nk=nk, hp=hp, s=s: e.tensor_tensor(
                                PT[s][0:nk, :, kb, :], PS[k][0:nk, :].rearrange("p (a b q) -> p a b q", a=2, b=2)[:, :, kb, :],
                                biasT[0:nk, 2 * hp:2 * hp + 2, kb, :], ALU.add), [B_PS[k], B_bias], [B_PT[s]])
                            P.op("act", lambda e, kb=kb, nk=nk, s=s: e.activation(PT[s][0:nk, :, kb, :], PT[s][0:nk, :, kb, :], AF.Exp),
                                 [B_PT[s]], [B_PT[s]])
                        if not sample:
                            P.op("dve", lambda e, s=s: e.memset(PT[s][64:128, :, 1, 0:64], 0.0), [], [B_PT[s]])
                            if has_prev:
                                P.op("dve", lambda e, s=s: e.memset(PT[s][0:64, :, 0, 64:128], 0.0), [], [B_PT[s]])
                        for hh in range(2):
                            h = 2 * hp + hh
                            kv = h // 8
                            bk = pso[h // 6]
                            hl = h % 6
                            for n_, kb in enumerate(kbs):
                                nk = 128 if kb == 0 else ncur
                                slot = i + kb
                                P.op("pe", lambda e, bk=bk, hl=hl, s=s, hh=hh, kb=kb, nk=nk, slot=slot, kv=kv, n_=n_, kbs=kbs: e.matmul(
                                    PS[bk][:, hl * 65:(hl + 1) * 65], PT[s][0:nk, hh, kb, :], VA[0:nk, slot, kv, :],
                                    start=(n_ == 0), stop=(n_ == len(kbs) - 1)), [B_PT[s], B_VA], [B_PS[bk]])
                    for b3 in range(3):
                        nh = 6 if b3 < 2 else 4
                        h0 = b3 * 6
                        P.op("dve", lambda e, b3=b3, nh=nh, h0=h0: e.tensor_tensor(
                            DEN[:, h0:h0 + nh], PS[pso[b3]][:, 0:nh * 65].rearrange("p (h c) -> p h c", c=65)[:, :, 64],
                            esink[:, h0:h0 + nh], ALU.add), [B_PS[pso[b3]], B_const], [B_DEN])
                    P.op("dve", lambda e: e.reciprocal(RDEN[:], DEN[:]), [B_DEN], [B_DEN])
                    for h in range(16):
                        bk = pso[h // 6]
                        hl = h % 6
                        P.op("dve", lambda e, bk=bk, hl=hl, h=h, i=i: e.tensor_scalar(
                            R3[:, 2, i * 1024 + h * 64: i * 1024 + (h + 1) * 64], PS[bk][:, hl * 65:hl * 65 + 64], RDEN[:, h:h + 1], None, ALU.mult),
                            [B_PS[bk], B_DEN], B_R3[2])
                for i in range(ng):
                    ks = [nps(), nps()]
                    for g in range(8):
                        k = ks[g // 4]
                        P.op("pe", lambda e, k=k, g=g, i=i: e.matmul(PS[k][:, (g % 4) * 128:(g % 4 + 1) * 128], wmT[:, g, :],
                                                                    R3[:, 1, i * 1024 + g * 128: i * 1024 + (g + 1) * 128], start=True, stop=True),
                             [B_const] + B_R3[1], [B_PS[k]])
                    for g in range(8):
                        k = ks[g // 4]
                        Ug = R3[:, 0, i * 1024 + g * 128: i * 1024 + (g + 1) * 128]
                        P.op("dve", lambda e, k=k, g=g, Ug=Ug: e.scalar_tensor_tensor(out=Ug, in0=PS[k][:, (g % 4) * 128:(g % 4 + 1) * 128],
                                                                                    scalar=bsT[:, g:g + 1], in1=Ug, op0=ALU.add, op1=ALU.mult),
                             [B_PS[k], B_const] + B_R3[0], B_R3[0])
                if sample:
                    P.dma("sp", lambda e: e.dma_start(out=nk_s, in_=KTOK[0:16, 0, :]), B_KTOK, reads=[B_KTOK], is_out=True)
                    P.dma("sp", lambda e: e.dma_start(out=nv_s.rearrange("p (k d) -> p k d", k=2), in_=VA[0:16, 1, :, 0:64]), B_VA, reads=[B_VA], is_out=True)
                    P.dma("sp", lambda e: e.dma_start(out=nsgu, in_=R3[0:16, 1, 0:1024]), B_R3[1][0], reads=B_R3[1], is_out=True)
                elif tiles[-1] == nt - 1:
                    il = ng - 1
                    P.dma("sp", lambda e: e.dma_start(out=nk_p, in_=KTOK[:, il, :]), B_KTOK, reads=[B_KTOK], is_out=True)
                    P.dma("sp", lambda e: e.dma_start(out=nv_p.rearrange("p (k d) -> p k d", k=2), in_=VA[:, 1 + il, :, 0:64]), B_VA, reads=[B_VA], is_out=True)
                for i in range(ng):
                    for src_slot, dst0, dB in ((2, 0, B_R1[0]), (0, 8, B_R1[1])):
                        for f4 in range(2):
                            k = nps()
                            for j in range(4):
                                f = f4 * 4 + j
                                P.op("pe", lambda e, k=k, j=j, f=f, i=i, src_slot=src_slot: e.transpose(
                                    PS[k][:, j * 128:(j + 1) * 128], R3[:, src_slot, i * 1024 + f * 128: i * 1024 + (f + 1) * 128], ident[:]),
                                    B_R3[src_slot] + [B_const], [B_PS[k]])
                            P.op("act", lambda e, k=k, f4=f4, i=i, dst0=dst0: e.activation(
                                R1[:, dst0 + f4 * 4: dst0 + (f4 + 1) * 4, i * 128:(i + 1) * 128], PS[k][:, :].rearrange("p (a t) -> p a t", a=4), AF.Copy),
                                [B_PS[k]], [dB])
            items.append((None, mixers))

            SGA = XG[:, 0:1024]
            SGB = XG[:, 1024:2048]

            def gate_block(which, nb):
                def fn(b):
                    dst = SGA if which == 0 else SGB
                    for i in range(ng):
                        k = nps()
                        for kc in range(16):
                            P.op("pe", lambda e, k=k, kc=kc, i=i, b=b: e.matmul(PS[k][:, 0:256], XNT[:, kc, i * 128:(i + 1) * 128], WB[b][:, kc, :],
                                                                               start=(kc == 0), stop=(kc == 15)), B_WB[b] + B_R2, [B_PS[k]])
                        P.op("act", lambda e, k=k, i=i, dst=dst: e.activation(dst[:, i * 256:(i + 1) * 256], PS[k][:, 0:256], AF.Sigmoid),
                             [B_PS[k]], [B_XG])
                return fn

            def papb_block(nb):
                def fn(b):
                    for i in range(ng):
                        ka = nps()
                        kb_ = nps()
                        for kc in range(8):
                            P.op("pe", lambda e, ka=ka, kc=kc, i=i, b=b: e.matmul(PS[ka][:, 0:256], R1[:, kc, i * 128:(i + 1) * 128], WB[b][:, kc, :],
                                                                                 start=(kc == 0), stop=(kc == 7)), B_WB[b] + B_R1, [B_PS[ka]])
                        for kc in range(8):
                            P.op("pe", lambda e, kb_=kb_, kc=kc, i=i, b=b: e.matmul(PS[kb_][:, 0:256], R1[:, 8 + kc, i * 128:(i + 1) * 128], WB[b][:, 8 + kc, :],
                                                                                   start=(kc == 0), stop=(kc == 7)), B_WB[b] + B_R1, [B_PS[kb_]])
                        sa = SGA[:, i * 256:(i + 1) * 256]
                        sb_ = SGB[:, i * 256:(i + 1) * 256]
                        P.op("dve", lambda e, ka=ka, sa=sa: e.tensor_tensor(sa, sa, PS[ka][:, 0:256], ALU.mult), [B_PS[ka], B_XG], [B_XG])
                        P.op("dve", lambda e, kb_=kb_, sb_=sb_: e.tensor_tensor(sb_, sb_, PS[kb_][:, 0:256], ALU.mult), [B_PS[kb_], B_XG], [B_XG])
                        P.op("dve", lambda e, sa=sa, sb_=sb_: e.tensor_tensor(sa, sa, sb_, ALU.add), [B_XG], [B_XG])
                        k = nps()
                        for j in range(2):
                            P.op("pe", lambda e, k=k, j=j, sa=sa: e.transpose(PS[k][:, j * 128:(j + 1) * 128], sa[:, j * 128:(j + 1) * 128], ident[:]),
                                 [B_XG, B_const], [B_PS[k]])
                        HTv = R4
                        P.op("dve", lambda e, k=k, i=i, HTv=HTv: e.tensor_copy(HTv[:, nb * 2:nb * 2 + 2, i * 128:(i + 1) * 128],
                                                                              PS[k][:, 0:256].rearrange("p (a t) -> p a t", a=2)),
                             [B_PS[k]], B_R4)
                return fn
            for nb in range(8):
                items.append((full(w_in_v, 3328 + nb * 256), gate_block(0, nb)))
                items.append((full(w_in_v, 5376 + nb * 256), gate_block(1, nb)))
                items.append(([(lambda b: WB[b][:, 0:8, :], w_pa_v[:, :, nb * 256:(nb + 1) * 256]),
                               (lambda b: WB[b][:, 8:16, :], w_pb_v[:, :, nb * 256:(nb + 1) * 256])], papb_block(nb)))

            def wout_block(nb):
                def fn(b):
                    HTv = R4
                    for i in range(ng):
                        k = nps()
                        for kc in range(16):
                            P.op("pe", lambda e, k=k, kc=kc, i=i, b=b: e.matmul(PS[k][:, 0:256], HTv[:, kc, i * 128:(i + 1) * 128], WB[b][:, kc, :],
                                                                               start=(kc == 0), stop=(kc == 15)), B_WB[b] + B_R4, [B_PS[k]])
                        xs_ = X[:, i, nb * 256:(nb + 1) * 256]
                        P.op("dve", lambda e, k=k, xs_=xs_: e.tensor_tensor(xs_, xs_, PS[k][:, 0:256], ALU.add), [B_PS[k], B_X[i]], [B_X[i]])
                return fn
            for nb in range(8):
                items.append((full(w_out_v, nb * 256), wout_block(nb)))

            def peer_norm(_):
                for i in range(ng):
                    norm_T(i, gffnT, XNTb, B_R2, 32 + i)
            items.append((None, peer_norm))

            def wq_block(blk):
                def fn(b):
                    for j in range(2):
                        c = blk * 2 + j
                        k = nps()
                        for kc in range(16):
                            P.op("pe", lambda e, k=k, kc=kc, j=j, b=b: e.matmul(PS[k][:, 0:T], WB[b][:, kc, j * 128:(j + 1) * 128], XNTb[:, kc, 0:T],
                                                                               start=(kc == 0), stop=(kc == 15)), B_WB[b] + B_R2, [B_PS[k]])
                        P.op("act", lambda e, k=k, c=c: e.activation(QPT[:, c, 0:T], PS[k][:, 0:T], AF.Copy), [B_PS[k]], [B_QPT])
                return fn
            for blk in range(8):
                items.append((full(w_q_v, blk * 256), wq_block(blk)))

            if "it=" in dbg:
                items = items[:int(dbg.split("it=")[1].split(",")[0])]
            run_items(items)

            for i, gt in enumerate(tiles):
                if "nopeer" in dbg:
                    break
                SC = R3[:, 0, :].rearrange("p (a n) -> p a n", a=16)
                TMP = R3[:, 1, :].rearrange("p (a n) -> p a n", a=16)
                CAND = R3[:, 2, :]
                sks = [nps(0, 4) for _ in range(4)]
                for hp in range(16):
                    k = sks[hp // 4]
                    P.op("pe", lambda e, k=k, hp=hp, i=i: e.matmul(PS[k][:, (hp % 4) * 128:(hp % 4 + 1) * 128], QPT[:, hp, i * 128:(i + 1) * 128],
                                                                  skT[:, hp, :], start=True, stop=True), [B_QPT, B_const], [B_PS[k]])
                for q4 in range(4):
                    P.op("act", lambda e, q4=q4: e.activation(R3[:, 0, q4 * 512:(q4 + 1) * 512], PS[sks[q4]][:, :], AF.Copy), [B_PS[sks[q4]]], B_R3[0])
                for hp in range(16):
                    P.op("dve", lambda e, hp=hp: e.max(out=SV[:, hp, 0:8], in_=SC[:, hp, :]), B_R3[0], [B_SV])
                for hp in range(16):
                    P.op("dve", lambda e, hp=hp: e.match_replace(out=TMP[:, hp, :], in_to_replace=SV[:, hp, 0:8], in_values=SC[:, hp, :], imm_value=-1e30),
                         B_R3[0] + [B_SV], B_R3[1])
                for hp in range(16):
                    P.op("dve", lambda e, hp=hp: e.max(out=SV[:, hp, 8:16], in_=TMP[:, hp, :]), B_R3[1], [B_SV])
                for hp in range(16):
                    for o in (0, 8):
                        P.op("dve", lambda e, hp=hp, o=o: e.max_index(out=SI[:, hp, o:o + 8], in_max=SV[:, hp, o:o + 8], in_values=SC[:, hp, :]),
                             B_R3[0] + [B_SV], [B_SI])
                P.op("dve", lambda e: e.tensor_copy(SIF[:], SI[:]), [B_SI], [B_SIF])
                sv4 = SV[:, :, :].rearrange("p (h two) k -> p h two k", two=2)
                sif4 = SIF[:, :, :].rearrange("p (h two) k -> p h two k", two=2)
                CAND4 = CAND.rearrange("p (h a b) -> p h a b", h=8, a=16)
                P.op("dve", lambda e: e.tensor_tensor(CAND4, sv4[:, :, 0, :].unsqueeze(3).broadcast_to([128, 8, 16, 16]),
                                                      sv4[:, :, 1, :].unsqueeze(2).broadcast_to([128, 8, 16, 16]), ALU.add), [B_SV], B_R3[2])
                CAND2 = CAND.rearrange("p (h m) -> p h m", h=8)
                TMPC = R3[:, 1, :].rearrange("p (h m) -> p h m", h=8)
                for h in range(8):
                    P.op("dve", lambda e, h=h: e.max(out=CV[:, h, 0:8], in_=CAND2[:, h, :]), B_R3[2], [B_CV])
                for h in range(8):
                    P.op("dve", lambda e, h=h: e.match_replace(out=TMPC[:, h, :], in_to_replace=CV[:, h, 0:8], in_values=CAND2[:, h, :], imm_value=-1e30),
                         B_R3[2] + [B_CV], B_R3[1])
                for h in range(8):
                    P.op("dve", lambda e, h=h: e.max(out=CV[:, h, 8:16], in_=TMPC[:, h, :]), B_R3[1], [B_CV])
                for h in range(8):
                    for o in (0, 8):
                        P.op("dve", lambda e, h=h, o=o: e.max_index(out=CI[:, h, o:o + 8], in_max=CV[:, h, o:o + 8], in_values=CAND2[:, h, :]),
                             B_R3[2] + [B_CV], [B_CI])
                CIf = CI[:, :, :].rearrange("p h k -> p (h k)")
                P.op("dve", lambda e: e.tensor_scalar(IK[:], CIf, shc[:, 0:1], None, ALU.logical_shift_right), [B_CI, B_const], [B_IK])
                P.op("dve", lambda e: e.tensor_scalar(JK[:], CIf, shc[:, 1:2], None, ALU.bitwise_and), [B_CI, B_const], [B_IK])
                P.op("dve", lambda e: e.tensor_copy(IKF[:, :, :].rearrange("p h k -> p (h k)"), IK[:]), [B_IK], [B_IKF])
                P.op("dve", lambda e: e.tensor_copy(JKF[:, :, :].rearrange("p h k -> p (h k)"), JK[:]), [B_IK], [B_IKF])
                io4 = iota16[:, :].unsqueeze(1).unsqueeze(1).broadcast_to([128, 8, 16, 16])
                for w_, (KF, SEL) in enumerate(((IKF, SEL0), (JKF, SEL1))):
                    E4w = R2f[:, w_ * 2048:(w_ + 1) * 2048].rearrange("p (h a b) -> p h a b", h=8, a=16)
                    E4r = E4w
                    P.op("dve", lambda e, KF=KF, E4w=E4w: e.tensor_tensor(E4w, KF[:, :, :].unsqueeze(3).broadcast_to([128, 8, 16, 16]), io4, ALU.is_equal),
                         [B_IKF, B_const], [B_R2[w_]])
                    P.op("dve", lambda e, E4w=E4w, E4r=E4r, w_=w_: e.tensor_tensor(E4w, E4r, sif4[:, :, w_, :].unsqueeze(2).broadcast_to([128, 8, 16, 16]), ALU.mult),
                         [B_R2[w_], B_SIF], [B_R2[w_]])
                    P.op("dve", lambda e, SEL=SEL, E4r=E4r: e.tensor_reduce(SEL[:], E4r, AX.X, ALU.add), [B_R2[w_]], [B_SEL])
                P.op("dve", lambda e: e.scalar_tensor_tensor(out=EIF[:], in0=SEL0[:, :, :].rearrange("p h k -> p (h k)"), scalar=128.0,
                                                             in1=SEL1[:, :, :].rearrange("p h k -> p (h k)"), op0=ALU.mult, op1=ALU.add), [B_SEL], [B_SEL])
                P.op("dve", lambda e: e.tensor_copy(EIDX[:], EIF[:]), [B_SEL], [B_EIDX])
                P.op("dve", lambda e: e.tensor_tensor(EW[:], CV[:], CV[:, :, 0:1].broadcast_to([128, 8, 16]), ALU.subtract), [B_CV], [B_GW])
                P.op("act", lambda e: e.activation(EW[:], EW[:], AF.Exp), [B_GW], [B_GW])
                P.op("dve", lambda e: e.tensor_reduce(SUMW[:], EW[:], AX.X, ALU.add), [B_GW], [B_GW])
                P.op("dve", lambda e: e.reciprocal(RW[:], SUMW[:]), [B_GW], [B_GW])
                P.op("dve", lambda e: e.tensor_tensor(GW[:], EW[:], RW[:, :].unsqueeze(2).broadcast_to([128, 8, 16]), ALU.mult), [B_GW], [B_GW])
                GWf = GW[:, :, :].rearrange("p h k -> p (h k)")
                P.op("dve", lambda e, i=i: e.scalar_tensor_tensor(out=XG[:], in0=X[:, i, :], scalar=scol(40 + i), in1=gffn[:], op0=ALU.mult, op1=ALU.mult),
                     [B_X[i], B_small, B_const], [B_XG])
                acc = [4, 5, 6, 7]
                NB = 9
                JUNK = R4f[0:nr, 0:2048]

                def UV(s):
                    if s < 3:
                        return R3[:, s, :].bitcast(BF16)[0:nr, :]
                    if s < 5:
                        return VEB[:, 2 * (s - 3):2 * (s - 3) + 2, :].rearrange("p a b -> p (a b)")[0:nr, :]
                    if s < 7:
                        return R2f[:, (s - 5) * 2048:(s - 4) * 2048].bitcast(BF16)[0:nr, :]
                    return WBraw[s - 7][:, :].bitcast(BF16)[0:nr, :]

                def BUV(s):
                    if s < 3:
                        return B_R3[s]
                    if s < 5:
                        return [B_VE[2 * (s - 3)], B_VE[2 * (s - 3) + 1]]
                    if s < 7:
                        return [B_R2[s - 5]]
                    return B_WB[s - 7]
                LA = NB - 2

                def gather(cg):
                    sg_ = cg % NB
                    P.dma("pool", lambda e, cg=cg, sg_=sg_: e.indirect_dma_start(out=UV(sg_), out_offset=None, in_=euvb,
                                                                                 in_offset=bass.IndirectOffsetOnAxis(ap=EIDX[0:nr, cg:cg + 1], axis=0)),
                          BUV(sg_)[0], reads=[B_EIDX, B_TUV], writes=BUV(sg_))
                if "nogather" not in dbg:
                    for cg in range(LA):
                        gather(cg)
                for c in range(129 if "nogather" not in dbg else 0):
                    if c + LA < 128:
                        gather(c + LA)
                    if c < 128:
                        sb_ = c % NB
                        p4 = c % 4
                        P.op("dve", lambda e, c=c, sb_=sb_: e.scalar_tensor_tensor(out=JUNK, in0=UV(sb_)[:, 0:2048], scalar=1.0, in1=XG[0:nr, :],
                                                                                   op0=ALU.mult, op1=ALU.mult, accum_out=AA[0:nr, c:c + 1]),
                             BUV(sb_) + [B_XG], [B_R4[0], B_AA[p4]])
                        P.op("act", lambda e, c=c: e.activation(AGL[0:nr, c:c + 1], AA[0:nr, c:c + 1], AF.Gelu_apprx_tanh), [B_AA[p4]], [B_AGL[p4]])
                    if c >= 1:
                        c1 = c - 1
                        sb_ = c1 % NB
                        p4 = c1 % 4
                        P.op("act", lambda e, c1=c1: e.activation(HW[0:nr, c1:c1 + 1], AGL[0:nr, c1:c1 + 1], AF.Copy, scale=GWf[0:nr, c1:c1 + 1]),
                             [B_AGL[p4], B_GW], [B_HW[p4]])
                        P.op("act", lambda e, c1=c1, p4=p4: e.activation(DIAG[p4][0:nr, :], ident[0:nr, :], AF.Copy, scale=HW[0:nr, c1:c1 + 1]),
                             [B_HW[p4], B_const], [B_DIAG[p4]])
                        for j in range(4):
                            P.op("pe", lambda e, j=j, sb_=sb_, p4=p4, c1=c1: e.matmul(PS[acc[j]][:, :], DIAG[p4][0:nr, :], UV(sb_)[:, 2048 + j * 512:2048 + (j + 1) * 512],
                                                                                  start=(c1 == 0), stop=(c1 == 127)), [B_DIAG[p4]] + BUV(sb_), [B_PS[acc[j]]])
                for j in range(4):
                    xs_ = X[0:nr, i, j * 512:(j + 1) * 512]
                    P.op("dve", lambda e, j=j, xs_=xs_: e.tensor_tensor(xs_, xs_, PS[acc[j]][0:nr, :], ALU.add), [B_PS[acc[j]], B_X[i]], [B_X[i]])
                P.op("act", lambda e, i=i: e.activation(R4f[0:nr, 2048:4096], X[0:nr, i, :], AF.Square, accum_out=small[0:nr, 48 + i:49 + i]),
                     [B_X[i]], [B_R4[1], B_small])
                rstd_from(48 + i, 56 + i, D, rows=nr)
                P.op("dve", lambda e, i=i: e.scalar_tensor_tensor(out=X[0:nr, i, :], in0=X[0:nr, i, :], scalar=small[0:nr, 56 + i:57 + i], in1=gfin[0:nr, :],
                                                                  op0=ALU.mult, op1=ALU.mult), [B_X[i], B_small, B_const], [B_X[i]])
                if sample:
                    P.dma("sp", lambda e, i=i: e.dma_start(out=y_s, in_=X[0:16, i, :]), B_X[i], reads=[B_X[i]], is_out=True)
                else:
                    P.dma("sp", lambda e, i=i, gt=gt: e.dma_start(out=y_p[gt * 128:(gt + 1) * 128, :], in_=X[:, i, :]), B_X[i], reads=[B_X[i]], is_out=True)

        for g0 in range(0, nt, G):
            do_group(list(range(g0, g0 + G)), False)
            first_group[0] = False
        if "nosample" not in dbg:
            do_group([0], True)
        P.finish()
        P.emit()
    return nc


def _t5_bucket_np(rel):
    try:
        import jax
        import jax.numpy as jnp
        with jax.default_device(jax.devices("cpu")[0]):
            r = jnp.asarray(rel, dtype=jnp.int32)
            half = 16
            max_exact = 8
            ret = jnp.where(r > 0, half, 0)
            n = jnp.abs(r)
            nf = jnp.maximum(n, 1).astype(jnp.float32)
            large = max_exact + (jnp.log(nf / max_exact) / math.log(128 / max_exact) * (half - max_exact)).astype(jnp.int32)
            large = jnp.minimum(large, half - 1)
            return np.asarray(ret + jnp.where(n < max_exact, n, large))
    except Exception:
        r = np.asarray(rel, dtype=np.int32)
        ret = np.where(r > 0, 16, 0)
        n = np.abs(r)
        nf = np.maximum(n, 1).astype(np.float32)
        large = 8 + (np.log(nf / np.float32(8)) / np.float32(math.log(16.0)) * np.float32(8)).astype(np.int32)
        large = np.minimum(large, 15)
        return ret + np.where(n < 8, n, large)


_NC_CACHE = {}


def kernel(x_prompt, x_sample, cache_k_swa, cache_v_swa, norm_mix_g, w_in, sgu_norm_g, sgu_w_s, sgu_b_s,
           attn_sinks, rel_bias_table, w_branch_attn, w_branch_sgu, w_out, norm_ffn_g, peer_w_query,
           peer_sub_keys, peer_expert_u, peer_expert_v, norm_final_g):
    f = lambda a: np.ascontiguousarray(np.asarray(a), dtype=np.float32)
    if "nc" not in _NC_CACHE:
        _NC_CACHE["nc"] = build_program()
    nc = _NC_CACHE["nc"]
    kb = np.arange(2)[:, None, None]; kk = np.arange(128)[None, :, None]; qq = np.arange(128)[None, None, :]
    rel = (kb - 1) * 128 + kk - qq
    bkt = _t5_bucket_np(rel).reshape(-1)
    oh = np.zeros((32, 32768), np.float32)
    oh[bkt, np.arange(32768)] = 1.0
    s_i = np.arange(128)[:, None, None]; t_i = np.arange(128)[None, None, :]
    trilT = np.ascontiguousarray(np.broadcast_to((t_i >= s_i), (128, 8, 128))).astype(np.float32)
    shared = dict(
        w_in=f(w_in[0]), w_pa=f(w_branch_attn[0]), w_pb=f(w_branch_sgu[0]), w_out=f(w_out[0]), w_q=f(peer_w_query[0]),
        eu=f(peer_expert_u[0]), ev=f(peer_expert_v[0]),
        gmixT=f(np.asarray(norm_mix_g[0]).reshape(16, 128).T), gffnT=f(np.asarray(norm_ffn_g[0]).reshape(16, 128).T),
        gffn_bc=f(np.broadcast_to(np.asarray(norm_ffn_g[0])[None, :], (128, D))),
        gfin_bc=f(np.broadcast_to(np.asarray(norm_final_g)[None, :], (128, D))),
        sgug_bc=f(np.broadcast_to(np.asarray(sgu_norm_g[0])[None, :], (128, 1024))),
        wsT=f(np.asarray(sgu_w_s[0]).transpose(2, 0, 1)), trilT=trilT, bsT=f(np.asarray(sgu_b_s[0]).T),
        sinks_bc=f(np.broadcast_to(np.asarray(attn_sinks[0])[None, :], (128, 16))), table=f(rel_bias_table), oh=oh,
        skT=f(np.asarray(peer_sub_keys[0]).reshape(16, 128, 128).transpose(2, 0, 1)),
        ident=np.eye(128, dtype=np.float32), iota16=f(np.broadcast_to(np.arange(16, dtype=np.float32)[None, :], (128, 16))),
        shc=np.ascontiguousarray(np.broadcast_to(np.array([[4, 15]], np.uint32), (128, 2))),
    )
    xpn = np.asarray(x_prompt); xsn = np.asarray(x_sample); ckn = np.asarray(cache_k_swa); cvn = np.asarray(cache_v_swa)
    in_maps = []
    for c in range(8):
        m = dict(shared)
        m["xp"] = f(xpn[c]); m["xs"] = f(xsn[c])
        m["ck"] = f(ckn[0, c].reshape(128, 128)); m["cv"] = f(cvn[0, c].reshape(128, 128))
        in_maps.append(m)
    res = run_bass_kernel_spmd(nc, in_maps, core_ids=list(range(8)))
    r = res.results
    y_prompt = np.stack([r[c]["y_p"] for c in range(8)]).astype(np.float32)
    y_sample = np.stack([r[c]["y_s"] for c in range(8)]).astype(np.float32)
    nk_p = np.stack([r[c]["nk_p"].reshape(128, 2, 64) for c in range(8)])[None].astype(np.float32)
    nv_p = np.stack([r[c]["nv_p"].reshape(128, 2, 64) for c in range(8)])[None].astype(np.float32)
    nk_s = np.stack([r[c]["nk_s"].reshape(16, 2, 64) for c in range(8)])[None].astype(np.float32)
    nv_s = np.stack([r[c]["nv_s"].reshape(16, 2, 64) for c in range(8)])[None].astype(np.float32)
    nsg = np.stack([r[c]["nsgu"] for c in range(8)])[None].astype(np.float32)
    return (y_prompt, y_sample, nk_p, nv_p, nk_s, nv_s, nsg)
```

```python
import math
import numpy as np
from contextlib import ExitStack
import concourse.bass as bass
import concourse.mybir as mybir
from concourse.bass_utils import run_bass_kernel_spmd

F32 = mybir.dt.float32
F32R = mybir.dt.float32r
BF16 = mybir.dt.bfloat16
I32 = mybir.dt.int32
U32 = mybir.dt.uint32
AF = mybir.ActivationFunctionType
ALU = mybir.AluOpType
AX = mybir.AxisListType

D = 2048
DIN = 7424
SEQ = 2048
NT = SEQ // 128
G = 2
TMAX = G * 128
EPS = 1e-6
WCOL = 256


class Buf:
    __slots__ = ("name", "lw", "rd", "dsem", "dcnt", "excl")

    def __init__(self, name, excl=False):
        self.name = name
        self.excl = excl
        self.lw = None
        self.rd = {}
        self.dsem = {}
        self.dcnt = {}


class Prog:
    ENG = ("sp", "act", "dve", "pool", "pe")

    def __init__(self, nc, es):
        self.nc = nc
        self.es = es
        self.st = {e: [] for e in self.ENG}
        self.sem = {}
        self.cnt = {}
        self.waited = {e: {} for e in self.ENG}
        self.nsem = 0
        self.out_toks = []
        for e in self.ENG:
            self._new_sem(e)

    def _mk(self, name):
        self.nsem += 1
        return self.es.enter_context(self.nc.semaphore(f"{name}{self.nsem}"))

    def _new_sem(self, e):
        self.sem[e] = self._mk("e" + e)
        self.cnt[e] = 0

    def _wait(self, e, tok):
        sem, val, src = tok
        if src == e and e == "pe":
            return
        w = self.waited[e]
        if w.get(id(sem), -1) >= val:
            return
        w[id(sem)] = val
        self.st[e].append(("w", sem, val))

    def _deps(self, e, reads, writes):
        need = {}
        def add(tok):
            k = id(tok[0])
            if k not in need or need[k][1] < tok[1]:
                need[k] = tok
        for b in reads:
            if b.lw is not None:
                add(b.lw)
            if b.excl:
                for t in b.rd.values():
                    if t[2] != e:
                        add(t)
        for b in writes:
            if b.lw is not None:
                add(b.lw)
            for t in b.rd.values():
                add(t)
        for tok in need.values():
            self._wait(e, tok)

    def _commit(self, tok, reads, writes):
        for b in reads:
            k = id(tok[0])
            if k not in b.rd or b.rd[k][1] < tok[1]:
                b.rd[k] = tok
        for b in writes:
            b.lw = tok
            b.rd = {}

    def op(self, e, fn, reads=(), writes=()):
        self._deps(e, reads, writes)
        if self.cnt[e] >= 30000:
            self._new_sem(e)
        self.cnt[e] += 1
        tok = (self.sem[e], self.cnt[e], e)
        self.st[e].append(("o", fn, self.sem[e], 1))
        self._commit(tok, reads, writes)
        return tok

    def dma(self, q, fn, dbuf, reads=(), writes=(), is_out=False):
        self._deps(q, reads, writes)
        if q not in dbuf.dsem or dbuf.dcnt[q] >= 48000:
            dbuf.dsem[q] = self._mk("d")
            dbuf.dcnt[q] = 0
        dbuf.dcnt[q] += 16
        tok = (dbuf.dsem[q], dbuf.dcnt[q], "dma")
        self.st[q].append(("o", fn, dbuf.dsem[q], 16))
        self._commit(tok, reads, writes)
        if is_out:
            self.out_toks.append(tok)
        return tok

    def finish(self):
        for tok in self.out_toks:
            self._wait("sp", tok)

    def emit(self):
        blk = self.es.enter_context(self.nc.Block())

        def run(e):
            def f(eng):
                for it in self.st[e]:
                    if it[0] == "w":
                        eng.wait_ge(it[1], it[2])
                    else:
                        it[1](eng).then_inc(it[2], it[3])
            return f
        blk.sync(run("sp"))
        blk.scalar(run("act"))
        blk.vector(run("dve"))
        blk.gpsimd(run("pool"))
        blk.tensor(run("pe"))


def build_program(nt=NT, dbg=""):
    SEQ = nt * 128
    nc = bass.Bass("TRN2", target_bir_lowering=False)

    def din(name, shape, dt=F32):
        return nc.dram_tensor(name, list(shape), dt, kind="ExternalInput").ap()

    def dout(name, shape, dt=F32):
        return nc.dram_tensor(name, list(shape), dt, kind="ExternalOutput").ap()

    xp = din("xp", [SEQ, D]); xs = din("xs", [16, D])
    ck = din("ck", [128, 128]); cvv = din("cv", [128, 128])
    w_in = din("w_in", [D, DIN]); w_pa = din("w_pa", [1024, D]); w_pb = din("w_pb", [1024, D])
    w_out = din("w_out", [D, D]); w_q = din("w_q", [D, D])
    eu = din("eu", [16384, D]); ev = din("ev", [16384, D])
    gmixT_d = din("gmixT", [128, 16]); gffnT_d = din("gffnT", [128, 16])
    gffn_d = din("gffn_bc", [128, D]); gfin_d = din("gfin_bc", [128, D]); sgug_d = din("sgug_bc", [128, 1024])
    wsT_d = din("wsT", [128, 8, 128]); trilT_d = din("trilT", [128, 8, 128]); bsT_d = din("bsT", [128, 8])
    sinks_d = din("sinks_bc", [128, 16]); table_d = din("table", [32, 16]); oh_d = din("oh", [32, 32768])
    skT_d = din("skT", [128, 16, 128]); ident_d = din("ident", [128, 128]); iota_d = din("iota16", [128, 16])
    shc_d = din("shc", [128, 2], U32)
    y_p = dout("y_p", [SEQ, D]); y_s = dout("y_s", [16, D])
    nk_p = dout("nk_p", [128, 128]); nv_p = dout("nv_p", [128, 128])
    nk_s = dout("nk_s", [16, 128]); nv_s = dout("nv_s", [16, 128]); nsgu = dout("nsgu", [16, 1024])
    bscr = nc.dram_tensor("bscr", [16, 32768], F32, kind="Internal").ap()
    euvb = nc.dram_tensor("euvb", [16384, 2 * D], BF16, kind="Internal").ap()
    wscr = nc.dram_tensor("wscr", [64, 128, 2048], F32, kind="Internal").ap()

    w_in_v = w_in.rearrange("(kc p) n -> p kc n", p=128)
    w_pa_v = w_pa.rearrange("(kc p) n -> p kc n", p=128)
    w_pb_v = w_pb.rearrange("(kc p) n -> p kc n", p=128)
    w_out_v = w_out.rearrange("(kc p) n -> p kc n", p=128)
    w_q_v = w_q.rearrange("(kc p) n -> p kc n", p=128)

    with ExitStack() as es:
        P = Prog(nc, es)

        def sb(name, shape, dt=F32):
            return es.enter_context(nc.sbuf_tensor("s_" + name, list(shape), dt))

        biasT = sb("biasT", [128, 16, 2, 128]); B_bias = Buf("biasT")
        gffn = sb("gffn", [128, D]); gfin = sb("gfin", [128, D]); sgug = sb("sgug", [128, 1024])
        skT = sb("skT", [128, 16, 128]); wmT = sb("wmT", [128, 8, 128]); trilT = sb("trilT", [128, 8, 128])
        ident = sb("ident", [128, 128]); gmixT = sb("gmixT", [128, 16]); gffnT = sb("gffnT", [128, 16])
        bsT = sb("bsT", [128, 8]); esink = sb("esink", [128, 16]); iota16 = sb("iota16", [128, 16])
        shc = sb("shc", [128, 2], U32); tab = sb("tab", [32, 16])
        B_const = Buf("const")
        X = sb("X", [128, G, D]); B_X = [Buf(f"X{i}") for i in range(G)]
        R1 = sb("R1", [128, 16, TMAX], BF16); B_R1 = [Buf("R1a"), Buf("R1b")]
        QPT = sb("QPT", [128, 16, TMAX]); B_QPT = Buf("QPT")
        R2f = sb("R2", [128, 16 * TMAX]); B_R2 = [Buf("R2a"), Buf("R2b")]
        XNTb = R2f[:, :].bitcast(BF16)[:, 0:16 * TMAX].rearrange("p (k t) -> p k t", k=16)
        XN2T = R2f[:, :].rearrange("p (k t) -> p k t", k=16)
        R4 = sb("R4", [128, 16, TMAX], BF16); B_R4 = [Buf("R4a"), Buf("R4b")]
        R4f = R4[:, :, :].rearrange("p a b -> p (a b)")
        WBraw = [sb(f"WB{i}", [128, 2048]) for i in range(2)]
        WB = [w[:, :].bitcast(BF16).rearrange("p (k n) -> p k n", k=16) for w in WBraw]
        WBf32 = [w[:, :].rearrange("p (k n) -> p k n", k=16) for w in WBraw]
        B_WB = [[Buf(f"WB{i}")] for i in range(2)]
        VEB = sb("VEB", [128, 4, 2048], BF16); B_VE = [Buf(f"VE{i}") for i in range(4)]
        for h_ in range(2):
            vv = VEB[:, 2 * h_:2 * h_ + 2, :].rearrange("p a b -> p (a b)")
            WB.append(vv.rearrange("p (k n) -> p k n", k=16))
            WBf32.append(vv.bitcast(F32).rearrange("p (k n) -> p k n", k=16))
            WBraw.append(vv.bitcast(F32))
            B_WB.append([B_VE[2 * h_], B_VE[2 * h_ + 1]])
        NWB = 4
        R3 = sb("R3", [128, 3, 2048]); B_R3 = [[Buf(f"R3_{j}a"), Buf(f"R3_{j}b")] for j in range(3)]
        XG = sb("XG", [128, 2048]); B_XG = Buf("XG")
        KT = sb("KT", [64, 2, 128 + TMAX], BF16); B_KT = Buf("KT")
        VA = sb("VA", [128, 1 + G, 2, 65]); B_VA = Buf("VA")
        KTOK = sb("KTOK", [128, G, 128]); B_KTOK = Buf("KTOK")
        PT = [sb(f"PT{i}", [128, 2, 2, 128]) for i in range(2)]; B_PT = [[Buf(f"PT{i}a"), Buf(f"PT{i}b")] for i in range(2)]
        DIAG = [sb(f"DIAG{i}", [128, 128], BF16) for i in range(4)]; B_DIAG = [Buf(f"DG{i}") for i in range(4)]
        small = sb("small", [128, 64]); B_small = Buf("small")
        DEN = sb("DEN", [128, 16]); RDEN = sb("RDEN", [128, 16]); B_DEN = Buf("DEN")
        SV = sb("SV", [128, 16, 16]); SI = sb("SI", [128, 16, 16], U32); SIF = sb("SIF", [128, 16, 16])
        CV = sb("CV", [128, 8, 16]); CI = sb("CI", [128, 8, 16], U32)
        IK = sb("IK", [128, 128], U32); JK = sb("JK", [128, 128], U32)
        IKF = sb("IKF", [128, 8, 16]); JKF = sb("JKF", [128, 8, 16])
        SEL0 = sb("SEL0", [128, 8, 16]); SEL1 = sb("SEL1", [128, 8, 16])
        EIF = sb("EIF", [128, 128]); EIDX = sb("EIDX", [128, 128], I32)
        EW = sb("EW", [128, 8, 16]); GW = sb("GW", [128, 8, 16]); SUMW = sb("SUMW", [128, 8]); RW = sb("RW", [128, 8])
        AA = sb("AA", [128, 128]); AGL = sb("AGL", [128, 128]); HW = sb("HW", [128, 128])
        B_SV = Buf("SV"); B_SI = Buf("SI"); B_SIF = Buf("SIF"); B_CV = Buf("CV"); B_CI = Buf("CI")
        B_IK = Buf("IK"); B_IKF = Buf("IKF"); B_SEL = Buf("SEL"); B_EIDX = Buf("EIDX"); B_GW = Buf("GW")
        B_AA = [Buf(f"AA{i}") for i in range(4)]; B_AGL = [Buf(f"AGL{i}") for i in range(4)]; B_HW = [Buf(f"HW{i}") for i in range(4)]
        PS = [es.enter_context(nc.psum_tensor(f"PS{i}", [128, 512], F32)) for i in range(8)]
        B_PS = [Buf(f"PS{i}", excl=True) for i in range(8)]
        psrr = [0]

        def nps(lo=0, hi=8):
            k = lo + psrr[0] % (hi - lo)
            psrr[0] += 1
            return k

        def scol(k):
            return small[:, k:k + 1]

        def rstd_from(ss_col, out_col, n, rows=128):
            P.op("dve", lambda e: e.tensor_scalar(small[0:rows, out_col:out_col + 1], small[0:rows, ss_col:ss_col + 1],
                                                  1.0 / n, EPS, ALU.mult, ALU.add), [B_small], [B_small])
            P.op("act", lambda e: e.activation(small[0:rows, out_col:out_col + 1], small[0:rows, out_col:out_col + 1], AF.Sqrt),
                 [B_small], [B_small])
            P.op("dve", lambda e: e.reciprocal(small[0:rows, out_col:out_col + 1], small[0:rows, out_col:out_col + 1]),
                 [B_small], [B_small])

        def ld(dst, src, buf=B_const):
            P.dma("sp", lambda e: e.dma_start(out=dst, in_=src), buf, writes=[buf])
        ld(gffn[:], gffn_d); ld(gfin[:], gfin_d); ld(sgug[:], sgug_d); ld(skT[:], skT_d)
        ld(wmT[:], wsT_d); ld(trilT[:], trilT_d); ld(ident[:], ident_d); ld(gmixT[:], gmixT_d); ld(gffnT[:], gffnT_d)
        ld(bsT[:], bsT_d); ld(esink[:], sinks_d); ld(iota16[:], iota_d); ld(shc[:], shc_d); ld(tab[:], table_d)
        P.op("dve", lambda e: e.tensor_tensor(wmT[:], wmT[:], trilT[:], ALU.mult), [B_const], [B_const])
        P.op("act", lambda e: e.activation(esink[:], esink[:], AF.Exp), [B_const], [B_const])
        P.op("dve", lambda e: e.memset(VA[:, :, :, 64:65], 1.0), [], [B_VA])
        for pc in range(8 if "nobias" not in dbg else 0):
            P.dma("sp", lambda e, pc=pc: e.dma_start(out=R3[0:32, 0, :], in_=oh_d[:, pc * 4096:pc * 4096 + 2048]), B_R3[0][0], writes=B_R3[0])
            P.dma("sp", lambda e, pc=pc: e.dma_start(out=R3[0:32, 1, :], in_=oh_d[:, pc * 4096 + 2048:(pc + 1) * 4096]), B_R3[1][0], writes=B_R3[1])
            for hf in range(2):
                for q4 in range(4):
                    k = nps()
                    P.op("pe", lambda e, k=k, hf=hf, q4=q4: e.matmul(PS[k][0:16, :], tab[:, :], R3[0:32, hf, q4 * 512:(q4 + 1) * 512],
                                                                    start=True, stop=True), [B_const] + B_R3[hf], [B_PS[k]])
                    P.op("act", lambda e, k=k, q4=q4: e.activation(XG[0:16, q4 * 512:(q4 + 1) * 512], PS[k][0:16, :], AF.Copy),
                         [B_PS[k]], [B_XG])
                P.dma("sp", lambda e, pc=pc, hf=hf: e.dma_start(out=bscr[:, pc * 4096 + hf * 2048: pc * 4096 + (hf + 1) * 2048], in_=XG[0:16, :]),
                      B_XG, reads=[B_XG], writes=[B_bias])
        if "nobias" not in dbg:
          P.dma("sp", lambda e: e.dma_start(out=biasT[:], in_=bscr.rearrange("h (kb kk qq) -> kk h kb qq", kb=2, kk=128)),
              B_bias, reads=[B_bias], writes=[B_bias])

        B_TUV = Buf("tblUV")
        RC = 2
        cvt = 0
        for t_, src in enumerate((eu, ev)):
            Bt = B_TUV
            srcv = src.rearrange("(p r) d -> p r d", p=128)
            dstv = euvb.rearrange("(p r) (t d) -> p r t d", p=128, t=2)[:, :, t_, :]
            for r0 in range(0, 128, RC):
                b = cvt % 2
                cvt += 1
                stg = VEB[:, 2 * b:2 * b + 2, :]
                sB = [B_VE[2 * b], B_VE[2 * b + 1]]
                P.dma("pool", lambda e, stg=stg, srcv=srcv, r0=r0: e.dma_start(out=stg, in_=srcv[:, r0:r0 + RC, :]), sB[0], writes=sB)
                P.dma("sp", lambda e, stg=stg, dstv=dstv, r0=r0: e.dma_start(out=dstv[:, r0:r0 + RC, :], in_=stg), Bt, reads=sB, writes=[Bt])

        wcount = [0]

        B_WSCR = Buf("wscr")
        first_group = [True]

        def load_block(specs, blk):
            b = wcount[0] % NWB
            wcount[0] += 1
            if first_group[0]:
                for (dst_fn, src) in specs:
                    P.dma("pool", lambda e, dst_fn=dst_fn, src=src, b=b: e.dma_start(out=dst_fn(b), in_=src),
                          B_WB[b][0], writes=B_WB[b])
                P.dma("sp", lambda e, b=b, blk=blk: e.dma_start(out=wscr[blk], in_=WBraw[b][:, :]), B_WSCR, reads=B_WB[b], writes=[B_WSCR])
            else:
                P.dma("sp", lambda e, b=b, blk=blk: e.dma_start(out=WBraw[b][:, :], in_=wscr[blk]), B_WB[b][0], reads=[B_WSCR], writes=B_WB[b])
            return b

        def run_items(items):
            widx = [k for k, it in enumerate(items) if it[0] is not None]
            bufs = {}
            depth = NWB - 1
            for j0 in range(min(depth, len(widx))):
                bufs[widx[j0]] = load_block(items[widx[j0]][0], j0)
            for k, (w, fn) in enumerate(items):
                if w is not None:
                    j = widx.index(k)
                    if j + depth < len(widx):
                        bufs[widx[j + depth]] = load_block(items[widx[j + depth]][0], j + depth)
                    fn(bufs[k])
                else:
                    fn(None)

        def full(src_v, c0, ncol=WCOL):
            return [(lambda b: WB[b][:, :, 0:ncol], src_v[:, :, c0:c0 + ncol])]

        def do_group(tiles, sample):
            ng = len(tiles)
            T = ng * 128
            nr = 16 if sample else 128
            XNT = XNTb
            QT = R1

            for i, gt in enumerate(tiles):
                if sample:
                    P.op("dve", lambda e, i=i: e.memset(X[:, i, :], 0.0), [], [B_X[i]])
                    P.dma("sp", lambda e, i=i: e.dma_start(out=X[0:16, i, :], in_=xs), B_X[i], writes=[B_X[i]])
                else:
                    P.dma("sp", lambda e, i=i, gt=gt: e.dma_start(out=X[:, i, :], in_=xp[gt * 128:(gt + 1) * 128, :]), B_X[i], writes=[B_X[i]])
            if sample:
                P.dma("sp", lambda e: e.dma_start(out=KTOK[:, 0, :], in_=ck), B_KTOK, writes=[B_KTOK])
                k = nps()
                P.op("pe", lambda e, k=k: e.transpose(PS[k][:, 0:128], KTOK[:, 0, :], ident[:]), [B_KTOK, B_const], [B_PS[k]])
                P.op("act", lambda e, k=k: e.activation(KT[0:64, 0, 0:128], PS[k][0:64, 0:128], AF.Copy), [B_PS[k]], [B_KT])
                P.op("act", lambda e, k=k: e.activation(KT[0:64, 1, 0:128], PS[k][64:128, 0:128], AF.Copy), [B_PS[k]], [B_KT])
                P.dma("sp", lambda e: e.dma_start(out=VA[:, 0, :, 0:64], in_=cvv.rearrange("p (k d) -> p k d", k=2)), B_VA, writes=[B_VA])
            elif tiles[0] != 0:
                P.op("act", lambda e: e.activation(KT[0:64, :, 0:128], KT[0:64, :, TMAX:TMAX + 128], AF.Copy), [B_KT], [B_KT])
                P.op("dve", lambda e: e.tensor_copy(VA[:, 0, :, 0:64], VA[:, G, :, 0:64]), [B_VA], [B_VA])

            def norm_T(i, gT, dstR, dstB, col):
                XR = R3[:, 2, :]
                P.op("act", lambda e: e.activation(XR, X[:, i, :], AF.Square, accum_out=scol(col)), [B_X[i]], B_R3[2] + [B_small])
                rstd_from(col, col + 8, D)
                P.op("dve", lambda e: e.tensor_scalar(XR, X[:, i, :], scol(col + 8), None, ALU.mult), [B_X[i], B_small], B_R3[2])
                for k4 in range(4):
                    k = nps()
                    for j in range(4):
                        kc = k4 * 4 + j
                        P.op("pe", lambda e, k=k, j=j, kc=kc: e.transpose(PS[k][:, j * 128:(j + 1) * 128], XR[:, kc * 128:(kc + 1) * 128], ident[:]),
                             B_R3[2] + [B_const], [B_PS[k]])
                    P.op("dve", lambda e, k=k, k4=k4: e.tensor_tensor(dstR[:, k4 * 4:(k4 + 1) * 4, i * 128:(i + 1) * 128],
                                                                      PS[k][:, :].rearrange("p (a t) -> p a t", a=4),
                                                                      gT[:, k4 * 4:(k4 + 1) * 4].unsqueeze(2).broadcast_to([128, 4, 128]), ALU.mult),
                         [B_PS[k], B_const], dstB)
            for i in range(ng):
                norm_T(i, gmixT, XNT, B_R2, i)

            items = []

            def q_block(blk):
                def fn(b):
                    for j in range(2):
                        pj = blk * 2 + j
                        k = nps()
                        for kc in range(16):
                            P.op("pe", lambda e, k=k, kc=kc, j=j, b=b: e.matmul(PS[k][:, 0:T], WB[b][:, kc, j * 128:(j + 1) * 128], XNT[:, kc, 0:T],
                                                                               start=(kc == 0), stop=(kc == 15)), B_WB[b] + B_R2, [B_PS[k]])
                        P.op("act", lambda e, k=k, pj=pj: e.activation(QT[0:64, 2 * pj, 0:T], PS[k][0:64, 0:T], AF.Copy, scale=0.125), [B_PS[k]], B_R1)
                        P.op("act", lambda e, k=k, pj=pj: e.activation(QT[0:64, 2 * pj + 1, 0:T], PS[k][64:128, 0:T], AF.Copy, scale=0.125), [B_PS[k]], B_R1)
                return fn
            for blk in range(4):
                items.append((full(w_in_v, blk * 256), q_block(blk)))

            def kv_fn(b):
                for i in range(ng):
                    k = nps()
                    for kc in range(16):
                        P.op("pe", lambda e, k=k, kc=kc, i=i, b=b: e.matmul(PS[k][:, 0:256], XNT[:, kc, i * 128:(i + 1) * 128], WB[b][:, kc, :],
                                                                           start=(kc == 0), stop=(kc == 15)), B_WB[b] + B_R2, [B_PS[k]])
                    P.op("act", lambda e, k=k, i=i: e.activation(KTOK[:, i, :], PS[k][:, 0:128], AF.Copy), [B_PS[k]], [B_KTOK])
                    P.op("dve", lambda e, k=k, i=i: e.tensor_copy(VA[:, 1 + i, :, 0:64], PS[k][:, 128:256].rearrange("p (k d) -> p k d", k=2)),
                         [B_PS[k]], [B_VA])
                    k2 = nps()
                    P.op("pe", lambda e, k2=k2, i=i: e.transpose(PS[k2][:, 0:128], KTOK[:, i, :], ident[:]), [B_KTOK, B_const], [B_PS[k2]])
                    P.op("act", lambda e, k2=k2, i=i: e.activation(KT[0:64, 0, 128 + i * 128:256 + i * 128], PS[k2][0:64, 0:128], AF.Copy), [B_PS[k2]], [B_KT])
                    P.op("act", lambda e, k2=k2, i=i: e.activation(KT[0:64, 1, 128 + i * 128:256 + i * 128], PS[k2][64:128, 0:128], AF.Copy), [B_PS[k2]], [B_KT])
            items.append((full(w_in_v, 1024), kv_fn))

            def uv_block(slot, blk):
                def fn(b):
                    for i in range(ng):
                        k = nps()
                        for kc in range(16):
                            P.op("pe", lambda e, k=k, kc=kc, i=i, b=b: e.matmul(PS[k][:, 0:256], XNT[:, kc, i * 128:(i + 1) * 128], WB[b][:, kc, :],
                                                                               start=(kc == 0), stop=(kc == 15)), B_WB[b] + B_R2, [B_PS[k]])
                        P.op("act", lambda e, k=k, i=i: e.activation(R3[:, slot, i * 1024 + blk * 256: i * 1024 + (blk + 1) * 256], PS[k][:, 0:256],
                                                                     AF.Gelu_apprx_tanh), [B_PS[k]], B_R3[slot])
                return fn
            for blk in range(4):
                items.append((full(w_in_v, 1280 + blk * 256), uv_block(0, blk)))
            for blk in range(4):
                items.append((full(w_in_v, 2304 + blk * 256), uv_block(1, blk)))

            def mixers(_):
                for i in range(ng):
                    VNi = R3[:, 1, i * 1024:(i + 1) * 1024]
                    P.op("act", lambda e, i=i, VNi=VNi: e.activation(R4f[:, 0:1024], VNi, AF.Square, accum_out=scol(16 + i)),
                         B_R3[1], [B_R4[0], B_small])
                    rstd_from(16 + i, 24 + i, 1024)
                    P.op("dve", lambda e, i=i, VNi=VNi: e.scalar_tensor_tensor(out=VNi, in0=VNi, scalar=scol(24 + i), in1=sgug[:],
                                                                              op0=ALU.mult, op1=ALU.mult), B_R3[1] + [B_small, B_const], B_R3[1])
                for i, gt in enumerate(tiles):
                    has_prev = sample or gt != 0
                    ncur = 16 if sample else 128
                    pso = [5, 6, 7]
                    kbs = ([0] if has_prev else []) + [1]

                    def st_S(hp):
                        k = nps(0, 5)
                        for hh in range(2):
                            h = 2 * hp + hh
                            kv = h // 8
                            for kb in kbs:
                                nk = 128 if kb == 0 else ncur
                                c0 = i * 128 + kb * 128
                                P.op("pe", lambda e, k=k, hh=hh, kb=kb, nk=nk, c0=c0, kv=kv, h=h, i=i: e.matmul(
                                    PS[k][0:nk, (hh * 2 + kb) * 128:(hh * 2 + kb + 1) * 128], KT[0:64, kv, c0:c0 + nk],
                                    QT[0:64, h, i * 128:(i + 1) * 128], start=True, stop=True), [B_KT] + B_R1, [B_PS[k]])
                        return k

                    def st_chain(hp, k):
                        s = hp % 2
                        for kb in kbs:
                            nk = 128 if kb == 0 else ncur
                            P.op("dve", lambda e, k=k, kb=kb, nk=nk, hp=hp, s=s: e.tensor_tensor(
                                PT[s][0:nk, :, kb, :], PS[k][0:nk, :].rearrange("p (a b q) -> p a b q", a=2, b=2)[:, :, kb, :],
                                biasT[0:nk, 2 * hp:2 * hp + 2, kb, :], ALU.add), [B_PS[k], B_bias], [B_PT[s][kb]])
                        for kb in kbs:
                            nk = 128 if kb == 0 else ncur
                            P.op("act", lambda e, kb=kb, nk=nk, s=s: e.activation(PT[s][0:nk, :, kb, :], PT[s][0:nk, :, kb, :], AF.Exp),
                                 [B_PT[s][kb]], [B_PT[s][kb]])
                        if not sample:
                            P.op("dve", lambda e, s=s: e.memset(PT[s][64:128, :, 1, 0:64], 0.0), [], [B_PT[s][1]])
                            if has_prev:
                                P.op("dve", lambda e, s=s: e.memset(PT[s][0:64, :, 0, 64:128], 0.0), [], [B_PT[s][0]])

                    def st_PV(hp):
                        s = hp % 2
                        for hh in range(2):
                            h = 2 * hp + hh
                            kv = h // 8
                            bk = pso[h // 6]
                            hl = h % 6
                            for n_, kb in enumerate(kbs):
                                nk = 128 if kb == 0 else ncur
                                slot = i + kb
                                P.op("pe", lambda e, bk=bk, hl=hl, s=s, hh=hh, kb=kb, nk=nk, slot=slot, kv=kv, n_=n_, nkb=len(kbs): e.matmul(
                                    PS[bk][:, hl * 65:(hl + 1) * 65], PT[s][0:nk, hh, kb, :], VA[0:nk, slot, kv, :],
                                    start=(n_ == 0), stop=(n_ == nkb - 1)), [B_PT[s][kb], B_VA], [B_PS[bk]])
                    kcur = st_S(0)
                    for hp in range(8):
                        knext = st_S(hp + 1) if hp + 1 < 8 else None
                        st_chain(hp, kcur)
                        st_PV(hp)
                        kcur = knext
                    for b3 in range(3):
                        nh = 6 if b3 < 2 else 4
                        h0 = b3 * 6
                        P.op("dve", lambda e, b3=b3, nh=nh, h0=h0: e.tensor_tensor(
                            DEN[:, h0:h0 + nh], PS[pso[b3]][:, 0:nh * 65].rearrange("p (h c) -> p h c", c=65)[:, :, 64],
                            esink[:, h0:h0 + nh], ALU.add), [B_PS[pso[b3]], B_const], [B_DEN])
                    P.op("dve", lambda e: e.reciprocal(RDEN[:], DEN[:]), [B_DEN], [B_DEN])
                    for h in range(16):
                        bk = pso[h // 6]
                        hl = h % 6
                        P.op("dve", lambda e, bk=bk, hl=hl, h=h, i=i: e.tensor_scalar(
                            R3[:, 2, i * 1024 + h * 64: i * 1024 + (h + 1) * 64], PS[bk][:, hl * 65:hl * 65 + 64], RDEN[:, h:h + 1], None, ALU.mult),
                            [B_PS[bk], B_DEN], B_R3[2])
                for i in range(ng):
                    ks = [nps(), nps()]
                    for g in range(8):
                        k = ks[g // 4]
                        P.op("pe", lambda e, k=k, g=g, i=i: e.matmul(PS[k][:, (g % 4) * 128:(g % 4 + 1) * 128], wmT[:, g, :],
                                                                    R3[:, 1, i * 1024 + g * 128: i * 1024 + (g + 1) * 128], start=True, stop=True),
                             [B_const] + B_R3[1], [B_PS[k]])
                    for g in range(8):
                        k = ks[g // 4]
                        Ug = R3[:, 0, i * 1024 + g * 128: i * 1024 + (g + 1) * 128]
                        P.op("dve", lambda e, k=k, g=g, Ug=Ug: e.scalar_tensor_tensor(out=Ug, in0=PS[k][:, (g % 4) * 128:(g % 4 + 1) * 128],
                                                                                    scalar=bsT[:, g:g + 1], in1=Ug, op0=ALU.add, op1=ALU.mult),
                             [B_PS[k], B_const] + B_R3[0], B_R3[0])
                if sample:
                    P.dma("sp", lambda e: e.dma_start(out=nk_s, in_=KTOK[0:16, 0, :]), B_KTOK, reads=[B_KTOK], is_out=True)
                    P.dma("sp", lambda e: e.dma_start(out=nv_s.rearrange("p (k d) -> p k d", k=2), in_=VA[0:16, 1, :, 0:64]), B_VA, reads=[B_VA], is_out=True)
                    P.dma("sp", lambda e: e.dma_start(out=nsgu, in_=R3[0:16, 1, 0:1024]), B_R3[1][0], reads=B_R3[1], is_out=True)
                elif tiles[-1] == nt - 1:
                    il = ng - 1
                    P.dma("sp", lambda e: e.dma_start(out=nk_p, in_=KTOK[:, il, :]), B_KTOK, reads=[B_KTOK], is_out=True)
                    P.dma("sp", lambda e: e.dma_start(out=nv_p.rearrange("p (k d) -> p k d", k=2), in_=VA[:, 1 + il, :, 0:64]), B_VA, reads=[B_VA], is_out=True)
                for i in range(ng):
                    for src_slot, dst0, dB in ((2, 0, B_R1[0]), (0, 8, B_R1[1])):
                        for f4 in range(2):
                            k = nps()
                            for j in range(4):
                                f = f4 * 4 + j
                                P.op("pe", lambda e, k=k, j=j, f=f, i=i, src_slot=src_slot: e.transpose(
                                    PS[k][:, j * 128:(j + 1) * 128], R3[:, src_slot, i * 1024 + f * 128: i * 1024 + (f + 1) * 128], ident[:]),
                                    B_R3[src_slot] + [B_const], [B_PS[k]])
                            P.op("act", lambda e, k=k, f4=f4, i=i, dst0=dst0: e.activation(
                                R1[:, dst0 + f4 * 4: dst0 + (f4 + 1) * 4, i * 128:(i + 1) * 128], PS[k][:, :].rearrange("p (a t) -> p a t", a=4), AF.Copy),
                                [B_PS[k]], [dB])
            items.append((None, mixers))

            SGA = XG[:, 0:1024]
            SGB = XG[:, 1024:2048]

            def gate_block(which, nb):
                def fn(b):
                    dst = SGA if which == 0 else SGB
                    for i in range(ng):
                        k = nps()
                        for kc in range(16):
                            P.op("pe", lambda e, k=k, kc=kc, i=i, b=b: e.matmul(PS[k][:, 0:256], XNT[:, kc, i * 128:(i + 1) * 128], WB[b][:, kc, :],
                                                                               start=(kc == 0), stop=(kc == 15)), B_WB[b] + B_R2, [B_PS[k]])
                        P.op("act", lambda e, k=k, i=i, dst=dst: e.activation(dst[:, i * 256:(i + 1) * 256], PS[k][:, 0:256], AF.Sigmoid),
                             [B_PS[k]], [B_XG])
                return fn

            def papb_block(nb):
                def fn(b):
                    for i in range(ng):
                        ka = nps()
                        kb_ = nps()
                        for kc in range(8):
                            P.op("pe", lambda e, ka=ka, kc=kc, i=i, b=b: e.matmul(PS[ka][:, 0:256], R1[:, kc, i * 128:(i + 1) * 128], WB[b][:, kc, :],
                                                                                 start=(kc == 0), stop=(kc == 7)), B_WB[b] + B_R1, [B_PS[ka]])
                        for kc in range(8):
                            P.op("pe", lambda e, kb_=kb_, kc=kc, i=i, b=b: e.matmul(PS[kb_][:, 0:256], R1[:, 8 + kc, i * 128:(i + 1) * 128], WB[b][:, 8 + kc, :],
                                                                                   start=(kc == 0), stop=(kc == 7)), B_WB[b] + B_R1, [B_PS[kb_]])
                        sa = SGA[:, i * 256:(i + 1) * 256]
                        sb_ = SGB[:, i * 256:(i + 1) * 256]
                        P.op("dve", lambda e, ka=ka, sa=sa: e.tensor_tensor(sa, sa, PS[ka][:, 0:256], ALU.mult), [B_PS[ka], B_XG], [B_XG])
                        P.op("dve", lambda e, kb_=kb_, sb_=sb_: e.tensor_tensor(sb_, sb_, PS[kb_][:, 0:256], ALU.mult), [B_PS[kb_], B_XG], [B_XG])
                        P.op("dve", lambda e, sa=sa, sb_=sb_: e.tensor_tensor(sa, sa, sb_, ALU.add), [B_XG], [B_XG])
                        k = nps()
                        for j in range(2):
                            P.op("pe", lambda e, k=k, j=j, sa=sa: e.transpose(PS[k][:, j * 128:(j + 1) * 128], sa[:, j * 128:(j + 1) * 128], ident[:]),
                                 [B_XG, B_const], [B_PS[k]])
                        HTv = R4
                        P.op("dve", lambda e, k=k, i=i, HTv=HTv: e.tensor_copy(HTv[:, nb * 2:nb * 2 + 2, i * 128:(i + 1) * 128],
                                                                              PS[k][:, 0:256].rearrange("p (a t) -> p a t", a=2)),
                             [B_PS[k]], B_R4)
                return fn
            for nb in range(8):
                items.append((full(w_in_v, 3328 + nb * 256), gate_block(0, nb)))
                items.append((full(w_in_v, 5376 + nb * 256), gate_block(1, nb)))
                items.append(([(lambda b: WB[b][:, 0:8, :], w_pa_v[:, :, nb * 256:(nb + 1) * 256]),
                               (lambda b: WB[b][:, 8:16, :], w_pb_v[:, :, nb * 256:(nb + 1) * 256])], papb_block(nb)))

            def wout_block(nb):
                def fn(b):
                    HTv = R4
                    for i in range(ng):
                        k = nps()
                        for kc in range(16):
                            P.op("pe", lambda e, k=k, kc=kc, i=i, b=b: e.matmul(PS[k][:, 0:256], HTv[:, kc, i * 128:(i + 1) * 128], WB[b][:, kc, :],
                                                                               start=(kc == 0), stop=(kc == 15)), B_WB[b] + B_R4, [B_PS[k]])
                        xs_ = X[:, i, nb * 256:(nb + 1) * 256]
                        P.op("dve", lambda e, k=k, xs_=xs_: e.tensor_tensor(xs_, xs_, PS[k][:, 0:256], ALU.add), [B_PS[k], B_X[i]], [B_X[i]])
                return fn
            for nb in range(8):
                items.append((full(w_out_v, nb * 256), wout_block(nb)))

            def peer_norm(_):
                for i in range(ng):
                    norm_T(i, gffnT, XNTb, B_R2, 32 + i)
            items.append((None, peer_norm))

            def wq_block(blk):
                def fn(b):
                    for j in range(2):
                        c = blk * 2 + j
                        k = nps()
                        for kc in range(16):
                            P.op("pe", lambda e, k=k, kc=kc, j=j, b=b: e.matmul(PS[k][:, 0:T], WB[b][:, kc, j * 128:(j + 1) * 128], XNTb[:, kc, 0:T],
                                                                               start=(kc == 0), stop=(kc == 15)), B_WB[b] + B_R2, [B_PS[k]])
                        P.op("act", lambda e, k=k, c=c: e.activation(QPT[:, c, 0:T], PS[k][:, 0:T], AF.Copy), [B_PS[k]], [B_QPT])
                return fn
            for blk in range(8):
                items.append((full(w_q_v, blk * 256), wq_block(blk)))

            if "it=" in dbg:
                items = items[:int(dbg.split("it=")[1].split(",")[0])]
            run_items(items)

            for i, gt in enumerate(tiles):
                if "nopeer" in dbg:
                    break
                SC = R3[:, 0, :].rearrange("p (a n) -> p a n", a=16)
                TMP = R3[:, 1, :].rearrange("p (a n) -> p a n", a=16)
                CAND = R3[:, 2, :]
                sks = [nps(0, 4) for _ in range(4)]
                for hp in range(16):
                    k = sks[hp // 4]
                    P.op("pe", lambda e, k=k, hp=hp, i=i: e.matmul(PS[k][:, (hp % 4) * 128:(hp % 4 + 1) * 128], QPT[:, hp, i * 128:(i + 1) * 128],
                                                                  skT[:, hp, :], start=True, stop=True), [B_QPT, B_const], [B_PS[k]])
                for q4 in range(4):
                    P.op("act", lambda e, q4=q4: e.activation(R3[:, 0, q4 * 512:(q4 + 1) * 512], PS[sks[q4]][:, :], AF.Copy), [B_PS[sks[q4]]], B_R3[0])
                for hp in range(16):
                    P.op("dve", lambda e, hp=hp: e.max(out=SV[:, hp, 0:8], in_=SC[:, hp, :]), B_R3[0], [B_SV])
                for hp in range(16):
                    P.op("dve", lambda e, hp=hp: e.match_replace(out=TMP[:, hp, :], in_to_replace=SV[:, hp, 0:8], in_values=SC[:, hp, :], imm_value=-1e30),
                         B_R3[0] + [B_SV], B_R3[1])
                for hp in range(16):
                    P.op("dve", lambda e, hp=hp: e.max(out=SV[:, hp, 8:16], in_=TMP[:, hp, :]), B_R3[1], [B_SV])
                for hp in range(16):
                    for o in (0, 8):
                        P.op("dve", lambda e, hp=hp, o=o: e.max_index(out=SI[:, hp, o:o + 8], in_max=SV[:, hp, o:o + 8], in_values=SC[:, hp, :]),
                             B_R3[0] + [B_SV], [B_SI])
                P.op("dve", lambda e: e.tensor_copy(SIF[:], SI[:]), [B_SI], [B_SIF])
                sv4 = SV[:, :, :].rearrange("p (h two) k -> p h two k", two=2)
                sif4 = SIF[:, :, :].rearrange("p (h two) k -> p h two k", two=2)
                CAND4 = CAND.rearrange("p (h a b) -> p h a b", h=8, a=16)
                P.op("dve", lambda e: e.tensor_tensor(CAND4, sv4[:, :, 0, :].unsqueeze(3).broadcast_to([128, 8, 16, 16]),
                                                      sv4[:, :, 1, :].unsqueeze(2).broadcast_to([128, 8, 16, 16]), ALU.add), [B_SV], B_R3[2])
                CAND2 = CAND.rearrange("p (h m) -> p h m", h=8)
                TMPC = R3[:, 1, :].rearrange("p (h m) -> p h m", h=8)
                for h in range(8):
                    P.op("dve", lambda e, h=h: e.max(out=CV[:, h, 0:8], in_=CAND2[:, h, :]), B_R3[2], [B_CV])
                for h in range(8):
                    P.op("dve", lambda e, h=h: e.match_replace(out=TMPC[:, h, :], in_to_replace=CV[:, h, 0:8], in_values=CAND2[:, h, :], imm_value=-1e30),
                         B_R3[2] + [B_CV], B_R3[1])
                for h in range(8):
                    P.op("dve", lambda e, h=h: e.max(out=CV[:, h, 8:16], in_=TMPC[:, h, :]), B_R3[1], [B_CV])
                for h in range(8):
                    for o in (0, 8):
                        P.op("dve", lambda e, h=h, o=o: e.max_index(out=CI[:, h, o:o + 8], in_max=CV[:, h, o:o + 8], in_values=CAND2[:, h, :]),
                             B_R3[2] + [B_CV], [B_CI])
                CIf = CI[:, :, :].rearrange("p h k -> p (h k)")
                P.op("dve", lambda e: e.tensor_scalar(IK[:], CIf, shc[:, 0:1], None, ALU.logical_shift_right), [B_CI, B_const], [B_IK])
                P.op("dve", lambda e: e.tensor_scalar(JK[:], CIf, shc[:, 1:2], None, ALU.bitwise_and), [B_CI, B_const], [B_IK])
                P.op("dve", lambda e: e.tensor_copy(IKF[:, :, :].rearrange("p h k -> p (h k)"), IK[:]), [B_IK], [B_IKF])
                P.op("dve", lambda e: e.tensor_copy(JKF[:, :, :].rearrange("p h k -> p (h k)"), JK[:]), [B_IK], [B_IKF])
                io4 = iota16[:, :].unsqueeze(1).unsqueeze(1).broadcast_to([128, 8, 16, 16])
                for w_, (KF, SEL) in enumerate(((IKF, SEL0), (JKF, SEL1))):
                    E4w = R2f[:, w_ * 2048:(w_ + 1) * 2048].rearrange("p (h a b) -> p h a b", h=8, a=16)
                    E4r = E4w
                    P.op("dve", lambda e, KF=KF, E4w=E4w: e.tensor_tensor(E4w, KF[:, :, :].unsqueeze(3).broadcast_to([128, 8, 16, 16]), io4, ALU.is_equal),
                         [B_IKF, B_const], [B_R2[w_]])
                    P.op("dve", lambda e, E4w=E4w, E4r=E4r, w_=w_: e.tensor_tensor(E4w, E4r, sif4[:, :, w_, :].unsqueeze(2).broadcast_to([128, 8, 16, 16]), ALU.mult),
                         [B_R2[w_], B_SIF], [B_R2[w_]])
                    P.op("dve", lambda e, SEL=SEL, E4r=E4r: e.tensor_reduce(SEL[:], E4r, AX.X, ALU.add), [B_R2[w_]], [B_SEL])
                P.op("dve", lambda e: e.scalar_tensor_tensor(out=EIF[:], in0=SEL0[:, :, :].rearrange("p h k -> p (h k)"), scalar=128.0,
                                                             in1=SEL1[:, :, :].rearrange("p h k -> p (h k)"), op0=ALU.mult, op1=ALU.add), [B_SEL], [B_SEL])
                P.op("dve", lambda e: e.tensor_copy(EIDX[:], EIF[:]), [B_SEL], [B_EIDX])
                P.op("dve", lambda e: e.tensor_tensor(EW[:], CV[:], CV[:, :, 0:1].broadcast_to([128, 8, 16]), ALU.subtract), [B_CV], [B_GW])
                P.op("act", lambda e: e.activation(EW[:], EW[:], AF.Exp), [B_GW], [B_GW])
                P.op("dve", lambda e: e.tensor_reduce(SUMW[:], EW[:], AX.X, ALU.add), [B_GW], [B_GW])
                P.op("dve", lambda e: e.reciprocal(RW[:], SUMW[:]), [B_GW], [B_GW])
                P.op("dve", lambda e: e.tensor_tensor(GW[:], EW[:], RW[:, :].unsqueeze(2).broadcast_to([128, 8, 16]), ALU.mult), [B_GW], [B_GW])
                GWf = GW[:, :, :].rearrange("p h k -> p (h k)")
                P.op("dve", lambda e, i=i: e.scalar_tensor_tensor(out=XG[:], in0=X[:, i, :], scalar=scol(40 + i), in1=gffn[:], op0=ALU.mult, op1=ALU.mult),
                     [B_X[i], B_small, B_const], [B_XG])
                acc = [4, 5, 6, 7]
                NB = 9
                JUNK = R4f[0:nr, 0:2048]

                def UV(s):
                    if s < 3:
                        return R3[:, s, :].bitcast(BF16)[0:nr, :]
                    if s < 5:
                        return VEB[:, 2 * (s - 3):2 * (s - 3) + 2, :].rearrange("p a b -> p (a b)")[0:nr, :]
                    if s < 7:
                        return R2f[:, (s - 5) * 2048:(s - 4) * 2048].bitcast(BF16)[0:nr, :]
                    return WBraw[s - 7][:, :].bitcast(BF16)[0:nr, :]

                def BUV(s):
                    if s < 3:
                        return B_R3[s]
                    if s < 5:
                        return [B_VE[2 * (s - 3)], B_VE[2 * (s - 3) + 1]]
                    if s < 7:
                        return [B_R2[s - 5]]
                    return B_WB[s - 7]
                LA = NB - 2

                def gather(cg):
                    sg_ = cg % NB
                    P.dma("pool", lambda e, cg=cg, sg_=sg_: e.indirect_dma_start(out=UV(sg_), out_offset=None, in_=euvb,
                                                                                 in_offset=bass.IndirectOffsetOnAxis(ap=EIDX[0:nr, cg:cg + 1], axis=0)),
                          BUV(sg_)[0], reads=[B_EIDX, B_TUV], writes=BUV(sg_))
                if "nogather" not in dbg:
                    for cg in range(LA):
                        gather(cg)
                for c in range(129 if "nogather" not in dbg else 0):
                    if c + LA < 128:
                        gather(c + LA)
                    if c < 128:
                        sb_ = c % NB
                        p4 = c % 4
                        P.op("dve", lambda e, c=c, sb_=sb_: e.scalar_tensor_tensor(out=JUNK, in0=UV(sb_)[:, 0:2048], scalar=1.0, in1=XG[0:nr, :],
                                                                                   op0=ALU.mult, op1=ALU.mult, accum_out=AA[0:nr, c:c + 1]),
                             BUV(sb_) + [B_XG], [B_R4[0], B_AA[p4]])
                        P.op("act", lambda e, c=c: e.activation(AGL[0:nr, c:c + 1], AA[0:nr, c:c + 1], AF.Gelu_apprx_tanh), [B_AA[p4]], [B_AGL[p4]])
                    if c >= 1:
                        c1 = c - 1
                        sb_ = c1 % NB
                        p4 = c1 % 4
                        P.op("act", lambda e, c1=c1: e.activation(HW[0:nr, c1:c1 + 1], AGL[0:nr, c1:c1 + 1], AF.Copy, scale=GWf[0:nr, c1:c1 + 1]),
                             [B_AGL[p4], B_GW], [B_HW[p4]])
                        P.op("act", lambda e, c1=c1, p4=p4: e.activation(DIAG[p4][0:nr, :], ident[0:nr, :], AF.Copy, scale=HW[0:nr, c1:c1 + 1]),
                             [B_HW[p4], B_const], [B_DIAG[p4]])
                        for j in range(4):
                            P.op("pe", lambda e, j=j, sb_=sb_, p4=p4, c1=c1: e.matmul(PS[acc[j]][:, :], DIAG[p4][0:nr, :], UV(sb_)[:, 2048 + j * 512:2048 + (j + 1) * 512],
                                                                                  start=(c1 == 0), stop=(c1 == 127)), [B_DIAG[p4]] + BUV(sb_), [B_PS[acc[j]]])
                for j in range(4):
                    xs_ = X[0:nr, i, j * 512:(j + 1) * 512]
                    P.op("dve", lambda e, j=j, xs_=xs_: e.tensor_tensor(xs_, xs_, PS[acc[j]][0:nr, :], ALU.add), [B_PS[acc[j]], B_X[i]], [B_X[i]])
                P.op("act", lambda e, i=i: e.activation(R4f[0:nr, 2048:4096], X[0:nr, i, :], AF.Square, accum_out=small[0:nr, 48 + i:49 + i]),
                     [B_X[i]], [B_R4[1], B_small])
                rstd_from(48 + i, 56 + i, D, rows=nr)
                P.op("dve", lambda e, i=i: e.scalar_tensor_tensor(out=X[0:nr, i, :], in0=X[0:nr, i, :], scalar=small[0:nr, 56 + i:57 + i], in1=gfin[0:nr, :],
                                                                  op0=ALU.mult, op1=ALU.mult), [B_X[i], B_small, B_const], [B_X[i]])
                if sample:
                    P.dma("sp", lambda e, i=i: e.dma_start(out=y_s, in_=X[0:16, i, :]), B_X[i], reads=[B_X[i]], is_out=True)
                else:
                    P.dma("sp", lambda e, i=i, gt=gt: e.dma_start(out=y_p[gt * 128:(gt + 1) * 128, :], in_=X[:, i, :]), B_X[i], reads=[B_X[i]], is_out=True)

        for g0 in range(0, nt, G):
            do_group(list(range(g0, g0 + G)), False)
            first_group[0] = False
        if "nosample" not in dbg:
            do_group([0], True)
        P.finish()
        P.emit()
    return nc


def _t5_bucket_np(rel):
    try:
        import jax
        import jax.numpy as jnp
        with jax.default_device(jax.devices("cpu")[0]):
            r = jnp.asarray(rel, dtype=jnp.int32)
            half = 16
            max_exact = 8
            ret = jnp.where(r > 0, half, 0)
            n = jnp.abs(r)
            nf = jnp.maximum(n, 1).astype(jnp.float32)
            large = max_exact + (jnp.log(nf / max_exact) / math.log(128 / max_exact) * (half - max_exact)).astype(jnp.int32)
            large = jnp.minimum(large, half - 1)
            return np.asarray(ret + jnp.where(n < max_exact, n, large))
    except Exception:
        r = np.asarray(rel, dtype=np.int32)
        ret = np.where(r > 0, 16, 0)
        n = np.abs(r)
        nf = np.maximum(n, 1).astype(np.float32)
        large = 8 + (np.log(nf / np.float32(8)) / np.float32(math.log(16.0)) * np.float32(8)).astype(np.int32)
        large = np.minimum(large, 15)
        return ret + np.where(n < 8, n, large)


_NC_CACHE = {}


def kernel(x_prompt, x_sample, cache_k_swa, cache_v_swa, norm_mix_g, w_in, sgu_norm_g, sgu_w_s, sgu_b_s,
           attn_sinks, rel_bias_table, w_branch_attn, w_branch_sgu, w_out, norm_ffn_g, peer_w_query,
           peer_sub_keys, peer_expert_u, peer_expert_v, norm_final_g):
    f = lambda a: np.ascontiguousarray(np.asarray(a), dtype=np.float32)
    if "nc" not in _NC_CACHE:
        _NC_CACHE["nc"] = build_program()
    nc = _NC_CACHE["nc"]
    kb = np.arange(2)[:, None, None]; kk = np.arange(128)[None, :, None]; qq = np.arange(128)[None, None, :]
    rel = (kb - 1) * 128 + kk - qq
    bkt = _t5_bucket_np(rel).reshape(-1)
    oh = np.zeros((32, 32768), np.float32)
    oh[bkt, np.arange(32768)] = 1.0
    s_i = np.arange(128)[:, None, None]; t_i = np.arange(128)[None, None, :]
    trilT = np.ascontiguousarray(np.broadcast_to((t_i >= s_i), (128, 8, 128))).astype(np.float32)
    shared = dict(
        w_in=f(w_in[0]), w_pa=f(w_branch_attn[0]), w_pb=f(w_branch_sgu[0]), w_out=f(w_out[0]), w_q=f(peer_w_query[0]),
        eu=f(peer_expert_u[0]), ev=f(peer_expert_v[0]),
        gmixT=f(np.asarray(norm_mix_g[0]).reshape(16, 128).T), gffnT=f(np.asarray(norm_ffn_g[0]).reshape(16, 128).T),
        gffn_bc=f(np.broadcast_to(np.asarray(norm_ffn_g[0])[None, :], (128, D))),
        gfin_bc=f(np.broadcast_to(np.asarray(norm_final_g)[None, :], (128, D))),
        sgug_bc=f(np.broadcast_to(np.asarray(sgu_norm_g[0])[None, :], (128, 1024))),
        wsT=f(np.asarray(sgu_w_s[0]).transpose(2, 0, 1)), trilT=trilT, bsT=f(np.asarray(sgu_b_s[0]).T),
        sinks_bc=f(np.broadcast_to(np.asarray(attn_sinks[0])[None, :], (128, 16))), table=f(rel_bias_table), oh=oh,
        skT=f(np.asarray(peer_sub_keys[0]).reshape(16, 128, 128).transpose(2, 0, 1)),
        ident=np.eye(128, dtype=np.float32), iota16=f(np.broadcast_to(np.arange(16, dtype=np.float32)[None, :], (128, 16))),
        shc=np.ascontiguousarray(np.broadcast_to(np.array([[4, 15]], np.uint32), (128, 2))),
    )
    xpn = np.asarray(x_prompt); xsn = np.asarray(x_sample); ckn = np.asarray(cache_k_swa); cvn = np.asarray(cache_v_swa)
    in_maps = []
    for c in range(8):
        m = dict(shared)
        m["xp"] = f(xpn[c]); m["xs"] = f(xsn[c])
        m["ck"] = f(ckn[0, c].reshape(128, 128)); m["cv"] = f(cvn[0, c].reshape(128, 128))
        in_maps.append(m)
    res = run_bass_kernel_spmd(nc, in_maps, core_ids=list(range(8)))
    r = res.results
    y_prompt = np.stack([r[c]["y_p"] for c in range(8)]).astype(np.float32)
    y_sample = np.stack([r[c]["y_s"] for c in range(8)]).astype(np.float32)
    nk_p = np.stack([r[c]["nk_p"].reshape(128, 2, 64) for c in range(8)])[None].astype(np.float32)
    nv_p = np.stack([r[c]["nv_p"].reshape(128, 2, 64) for c in range(8)])[None].astype(np.float32)
    nk_s = np.stack([r[c]["nk_s"].reshape(16, 2, 64) for c in range(8)])[None].astype(np.float32)
    nv_s = np.stack([r[c]["nv_s"].reshape(16, 2, 64) for c in range(8)])[None].astype(np.float32)
    nsg = np.stack([r[c]["nsgu"] for c in range(8)])[None].astype(np.float32)
    return (y_prompt, y_sample, nk_p, nv_p, nk_s, nv_s, nsg)
```

```python
import math
import numpy as np
from contextlib import ExitStack
import concourse.bass as bass
import concourse.mybir as mybir
from concourse.bass_utils import run_bass_kernel_spmd

F32 = mybir.dt.float32
F32R = mybir.dt.float32r
BF16 = mybir.dt.bfloat16
I32 = mybir.dt.int32
U32 = mybir.dt.uint32
AF = mybir.ActivationFunctionType
ALU = mybir.AluOpType
AX = mybir.AxisListType

D = 2048
DIN = 7424
SEQ = 2048
NT = SEQ // 128
G = 2
TMAX = G * 128
EPS = 1e-6
WCOL = 256


class Buf:
    __slots__ = ("name", "lw", "rd", "dsem", "dcnt", "excl")

    def __init__(self, name, excl=False):
        self.name = name
        self.excl = excl
        self.lw = None
        self.rd = {}
        self.dsem = {}
        self.dcnt = {}


class Prog:
    ENG = ("sp", "act", "dve", "pool", "pe")

    def __init__(self, nc, es):
        self.nc = nc
        self.es = es
        self.st = {e: [] for e in self.ENG}
        self.sem = {}
        self.cnt = {}
        self.waited = {e: {} for e in self.ENG}
        self.nsem = 0
        self.out_toks = []
        for e in self.ENG:
            self._new_sem(e)

    def _mk(self, name):
        self.nsem += 1
        return self.es.enter_context(self.nc.semaphore(f"{name}{self.nsem}"))

    def _new_sem(self, e):
        self.sem[e] = self._mk("e" + e)
        self.cnt[e] = 0

    def _wait(self, e, tok):
        sem, val, src = tok
        if src == e and e == "pe":
            return
        w = self.waited[e]
        if w.get(id(sem), -1) >= val:
            return
        w[id(sem)] = val
        self.st[e].append(("w", sem, val))

    def _deps(self, e, reads, writes):
        need = {}
        def add(tok):
            k = id(tok[0])
            if k not in need or need[k][1] < tok[1]:
                need[k] = tok
        for b in reads:
            if b.lw is not None:
                add(b.lw)
            if b.excl:
                for t in b.rd.values():
                    if t[2] != e:
                        add(t)
        for b in writes:
            if b.lw is not None:
                add(b.lw)
            for t in b.rd.values():
                add(t)
        for tok in need.values():
            self._wait(e, tok)

    def _commit(self, tok, reads, writes):
        for b in reads:
            k = id(tok[0])
            if k not in b.rd or b.rd[k][1] < tok[1]:
                b.rd[k] = tok
        for b in writes:
            b.lw = tok
            b.rd = {}

    def op(self, e, fn, reads=(), writes=()):
        self._deps(e, reads, writes)
        if self.cnt[e] >= 30000:
            self._new_sem(e)
        self.cnt[e] += 1
        tok = (self.sem[e], self.cnt[e], e)
        self.st[e].append(("o", fn, self.sem[e], 1))
        self._commit(tok, reads, writes)
        return tok

    def dma(self, q, fn, dbuf, reads=(), writes=(), is_out=False):
        self._deps(q, reads, writes)
        if q not in dbuf.dsem or dbuf.dcnt[q] >= 48000:
            dbuf.dsem[q] = self._mk("d")
            dbuf.dcnt[q] = 0
        dbuf.dcnt[q] += 16
        tok = (dbuf.dsem[q], dbuf.dcnt[q], "dma")
        self.st[q].append(("o", fn, dbuf.dsem[q], 16))
        self._commit(tok, reads, writes)
        if is_out:
            self.out_toks.append(tok)
        return tok

    def finish(self):
        for tok in self.out_toks:
            self._wait("sp", tok)

    def emit(self):
        blk = self.es.enter_context(self.nc.Block())

        def run(e):
            def f(eng):
                for it in self.st[e]:
                    if it[0] == "w":
                        eng.wait_ge(it[1], it[2])
                    else:
                        it[1](eng).then_inc(it[2], it[3])
            return f
        blk.sync(run("sp"))
        blk.scalar(run("act"))
        blk.vector(run("dve"))
        blk.gpsimd(run("pool"))
        blk.tensor(run("pe"))


def build_program(nt=NT, dbg=""):
    SEQ = nt * 128
    nc = bass.Bass("TRN2", target_bir_lowering=False)

    def din(name, shape, dt=F32):
        return nc.dram_tensor(name, list(shape), dt, kind="ExternalInput").ap()

    def dout(name, shape, dt=F32):
        return nc.dram_tensor(name, list(shape), dt, kind="ExternalOutput").ap()

    xp = din("xp", [SEQ, D]); xs = din("xs", [16, D])
    ck = din("ck", [128, 128]); cvv = din("cv", [128, 128])
    w_in = din("w_in", [D, DIN]); w_pa = din("w_pa", [1024, D]); w_pb = din("w_pb", [1024, D])
    w_out = din("w_out", [D, D]); w_q = din("w_q", [D, D])
    eu = din("eu", [16384, D]); ev = din("ev", [16384, D])
    gmixT_d = din("gmixT", [128, 16]); gffnT_d = din("gffnT", [128, 16])
    gffn_d = din("gffn_bc", [128, D]); gfin_d = din("gfin_bc", [128, D]); sgug_d = din("sgug_bc", [128, 1024])
    wsT_d = din("wsT", [128, 8, 128]); trilT_d = din("trilT", [128, 8, 128]); bsT_d = din("bsT", [128, 8])
    sinks_d = din("sinks_bc", [128, 16]); table_d = din("table", [32, 16]); oh_d = din("oh", [32, 32768])
    skT_d = din("skT", [128, 16, 128]); ident_d = din("ident", [128, 128]); iota_d = din("iota16", [128, 16])
    shc_d = din("shc", [128, 2], U32)
    y_p = dout("y_p", [SEQ, D]); y_s = dout("y_s", [16, D])
    nk_p = dout("nk_p", [128, 128]); nv_p = dout("nv_p", [128, 128])
    nk_s = dout("nk_s", [16, 128]); nv_s = dout("nv_s", [16, 128]); nsgu = dout("nsgu", [16, 1024])
    bscr = nc.dram_tensor("bscr", [16, 32768], F32, kind="Internal").ap()
    euvb = nc.dram_tensor("euvb", [16384, 2 * D], BF16, kind="Internal").ap()
    wscr = nc.dram_tensor("wscr", [64, 128, 2048], F32, kind="Internal").ap()

    w_in_v = w_in.rearrange("(kc p) n -> p kc n", p=128)
    w_pa_v = w_pa.rearrange("(kc p) n -> p kc n", p=128)
    w_pb_v = w_pb.rearrange("(kc p) n -> p kc n", p=128)
    w_out_v = w_out.rearrange("(kc p) n -> p kc n", p=128)
    w_q_v = w_q.rearrange("(kc p) n -> p kc n", p=128)

    with ExitStack() as es:
        P = Prog(nc, es)

        def sb(name, shape, dt=F32):
            return es.enter_context(nc.sbuf_tensor("s_" + name, list(shape), dt))

        biasT = sb("biasT", [128, 16, 2, 128]); B_bias = Buf("biasT")
        gffn = sb("gffn", [128, D]); gfin = sb("gfin", [128, D]); sgug = sb("sgug", [128, 1024])
        skT = sb("skT", [128, 16, 128]); wmT = sb("wmT", [128, 8, 128]); trilT = sb("trilT", [128, 8, 128])
        ident = sb("ident", [128, 128]); gmixT = sb("gmixT", [128, 16]); gffnT = sb("gffnT", [128, 16])
        bsT = sb("bsT", [128, 8]); esink = sb("esink", [128, 16]); iota16 = sb("iota16", [128, 16])
        shc = sb("shc", [128, 2], U32); tab = sb("tab", [32, 16])
        B_const = Buf("const")
        X = sb("X", [128, G, D]); B_X = [Buf(f"X{i}") for i in range(G)]
        R1 = sb("R1", [128, 16, TMAX], BF16); B_R1 = [Buf("R1a"), Buf("R1b")]
        QPT = sb("QPT", [128, 16, TMAX]); B_QPT = Buf("QPT")
        R2f = sb("R2", [128, 16 * TMAX]); B_R2 = [Buf("R2a"), Buf("R2b")]
        XNTb = R2f[:, :].bitcast(BF16)[:, 0:16 * TMAX].rearrange("p (k t) -> p k t", k=16)
        XN2T = R2f[:, :].rearrange("p (k t) -> p k t", k=16)
        R4 = sb("R4", [128, 16, TMAX], BF16); B_R4 = [Buf("R4a"), Buf("R4b")]
        R4f = R4[:, :, :].rearrange("p a b -> p (a b)")
        WBraw = [sb(f"WB{i}", [128, 2048]) for i in range(2)]
        WB = [w[:, :].bitcast(BF16).rearrange("p (k n) -> p k n", k=16) for w in WBraw]
        WBf32 = [w[:, :].rearrange("p (k n) -> p k n", k=16) for w in WBraw]
        B_WB = [[Buf(f"WB{i}")] for i in range(2)]
        VEB = sb("VEB", [128, 4, 2048], BF16); B_VE = [Buf(f"VE{i}") for i in range(4)]
        for h_ in range(2):
            vv = VEB[:, 2 * h_:2 * h_ + 2, :].rearrange("p a b -> p (a b)")
            WB.append(vv.rearrange("p (k n) -> p k n", k=16))
            WBf32.append(vv.bitcast(F32).rearrange("p (k n) -> p k n", k=16))
            WBraw.append(vv.bitcast(F32))
            B_WB.append([B_VE[2 * h_], B_VE[2 * h_ + 1]])
        NWB = 4
        R3 = sb("R3", [128, 3, 2048]); B_R3 = [[Buf(f"R3_{j}a"), Buf(f"R3_{j}b")] for j in range(3)]
        XG = sb("XG", [128, 2048]); B_XG = Buf("XG")
        KT = sb("KT", [64, 2, 128 + TMAX], BF16); B_KT = Buf("KT")
        VA = sb("VA", [128, 1 + G, 2, 65]); B_VA = Buf("VA")
        KTOK = sb("KTOK", [128, G, 128]); B_KTOK = Buf("KTOK")
        PT = [sb(f"PT{i}", [128, 2, 2, 128]) for i in range(2)]; B_PT = [[Buf(f"PT{i}a"), Buf(f"PT{i}b")] for i in range(2)]
        DIAG = [sb(f"DIAG{i}", [128, 128], BF16) for i in range(4)]; B_DIAG = [Buf(f"DG{i}") for i in range(4)]
        small = sb("small", [128, 64]); B_small = Buf("small")
        DEN = sb("DEN", [128, 16]); RDEN = sb("RDEN", [128, 16]); B_DEN = Buf("DEN")
        SV = sb("SV", [128, 16, 16]); SI = sb("SI", [128, 16, 16], U32); SIF = sb("SIF", [128, 16, 16])
        CV = sb("CV", [128, 8, 16]); CI = sb("CI", [128, 8, 16], U32)
        IK = sb("IK", [128, 128], U32); JK = sb("JK", [128, 128], U32)
        IKF = sb("IKF", [128, 8, 16]); JKF = sb("JKF", [128, 8, 16])
        SEL0 = sb("SEL0", [128, 8, 16]); SEL1 = sb("SEL1", [128, 8, 16])
        EIF = sb("EIF", [128, 128]); EIDX = sb("EIDX", [128, 128], I32)
        EW = sb("EW", [128, 8, 16]); GW = sb("GW", [128, 8, 16]); SUMW = sb("SUMW", [128, 8]); RW = sb("RW", [128, 8])
        AA = sb("AA", [128, 128]); AGL = sb("AGL", [128, 128]); HW = sb("HW", [128, 128])
        B_SV = Buf("SV"); B_SI = Buf("SI"); B_SIF = Buf("SIF"); B_CV = Buf("CV"); B_CI = Buf("CI")
        B_IK = Buf("IK"); B_IKF = Buf("IKF"); B_SEL = Buf("SEL"); B_EIDX = Buf("EIDX"); B_GW = Buf("GW")
        B_AA = [Buf(f"AA{i}") for i in range(4)]; B_AGL = [Buf(f"AGL{i}") for i in range(4)]; B_HW = [Buf(f"HW{i}") for i in range(4)]
        PS = [es.enter_context(nc.psum_tensor(f"PS{i}", [128, 512], F32)) for i in range(8)]
        B_PS = [Buf(f"PS{i}", excl=True) for i in range(8)]
        psrr = [0]

        def nps(lo=0, hi=8):
            k = lo + psrr[0] % (hi - lo)
            psrr[0] += 1
            return k

        def scol(k):
            return small[:, k:k + 1]

        def rstd_from(ss_col, out_col, n, rows=128):
            P.op("dve", lambda e: e.tensor_scalar(small[0:rows, out_col:out_col + 1], small[0:rows, ss_col:ss_col + 1],
                                                  1.0 / n, EPS, ALU.mult, ALU.add), [B_small], [B_small])
            P.op("act", lambda e: e.activation(small[0:rows, out_col:out_col + 1], small[0:rows, out_col:out_col + 1], AF.Sqrt),
                 [B_small], [B_small])
            P.op("dve", lambda e: e.reciprocal(small[0:rows, out_col:out_col + 1], small[0:rows, out_col:out_col + 1]),
                 [B_small], [B_small])

        def ld(dst, src, buf=B_const):
            P.dma("sp", lambda e: e.dma_start(out=dst, in_=src), buf, writes=[buf])
        ld(gffn[:], gffn_d); ld(gfin[:], gfin_d); ld(sgug[:], sgug_d); ld(skT[:], skT_d)
        ld(wmT[:], wsT_d); ld(trilT[:], trilT_d); ld(ident[:], ident_d); ld(gmixT[:], gmixT_d); ld(gffnT[:], gffnT_d)
        ld(bsT[:], bsT_d); ld(esink[:], sinks_d); ld(iota16[:], iota_d); ld(shc[:], shc_d); ld(tab[:], table_d)
        P.op("dve", lambda e: e.tensor_tensor(wmT[:], wmT[:], trilT[:], ALU.mult), [B_const], [B_const])
        P.op("act", lambda e: e.activation(esink[:], esink[:], AF.Exp), [B_const], [B_const])
        P.op("dve", lambda e: e.memset(VA[:, :, :, 64:65], 1.0), [], [B_VA])
        for pc in range(8 if "nobias" not in dbg else 0):
            P.dma("sp", lambda e, pc=pc: e.dma_start(out=R3[0:32, 0, :], in_=oh_d[:, pc * 4096:pc * 4096 + 2048]), B_R3[0][0], writes=B_R3[0])
            P.dma("sp", lambda e, pc=pc: e.dma_start(out=R3[0:32, 1, :], in_=oh_d[:, pc * 4096 + 2048:(pc + 1) * 4096]), B_R3[1][0], writes=B_R3[1])
            for hf in range(2):
                for q4 in range(4):
                    k = nps()
                    P.op("pe", lambda e, k=k, hf=hf, q4=q4: e.matmul(PS[k][0:16, :], tab[:, :], R3[0:32, hf, q4 * 512:(q4 + 1) * 512],
                                                                    start=True, stop=True), [B_const] + B_R3[hf], [B_PS[k]])
                    P.op("act", lambda e, k=k, q4=q4: e.activation(XG[0:16, q4 * 512:(q4 + 1) * 512], PS[k][0:16, :], AF.Copy),
                         [B_PS[k]], [B_XG])
                P.dma("sp", lambda e, pc=pc, hf=hf: e.dma_start(out=bscr[:, pc * 4096 + hf * 2048: pc * 4096 + (hf + 1) * 2048], in_=XG[0:16, :]),
                      B_XG, reads=[B_XG], writes=[B_bias])
        if "nobias" not in dbg:
          P.dma("sp", lambda e: e.dma_start(out=biasT[:], in_=bscr.rearrange("h (kb kk qq) -> kk h kb qq", kb=2, kk=128)),
              B_bias, reads=[B_bias], writes=[B_bias])

        B_TUV = Buf("tblUV")
        RC = 2
        conv_chunks = []
        for t_, src in enumerate((eu, ev)):
            srcv = src.rearrange("(p r) d -> p r d", p=128)
            dstv = euvb.rearrange("(p r) (t d) -> p r t d", p=128, t=2)[:, :, t_, :]
            for r0 in range(0, 128, RC):
                def chunk(srcv=srcv, dstv=dstv, r0=r0):
                    b = len(conv_done) % 2
                    conv_done.append(1)
                    stg = VEB[:, 2 * b:2 * b + 2, :]
                    sB = [B_VE[2 * b], B_VE[2 * b + 1]]
                    P.dma("pool", lambda e: e.dma_start(out=stg, in_=srcv[:, r0:r0 + RC, :]), sB[0], writes=sB)
                    P.dma("sp", lambda e: e.dma_start(out=dstv[:, r0:r0 + RC, :], in_=stg), B_TUV, reads=sB, writes=[B_TUV])
                conv_chunks.append(chunk)
        conv_done = []

        def emit_conv(n):
            for _ in range(n):
                if len(conv_done) < len(conv_chunks):
                    conv_chunks[len(conv_done)]()

        wcount = [0]

        B_WSCR = Buf("wscr")
        first_group = [True]

        def load_block(specs, blk):
            nwb = 2 if first_group[0] else NWB
            b = wcount[0] % nwb
            wcount[0] += 1
            if first_group[0]:
                for (dst_fn, src) in specs:
                    P.dma("pool", lambda e, dst_fn=dst_fn, src=src, b=b: e.dma_start(out=dst_fn(b), in_=src),
                          B_WB[b][0], writes=B_WB[b])
                P.dma("sp", lambda e, b=b, blk=blk: e.dma_start(out=wscr[blk], in_=WBraw[b][:, :]), B_WSCR, reads=B_WB[b], writes=[B_WSCR])
            else:
                P.dma("sp", lambda e, b=b, blk=blk: e.dma_start(out=WBraw[b][:, :], in_=wscr[blk]), B_WB[b][0], reads=[B_WSCR], writes=B_WB[b])
            return b

        def run_items(items):
            widx = [k for k, it in enumerate(items) if it[0] is not None]
            bufs = {}
            depth = (2 if first_group[0] else NWB) - 1
            for j0 in range(min(depth, len(widx))):
                bufs[widx[j0]] = load_block(items[widx[j0]][0], j0)
            for k, (w, fn) in enumerate(items):
                if w is not None:
                    j = widx.index(k)
                    if j + depth < len(widx):
                        bufs[widx[j + depth]] = load_block(items[widx[j + depth]][0], j + depth)
                    fn(bufs[k])
                    if first_group[0]:
                        emit_conv(3)
                else:
                    fn(None)
            if first_group[0]:
                emit_conv(len(conv_chunks))

        def full(src_v, c0, ncol=WCOL):
            return [(lambda b: WB[b][:, :, 0:ncol], src_v[:, :, c0:c0 + ncol])]

        def do_group(tiles, sample):
            ng = len(tiles)
            T = ng * 128
            nr = 16 if sample else 128
            XNT = XNTb
            QT = R1

            for i, gt in enumerate(tiles):
                if sample:
                    P.op("dve", lambda e, i=i: e.memset(X[:, i, :], 0.0), [], [B_X[i]])
                    P.dma("sp", lambda e, i=i: e.dma_start(out=X[0:16, i, :], in_=xs), B_X[i], writes=[B_X[i]])
                else:
                    P.dma("sp", lambda e, i=i, gt=gt: e.dma_start(out=X[:, i, :], in_=xp[gt * 128:(gt + 1) * 128, :]), B_X[i], writes=[B_X[i]])
            if sample:
                P.dma("sp", lambda e: e.dma_start(out=KTOK[:, 0, :], in_=ck), B_KTOK, writes=[B_KTOK])
                k = nps()
                P.op("pe", lambda e, k=k: e.transpose(PS[k][:, 0:128], KTOK[:, 0, :], ident[:]), [B_KTOK, B_const], [B_PS[k]])
                P.op("act", lambda e, k=k: e.activation(KT[0:64, 0, 0:128], PS[k][0:64, 0:128], AF.Copy), [B_PS[k]], [B_KT])
                P.op("act", lambda e, k=k: e.activation(KT[0:64, 1, 0:128], PS[k][64:128, 0:128], AF.Copy), [B_PS[k]], [B_KT])
                P.dma("sp", lambda e: e.dma_start(out=VA[:, 0, :, 0:64], in_=cvv.rearrange("p (k d) -> p k d", k=2)), B_VA, writes=[B_VA])
            elif tiles[0] != 0:
                P.op("act", lambda e: e.activation(KT[0:64, :, 0:128], KT[0:64, :, TMAX:TMAX + 128], AF.Copy), [B_KT], [B_KT])
                P.op("dve", lambda e: e.tensor_copy(VA[:, 0, :, 0:64], VA[:, G, :, 0:64]), [B_VA], [B_VA])

            def norm_T(i, gT, dstR, dstB, col):
                XR = R3[:, 2, :]
                P.op("act", lambda e: e.activation(XR, X[:, i, :], AF.Square, accum_out=scol(col)), [B_X[i]], B_R3[2] + [B_small])
                rstd_from(col, col + 8, D)
                P.op("dve", lambda e: e.tensor_scalar(XR, X[:, i, :], scol(col + 8), None, ALU.mult), [B_X[i], B_small], B_R3[2])
                for k4 in range(4):
                    k = nps()
                    for j in range(4):
                        kc = k4 * 4 + j
                        P.op("pe", lambda e, k=k, j=j, kc=kc: e.transpose(PS[k][:, j * 128:(j + 1) * 128], XR[:, kc * 128:(kc + 1) * 128], ident[:]),
                             B_R3[2] + [B_const], [B_PS[k]])
                    P.op("dve", lambda e, k=k, k4=k4: e.tensor_tensor(dstR[:, k4 * 4:(k4 + 1) * 4, i * 128:(i + 1) * 128],
                                                                      PS[k][:, :].rearrange("p (a t) -> p a t", a=4),
                                                                      gT[:, k4 * 4:(k4 + 1) * 4].unsqueeze(2).broadcast_to([128, 4, 128]), ALU.mult),
                         [B_PS[k], B_const], dstB)
            for i in range(ng):
                norm_T(i, gmixT, XNT, B_R2, i)

            items = []

            def q_block(blk):
                def fn(b):
                    for j in range(2):
                        pj = blk * 2 + j
                        k = nps()
                        for kc in range(16):
                            P.op("pe", lambda e, k=k, kc=kc, j=j, b=b: e.matmul(PS[k][:, 0:T], WB[b][:, kc, j * 128:(j + 1) * 128], XNT[:, kc, 0:T],
                                                                               start=(kc == 0), stop=(kc == 15)), B_WB[b] + B_R2, [B_PS[k]])
                        P.op("act", lambda e, k=k, pj=pj: e.activation(QT[0:64, 2 * pj, 0:T], PS[k][0:64, 0:T], AF.Copy, scale=0.125), [B_PS[k]], B_R1)
                        P.op("act", lambda e, k=k, pj=pj: e.activation(QT[0:64, 2 * pj + 1, 0:T], PS[k][64:128, 0:T], AF.Copy, scale=0.125), [B_PS[k]], B_R1)
                return fn
            for blk in range(4):
                items.append((full(w_in_v, blk * 256), q_block(blk)))

            def kv_fn(b):
                for i in range(ng):
                    k = nps()
                    for kc in range(16):
                        P.op("pe", lambda e, k=k, kc=kc, i=i, b=b: e.matmul(PS[k][:, 0:256], XNT[:, kc, i * 128:(i + 1) * 128], WB[b][:, kc, :],
                                                                           start=(kc == 0), stop=(kc == 15)), B_WB[b] + B_R2, [B_PS[k]])
                    P.op("act", lambda e, k=k, i=i: e.activation(KTOK[:, i, :], PS[k][:, 0:128], AF.Copy), [B_PS[k]], [B_KTOK])
                    P.op("dve", lambda e, k=k, i=i: e.tensor_copy(VA[:, 1 + i, :, 0:64], PS[k][:, 128:256].rearrange("p (k d) -> p k d", k=2)),
                         [B_PS[k]], [B_VA])
                    k2 = nps()
                    P.op("pe", lambda e, k2=k2, i=i: e.transpose(PS[k2][:, 0:128], KTOK[:, i, :], ident[:]), [B_KTOK, B_const], [B_PS[k2]])
                    P.op("act", lambda e, k2=k2, i=i: e.activation(KT[0:64, 0, 128 + i * 128:256 + i * 128], PS[k2][0:64, 0:128], AF.Copy), [B_PS[k2]], [B_KT])
                    P.op("act", lambda e, k2=k2, i=i: e.activation(KT[0:64, 1, 128 + i * 128:256 + i * 128], PS[k2][64:128, 0:128], AF.Copy), [B_PS[k2]], [B_KT])
            items.append((full(w_in_v, 1024), kv_fn))

            def uv_block(slot, blk):
                def fn(b):
                    for i in range(ng):
                        k = nps()
                        for kc in range(16):
                            P.op("pe", lambda e, k=k, kc=kc, i=i, b=b: e.matmul(PS[k][:, 0:256], XNT[:, kc, i * 128:(i + 1) * 128], WB[b][:, kc, :],
                                                                               start=(kc == 0), stop=(kc == 15)), B_WB[b] + B_R2, [B_PS[k]])
                        P.op("act", lambda e, k=k, i=i: e.activation(R3[:, slot, i * 1024 + blk * 256: i * 1024 + (blk + 1) * 256], PS[k][:, 0:256],
                                                                     AF.Gelu_apprx_tanh), [B_PS[k]], B_R3[slot])
                return fn
            for blk in range(4):
                items.append((full(w_in_v, 1280 + blk * 256), uv_block(0, blk)))
            for blk in range(4):
                items.append((full(w_in_v, 2304 + blk * 256), uv_block(1, blk)))

            def mixers(_):
                for i in range(ng):
                    VNi = R3[:, 1, i * 1024:(i + 1) * 1024]
                    P.op("act", lambda e, i=i, VNi=VNi: e.activation(R4f[:, 0:1024], VNi, AF.Square, accum_out=scol(16 + i)),
                         B_R3[1], [B_R4[0], B_small])
                    rstd_from(16 + i, 24 + i, 1024)
                    P.op("dve", lambda e, i=i, VNi=VNi: e.scalar_tensor_tensor(out=VNi, in0=VNi, scalar=scol(24 + i), in1=sgug[:],
                                                                              op0=ALU.mult, op1=ALU.mult), B_R3[1] + [B_small, B_const], B_R3[1])
                for i, gt in enumerate(tiles):
                    has_prev = sample or gt != 0
                    ncur = 16 if sample else 128
                    pso = [5, 6, 7]
                    kbs = ([0] if has_prev else []) + [1]

                    def st_S(hp):
                        k = nps(0, 5)
                        for hh in range(2):
                            h = 2 * hp + hh
                            kv = h // 8
                            for kb in kbs:
                                nk = 128 if kb == 0 else ncur
                                c0 = i * 128 + kb * 128
                                P.op("pe", lambda e, k=k, hh=hh, kb=kb, nk=nk, c0=c0, kv=kv, h=h, i=i: e.matmul(
                                    PS[k][0:nk, (hh * 2 + kb) * 128:(hh * 2 + kb + 1) * 128], KT[0:64, kv, c0:c0 + nk],
                                    QT[0:64, h, i * 128:(i + 1) * 128], start=True, stop=True), [B_KT] + B_R1, [B_PS[k]])
                        return k

                    def st_chain(hp, k):
                        s = hp % 2
                        for kb in kbs:
                            nk = 128 if kb == 0 else ncur
                            P.op("dve", lambda e, k=k, kb=kb, nk=nk, hp=hp, s=s: e.tensor_tensor(
                                PT[s][0:nk, :, kb, :], PS[k][0:nk, :].rearrange("p (a b q) -> p a b q", a=2, b=2)[:, :, kb, :],
                                biasT[0:nk, 2 * hp:2 * hp + 2, kb, :], ALU.add), [B_PS[k], B_bias], [B_PT[s][kb]])
                        for kb in kbs:
                            nk = 128 if kb == 0 else ncur
                            P.op("act", lambda e, kb=kb, nk=nk, s=s: e.activation(PT[s][0:nk, :, kb, :], PT[s][0:nk, :, kb, :], AF.Exp),
                                 [B_PT[s][kb]], [B_PT[s][kb]])
                        if not sample:
                            P.op("dve", lambda e, s=s: e.memset(PT[s][64:128, :, 1, 0:64], 0.0), [], [B_PT[s][1]])
                            if has_prev:
                                P.op("dve", lambda e, s=s: e.memset(PT[s][0:64, :, 0, 64:128], 0.0), [], [B_PT[s][0]])

                    def st_PV(hp):
                        s = hp % 2
                        for hh in range(2):
                            h = 2 * hp + hh
                            kv = h // 8
                            bk = pso[h // 6]
                            hl = h % 6
                            for n_, kb in enumerate(kbs):
                                nk = 128 if kb == 0 else ncur
                                slot = i + kb
                                P.op("pe", lambda e, bk=bk, hl=hl, s=s, hh=hh, kb=kb, nk=nk, slot=slot, kv=kv, n_=n_, nkb=len(kbs): e.matmul(
                                    PS[bk][:, hl * 65:(hl + 1) * 65], PT[s][0:nk, hh, kb, :], VA[0:nk, slot, kv, :],
                                    start=(n_ == 0), stop=(n_ == nkb - 1)), [B_PT[s][kb], B_VA], [B_PS[bk]])
                    kcur = st_S(0)
                    for hp in range(8):
                        knext = st_S(hp + 1) if hp + 1 < 8 else None
                        st_chain(hp, kcur)
                        st_PV(hp)
                        kcur = knext
                    for b3 in range(3):
                        nh = 6 if b3 < 2 else 4
                        h0 = b3 * 6
                        P.op("dve", lambda e, b3=b3, nh=nh, h0=h0: e.tensor_tensor(
                            DEN[:, h0:h0 + nh], PS[pso[b3]][:, 0:nh * 65].rearrange("p (h c) -> p h c", c=65)[:, :, 64],
                            esink[:, h0:h0 + nh], ALU.add), [B_PS[pso[b3]], B_const], [B_DEN])
                    P.op("dve", lambda e: e.reciprocal(RDEN[:], DEN[:]), [B_DEN], [B_DEN])
                    for h in range(16):
                        bk = pso[h // 6]
                        hl = h % 6
                        P.op("dve", lambda e, bk=bk, hl=hl, h=h, i=i: e.tensor_scalar(
                            R3[:, 2, i * 1024 + h * 64: i * 1024 + (h + 1) * 64], PS[bk][:, hl * 65:hl * 65 + 64], RDEN[:, h:h + 1], None, ALU.mult),
                            [B_PS[bk], B_DEN], B_R3[2])
                for i in range(ng):
                    ks = [nps(), nps()]
                    for g in range(8):
                        k = ks[g // 4]
                        P.op("pe", lambda e, k=k, g=g, i=i: e.matmul(PS[k][:, (g % 4) * 128:(g % 4 + 1) * 128], wmT[:, g, :],
                                                                    R3[:, 1, i * 1024 + g * 128: i * 1024 + (g + 1) * 128], start=True, stop=True),
                             [B_const] + B_R3[1], [B_PS[k]])
                    for g in range(8):
                        k = ks[g // 4]
                        Ug = R3[:, 0, i * 1024 + g * 128: i * 1024 + (g + 1) * 128]
                        P.op("dve", lambda e, k=k, g=g, Ug=Ug: e.scalar_tensor_tensor(out=Ug, in0=PS[k][:, (g % 4) * 128:(g % 4 + 1) * 128],
                                                                                    scalar=bsT[:, g:g + 1], in1=Ug, op0=ALU.add, op1=ALU.mult),
                             [B_PS[k], B_const] + B_R3[0], B_R3[0])
                if sample:
                    P.dma("sp", lambda e: e.dma_start(out=nk_s, in_=KTOK[0:16, 0, :]), B_KTOK, reads=[B_KTOK], is_out=True)
                    P.dma("sp", lambda e: e.dma_start(out=nv_s.rearrange("p (k d) -> p k d", k=2), in_=VA[0:16, 1, :, 0:64]), B_VA, reads=[B_VA], is_out=True)
                    P.dma("sp", lambda e: e.dma_start(out=nsgu, in_=R3[0:16, 1, 0:1024]), B_R3[1][0], reads=B_R3[1], is_out=True)
                elif tiles[-1] == nt - 1:
                    il = ng - 1
                    P.dma("sp", lambda e: e.dma_start(out=nk_p, in_=KTOK[:, il, :]), B_KTOK, reads=[B_KTOK], is_out=True)
                    P.dma("sp", lambda e: e.dma_start(out=nv_p.rearrange("p (k d) -> p k d", k=2), in_=VA[:, 1 + il, :, 0:64]), B_VA, reads=[B_VA], is_out=True)
                for i in range(ng):
                    for src_slot, dst0, dB in ((2, 0, B_R1[0]), (0, 8, B_R1[1])):
                        for f4 in range(2):
                            k = nps()
                            for j in range(4):
                                f = f4 * 4 + j
                                P.op("pe", lambda e, k=k, j=j, f=f, i=i, src_slot=src_slot: e.transpose(
                                    PS[k][:, j * 128:(j + 1) * 128], R3[:, src_slot, i * 1024 + f * 128: i * 1024 + (f + 1) * 128], ident[:]),
                                    B_R3[src_slot] + [B_const], [B_PS[k]])
                            P.op("act", lambda e, k=k, f4=f4, i=i, dst0=dst0: e.activation(
                                R1[:, dst0 + f4 * 4: dst0 + (f4 + 1) * 4, i * 128:(i + 1) * 128], PS[k][:, :].rearrange("p (a t) -> p a t", a=4), AF.Copy),
                                [B_PS[k]], [dB])
            items.append((None, mixers))

            SGA = XG[:, 0:1024]
            SGB = XG[:, 1024:2048]

            def gate_block(which, nb):
                def fn(b):
                    dst = SGA if which == 0 else SGB
                    for i in range(ng):
                        k = nps()
                        for kc in range(16):
                            P.op("pe", lambda e, k=k, kc=kc, i=i, b=b: e.matmul(PS[k][:, 0:256], XNT[:, kc, i * 128:(i + 1) * 128], WB[b][:, kc, :],
                                                                               start=(kc == 0), stop=(kc == 15)), B_WB[b] + B_R2, [B_PS[k]])
                        P.op("act", lambda e, k=k, i=i, dst=dst: e.activation(dst[:, i * 256:(i + 1) * 256], PS[k][:, 0:256], AF.Sigmoid),
                             [B_PS[k]], [B_XG])
                return fn

            def papb_block(nb):
                def fn(b):
                    for i in range(ng):
                        ka = nps()
                        kb_ = nps()
                        for kc in range(8):
                            P.op("pe", lambda e, ka=ka, kc=kc, i=i, b=b: e.matmul(PS[ka][:, 0:256], R1[:, kc, i * 128:(i + 1) * 128], WB[b][:, kc, :],
                                                                                 start=(kc == 0), stop=(kc == 7)), B_WB[b] + B_R1, [B_PS[ka]])
                        for kc in range(8):
                            P.op("pe", lambda e, kb_=kb_, kc=kc, i=i, b=b: e.matmul(PS[kb_][:, 0:256], R1[:, 8 + kc, i * 128:(i + 1) * 128], WB[b][:, 8 + kc, :],
                                                                                   start=(kc == 0), stop=(kc == 7)), B_WB[b] + B_R1, [B_PS[kb_]])
                        sa = SGA[:, i * 256:(i + 1) * 256]
                        sb_ = SGB[:, i * 256:(i + 1) * 256]
                        P.op("dve", lambda e, ka=ka, sa=sa: e.tensor_tensor(sa, sa, PS[ka][:, 0:256], ALU.mult), [B_PS[ka], B_XG], [B_XG])
                        P.op("dve", lambda e, kb_=kb_, sb_=sb_: e.tensor_tensor(sb_, sb_, PS[kb_][:, 0:256], ALU.mult), [B_PS[kb_], B_XG], [B_XG])
                        P.op("dve", lambda e, sa=sa, sb_=sb_: e.tensor_tensor(sa, sa, sb_, ALU.add), [B_XG], [B_XG])
                        k = nps()
                        for j in range(2):
                            P.op("pe", lambda e, k=k, j=j, sa=sa: e.transpose(PS[k][:, j * 128:(j + 1) * 128], sa[:, j * 128:(j + 1) * 128], ident[:]),
                                 [B_XG, B_const], [B_PS[k]])
                        HTv = R4
                        P.op("dve", lambda e, k=k, i=i, HTv=HTv: e.tensor_copy(HTv[:, nb * 2:nb * 2 + 2, i * 128:(i + 1) * 128],
                                                                              PS[k][:, 0:256].rearrange("p (a t) -> p a t", a=2)),
                             [B_PS[k]], B_R4)
                return fn
            for nb in range(8):
                items.append((full(w_in_v, 3328 + nb * 256), gate_block(0, nb)))
                items.append((full(w_in_v, 5376 + nb * 256), gate_block(1, nb)))
                items.append(([(lambda b: WB[b][:, 0:8, :], w_pa_v[:, :, nb * 256:(nb + 1) * 256]),
                               (lambda b: WB[b][:, 8:16, :], w_pb_v[:, :, nb * 256:(nb + 1) * 256])], papb_block(nb)))

            def wout_block(nb):
                def fn(b):
                    HTv = R4
                    for i in range(ng):
                        k = nps()
                        for kc in range(16):
                            P.op("pe", lambda e, k=k, kc=kc, i=i, b=b: e.matmul(PS[k][:, 0:256], HTv[:, kc, i * 128:(i + 1) * 128], WB[b][:, kc, :],
                                                                               start=(kc == 0), stop=(kc == 15)), B_WB[b] + B_R4, [B_PS[k]])
                        xs_ = X[:, i, nb * 256:(nb + 1) * 256]
                        P.op("dve", lambda e, k=k, xs_=xs_: e.tensor_tensor(xs_, xs_, PS[k][:, 0:256], ALU.add), [B_PS[k], B_X[i]], [B_X[i]])
                return fn
            for nb in range(8):
                items.append((full(w_out_v, nb * 256), wout_block(nb)))

            def peer_norm(_):
                for i in range(ng):
                    norm_T(i, gffnT, XNTb, B_R2, 32 + i)
            items.append((None, peer_norm))

            def wq_block(blk):
                def fn(b):
                    for j in range(2):
                        c = blk * 2 + j
                        k = nps()
                        for kc in range(16):
                            P.op("pe", lambda e, k=k, kc=kc, j=j, b=b: e.matmul(PS[k][:, 0:T], WB[b][:, kc, j * 128:(j + 1) * 128], XNTb[:, kc, 0:T],
                                                                               start=(kc == 0), stop=(kc == 15)), B_WB[b] + B_R2, [B_PS[k]])
                        P.op("act", lambda e, k=k, c=c: e.activation(QPT[:, c, 0:T], PS[k][:, 0:T], AF.Copy), [B_PS[k]], [B_QPT])
                return fn
            for blk in range(8):
                items.append((full(w_q_v, blk * 256), wq_block(blk)))

            if "it=" in dbg:
                items = items[:int(dbg.split("it=")[1].split(",")[0])]
            run_items(items)

            for i, gt in enumerate(tiles):
                if "nopeer" in dbg:
                    break
                SC = R3[:, 0, :].rearrange("p (a n) -> p a n", a=16)
                TMP = R3[:, 1, :].rearrange("p (a n) -> p a n", a=16)
                CAND = R3[:, 2, :]
                sks = [nps(0, 4) for _ in range(4)]
                for hp in range(16):
                    k = sks[hp // 4]
                    P.op("pe", lambda e, k=k, hp=hp, i=i: e.matmul(PS[k][:, (hp % 4) * 128:(hp % 4 + 1) * 128], QPT[:, hp, i * 128:(i + 1) * 128],
                                                                  skT[:, hp, :], start=True, stop=True), [B_QPT, B_const], [B_PS[k]])
                for q4 in range(4):
                    P.op("act", lambda e, q4=q4: e.activation(R3[:, 0, q4 * 512:(q4 + 1) * 512], PS[sks[q4]][:, :], AF.Copy), [B_PS[sks[q4]]], B_R3[0])
                for hp in range(16):
                    P.op("dve", lambda e, hp=hp: e.max(out=SV[:, hp, 0:8], in_=SC[:, hp, :]), B_R3[0], [B_SV])
                for hp in range(16):
                    P.op("dve", lambda e, hp=hp: e.match_replace(out=TMP[:, hp, :], in_to_replace=SV[:, hp, 0:8], in_values=SC[:, hp, :], imm_value=-1e30),
                         B_R3[0] + [B_SV], B_R3[1])
                for hp in range(16):
                    P.op("dve", lambda e, hp=hp: e.max(out=SV[:, hp, 8:16], in_=TMP[:, hp, :]), B_R3[1], [B_SV])
                for hp in range(16):
                    for o in (0, 8):
                        P.op("dve", lambda e, hp=hp, o=o: e.max_index(out=SI[:, hp, o:o + 8], in_max=SV[:, hp, o:o + 8], in_values=SC[:, hp, :]),
                             B_R3[0] + [B_SV], [B_SI])
                P.op("dve", lambda e: e.tensor_copy(SIF[:], SI[:]), [B_SI], [B_SIF])
                sv4 = SV[:, :, :].rearrange("p (h two) k -> p h two k", two=2)
                sif4 = SIF[:, :, :].rearrange("p (h two) k -> p h two k", two=2)
                CAND4 = CAND.rearrange("p (h a b) -> p h a b", h=8, a=16)
                P.op("dve", lambda e: e.tensor_tensor(CAND4, sv4[:, :, 0, :].unsqueeze(3).broadcast_to([128, 8, 16, 16]),
                                                      sv4[:, :, 1, :].unsqueeze(2).broadcast_to([128, 8, 16, 16]), ALU.add), [B_SV], B_R3[2])
                CAND2 = CAND.rearrange("p (h m) -> p h m", h=8)
                TMPC = R3[:, 1, :].rearrange("p (h m) -> p h m", h=8)
                for h in range(8):
                    P.op("dve", lambda e, h=h: e.max(out=CV[:, h, 0:8], in_=CAND2[:, h, :]), B_R3[2], [B_CV])
                for h in range(8):
                    P.op("dve", lambda e, h=h: e.match_replace(out=TMPC[:, h, :], in_to_replace=CV[:, h, 0:8], in_values=CAND2[:, h, :], imm_value=-1e30),
                         B_R3[2] + [B_CV], B_R3[1])
                for h in range(8):
                    P.op("dve", lambda e, h=h: e.max(out=CV[:, h, 8:16], in_=TMPC[:, h, :]), B_R3[1], [B_CV])
                for h in range(8):
                    for o in (0, 8):
                        P.op("dve", lambda e, h=h, o=o: e.max_index(out=CI[:, h, o:o + 8], in_max=CV[:, h, o:o + 8], in_values=CAND2[:, h, :]),
                             B_R3[2] + [B_CV], [B_CI])
                CIf = CI[:, :, :].rearrange("p h k -> p (h k)")
                P.op("dve", lambda e: e.tensor_scalar(IK[:], CIf, shc[:, 0:1], None, ALU.logical_shift_right), [B_CI, B_const], [B_IK])
                P.op("dve", lambda e: e.tensor_scalar(JK[:], CIf, shc[:, 1:2], None, ALU.bitwise_and), [B_CI, B_const], [B_IK])
                P.op("dve", lambda e: e.tensor_copy(IKF[:, :, :].rearrange("p h k -> p (h k)"), IK[:]), [B_IK], [B_IKF])
                P.op("dve", lambda e: e.tensor_copy(JKF[:, :, :].rearrange("p h k -> p (h k)"), JK[:]), [B_IK], [B_IKF])
                io4 = iota16[:, :].unsqueeze(1).unsqueeze(1).broadcast_to([128, 8, 16, 16])
                for w_, (KF, SEL) in enumerate(((IKF, SEL0), (JKF, SEL1))):
                    E4w = R2f[:, w_ * 2048:(w_ + 1) * 2048].rearrange("p (h a b) -> p h a b", h=8, a=16)
                    E4r = E4w
                    P.op("dve", lambda e, KF=KF, E4w=E4w: e.tensor_tensor(E4w, KF[:, :, :].unsqueeze(3).broadcast_to([128, 8, 16, 16]), io4, ALU.is_equal),
                         [B_IKF, B_const], [B_R2[w_]])
                    P.op("dve", lambda e, E4w=E4w, E4r=E4r, w_=w_: e.tensor_tensor(E4w, E4r, sif4[:, :, w_, :].unsqueeze(2).broadcast_to([128, 8, 16, 16]), ALU.mult),
                         [B_R2[w_], B_SIF], [B_R2[w_]])
                    P.op("dve", lambda e, SEL=SEL, E4r=E4r: e.tensor_reduce(SEL[:], E4r, AX.X, ALU.add), [B_R2[w_]], [B_SEL])
                P.op("dve", lambda e: e.scalar_tensor_tensor(out=EIF[:], in0=SEL0[:, :, :].rearrange("p h k -> p (h k)"), scalar=128.0,
                                                             in1=SEL1[:, :, :].rearrange("p h k -> p (h k)"), op0=ALU.mult, op1=ALU.add), [B_SEL], [B_SEL])
                P.op("dve", lambda e: e.tensor_copy(EIDX[:], EIF[:]), [B_SEL], [B_EIDX])
                P.op("dve", lambda e: e.tensor_tensor(EW[:], CV[:], CV[:, :, 0:1].broadcast_to([128, 8, 16]), ALU.subtract), [B_CV], [B_GW])
                P.op("act", lambda e: e.activation(EW[:], EW[:], AF.Exp), [B_GW], [B_GW])
                P.op("dve", lambda e: e.tensor_reduce(SUMW[:], EW[:], AX.X, ALU.add), [B_GW], [B_GW])
                P.op("dve", lambda e: e.reciprocal(RW[:], SUMW[:]), [B_GW], [B_GW])
                P.op("dve", lambda e: e.tensor_tensor(GW[:], EW[:], RW[:, :].unsqueeze(2).broadcast_to([128, 8, 16]), ALU.mult), [B_GW], [B_GW])
                GWf = GW[:, :, :].rearrange("p h k -> p (h k)")
                P.op("dve", lambda e, i=i: e.scalar_tensor_tensor(out=XG[:], in0=X[:, i, :], scalar=scol(40 + i), in1=gffn[:], op0=ALU.mult, op1=ALU.mult),
                     [B_X[i], B_small, B_const], [B_XG])
                acc = [4, 5, 6, 7]
                NB = 9
                JUNK = R4f[0:nr, 0:2048]

                def UV(s):
                    if s < 3:
                        return R3[:, s, :].bitcast(BF16)[0:nr, :]
                    if s < 5:
                        return VEB[:, 2 * (s - 3):2 * (s - 3) + 2, :].rearrange("p a b -> p (a b)")[0:nr, :]
                    if s < 7:
                        return R2f[:, (s - 5) * 2048:(s - 4) * 2048].bitcast(BF16)[0:nr, :]
                    return WBraw[s - 7][:, :].bitcast(BF16)[0:nr, :]

                def BUV(s):
                    if s < 3:
                        return B_R3[s]
                    if s < 5:
                        return [B_VE[2 * (s - 3)], B_VE[2 * (s - 3) + 1]]
                    if s < 7:
                        return [B_R2[s - 5]]
                    return B_WB[s - 7]
                LA = NB - 2

                def gather(cg):
                    sg_ = cg % NB
                    P.dma("pool", lambda e, cg=cg, sg_=sg_: e.indirect_dma_start(out=UV(sg_), out_offset=None, in_=euvb,
                                                                                 in_offset=bass.IndirectOffsetOnAxis(ap=EIDX[0:nr, cg:cg + 1], axis=0)),
                          BUV(sg_)[0], reads=[B_EIDX, B_TUV], writes=BUV(sg_))
                if "nogather" not in dbg:
                    for cg in range(LA):
                        gather(cg)
                for c in range(129 if "nogather" not in dbg else 0):
                    if c + LA < 128:
                        gather(c + LA)
                    if c < 128:
                        sb_ = c % NB
                        p4 = c % 4
                        P.op("dve", lambda e, c=c, sb_=sb_: e.scalar_tensor_tensor(out=JUNK, in0=UV(sb_)[:, 0:2048], scalar=1.0, in1=XG[0:nr, :],
                                                                                   op0=ALU.mult, op1=ALU.mult, accum_out=AA[0:nr, c:c + 1]),
                             BUV(sb_) + [B_XG], [B_R4[0], B_AA[p4]])
                        P.op("act", lambda e, c=c: e.activation(AGL[0:nr, c:c + 1], AA[0:nr, c:c + 1], AF.Gelu_apprx_tanh), [B_AA[p4]], [B_AGL[p4]])
                    if c >= 1:
                        c1 = c - 1
                        sb_ = c1 % NB
                        p4 = c1 % 4
                        P.op("act", lambda e, c1=c1: e.activation(HW[0:nr, c1:c1 + 1], AGL[0:nr, c1:c1 + 1], AF.Copy, scale=GWf[0:nr, c1:c1 + 1]),
                             [B_AGL[p4], B_GW], [B_HW[p4]])
                        P.op("act", lambda e, c1=c1, p4=p4: e.activation(DIAG[p4][0:nr, :], ident[0:nr, :], AF.Copy, scale=HW[0:nr, c1:c1 + 1]),
                             [B_HW[p4], B_const], [B_DIAG[p4]])
                        for j in range(4):
                            P.op("pe", lambda e, j=j, sb_=sb_, p4=p4, c1=c1: e.matmul(PS[acc[j]][:, :], DIAG[p4][0:nr, :], UV(sb_)[:, 2048 + j * 512:2048 + (j + 1) * 512],
                                                                                  start=(c1 == 0), stop=(c1 == 127)), [B_DIAG[p4]] + BUV(sb_), [B_PS[acc[j]]])
                for j in range(4):
                    xs_ = X[0:nr, i, j * 512:(j + 1) * 512]
                    P.op("dve", lambda e, j=j, xs_=xs_: e.tensor_tensor(xs_, xs_, PS[acc[j]][0:nr, :], ALU.add), [B_PS[acc[j]], B_X[i]], [B_X[i]])
                P.op("act", lambda e, i=i: e.activation(R4f[0:nr, 2048:4096], X[0:nr, i, :], AF.Square, accum_out=small[0:nr, 48 + i:49 + i]),
                     [B_X[i]], [B_R4[1], B_small])
                rstd_from(48 + i, 56 + i, D, rows=nr)
                P.op("dve", lambda e, i=i: e.scalar_tensor_tensor(out=X[0:nr, i, :], in0=X[0:nr, i, :], scalar=small[0:nr, 56 + i:57 + i], in1=gfin[0:nr, :],
                                                                  op0=ALU.mult, op1=ALU.mult), [B_X[i], B_small, B_const], [B_X[i]])
                if sample:
                    P.dma("sp", lambda e, i=i: e.dma_start(out=y_s, in_=X[0:16, i, :]), B_X[i], reads=[B_X[i]], is_out=True)
                else:
                    P.dma("sp", lambda e, i=i, gt=gt: e.dma_start(out=y_p[gt * 128:(gt + 1) * 128, :], in_=X[:, i, :]), B_X[i], reads=[B_X[i]], is_out=True)

        for g0 in range(0, nt, G):
            do_group(list(range(g0, g0 + G)), False)
            first_group[0] = False
        if "nosample" not in dbg:
            do_group([0], True)
        P.finish()
        P.emit()
    return nc


def _t5_bucket_np(rel):
    try:
        import jax
        import jax.numpy as jnp
        with jax.default_device(jax.devices("cpu")[0]):
            r = jnp.asarray(rel, dtype=jnp.int32)
            half = 16
            max_exact = 8
            ret = jnp.where(r > 0, half, 0)
            n = jnp.abs(r)
            nf = jnp.maximum(n, 1).astype(jnp.float32)
            large = max_exact + (jnp.log(nf / max_exact) / math.log(128 / max_exact) * (half - max_exact)).astype(jnp.int32)
            large = jnp.minimum(large, half - 1)
            return np.asarray(ret + jnp.where(n < max_exact, n, large))
    except Exception:
        r = np.asarray(rel, dtype=np.int32)
        ret = np.where(r > 0, 16, 0)
        n = np.abs(r)
        nf = np.maximum(n, 1).astype(np.float32)
        large = 8 + (np.log(nf / np.float32(8)) / np.float32(math.log(16.0)) * np.float32(8)).astype(np.int32)
        large = np.minimum(large, 15)
        return ret + np.where(n < 8, n, large)


_NC_CACHE = {}


def kernel(x_prompt, x_sample, cache_k_swa, cache_v_swa, norm_mix_g, w_in, sgu_norm_g, sgu_w_s, sgu_b_s,
           attn_sinks, rel_bias_table, w_branch_attn, w_branch_sgu, w_out, norm_ffn_g, peer_w_query,
           peer_sub_keys, peer_expert_u, peer_expert_v, norm_final_g):
    f = lambda a: np.ascontiguousarray(np.asarray(a), dtype=np.float32)
    if "nc" not in _NC_CACHE:
        _NC_CACHE["nc"] = build_program()
    nc = _NC_CACHE["nc"]
    kb = np.arange(2)[:, None, None]; kk = np.arange(128)[None, :, None]; qq = np.arange(128)[None, None, :]
    rel = (kb - 1) * 128 + kk - qq
    bkt = _t5_bucket_np(rel).reshape(-1)
    oh = np.zeros((32, 32768), np.float32)
    oh[bkt, np.arange(32768)] = 1.0
    s_i = np.arange(128)[:, None, None]; t_i = np.arange(128)[None, None, :]
    trilT = np.ascontiguousarray(np.broadcast_to((t_i >= s_i), (128, 8, 128))).astype(np.float32)
    shared = dict(
        w_in=f(w_in[0]), w_pa=f(w_branch_attn[0]), w_pb=f(w_branch_sgu[0]), w_out=f(w_out[0]), w_q=f(peer_w_query[0]),
        eu=f(peer_expert_u[0]), ev=f(peer_expert_v[0]),
        gmixT=f(np.asarray(norm_mix_g[0]).reshape(16, 128).T), gffnT=f(np.asarray(norm_ffn_g[0]).reshape(16, 128).T),
        gffn_bc=f(np.broadcast_to(np.asarray(norm_ffn_g[0])[None, :], (128, D))),
        gfin_bc=f(np.broadcast_to(np.asarray(norm_final_g)[None, :], (128, D))),
        sgug_bc=f(np.broadcast_to(np.asarray(sgu_norm_g[0])[None, :], (128, 1024))),
        wsT=f(np.asarray(sgu_w_s[0]).transpose(2, 0, 1)), trilT=trilT, bsT=f(np.asarray(sgu_b_s[0]).T),
        sinks_bc=f(np.broadcast_to(np.asarray(attn_sinks[0])[None, :], (128, 16))), table=f(rel_bias_table), oh=oh,
        skT=f(np.asarray(peer_sub_keys[0]).reshape(16, 128, 128).transpose(2, 0, 1)),
        ident=np.eye(128, dtype=np.float32), iota16=f(np.broadcast_to(np.arange(16, dtype=np.float32)[None, :], (128, 16))),
        shc=np.ascontiguousarray(np.broadcast_to(np.array([[4, 15]], np.uint32), (128, 2))),
    )
    xpn = np.asarray(x_prompt); xsn = np.asarray(x_sample); ckn = np.asarray(cache_k_swa); cvn = np.asarray(cache_v_swa)
    in_maps = []
    for c in range(8):
        m = dict(shared)
        m["xp"] = f(xpn[c]); m["xs"] = f(xsn[c])
        m["ck"] = f(ckn[0, c].reshape(128, 128)); m["cv"] = f(cvn[0, c].reshape(128, 128))
        in_maps.append(m)
    res = run_bass_kernel_spmd(nc, in_maps, core_ids=list(range(8)))
    r = res.results
    y_prompt = np.stack([r[c]["y_p"] for c in range(8)]).astype(np.float32)
    y_sample = np.stack([r[c]["y_s"] for c in range(8)]).astype(np.float32)
    nk_p = np.stack([r[c]["nk_p"].reshape(128, 2, 64) for c in range(8)])[None].astype(np.float32)
    nv_p = np.stack([r[c]["nv_p"].reshape(128, 2, 64) for c in range(8)])[None].astype(np.float32)
    nk_s = np.stack([r[c]["nk_s"].reshape(16, 2, 64) for c in range(8)])[None].astype(np.float32)
    nv_s = np.stack([r[c]["nv_s"].reshape(16, 2, 64) for c in range(8)])[None].astype(np.float32)
    nsg = np.stack([r[c]["nsgu"] for c in range(8)])[None].astype(np.float32)
    return (y_prompt, y_sample, nk_p, nv_p, nk_s, nv_s, nsg)
```

```python
import math
import numpy as np
from contextlib import ExitStack
import concourse.bass as bass
import concourse.mybir as mybir
from concourse.bass_utils import run_bass_kernel_spmd

F32 = mybir.dt.float32
F32R = mybir.dt.float32r
BF16 = mybir.dt.bfloat16
I32 = mybir.dt.int32
U32 = mybir.dt.uint32
AF = mybir.ActivationFunctionType
ALU = mybir.AluOpType
AX = mybir.AxisListType

D = 2048
DIN = 7424
SEQ = 2048
NT = SEQ // 128
G = 2
TMAX = G * 128
EPS = 1e-6
WCOL = 256


class Buf:
    __slots__ = ("name", "lw", "rd", "dsem", "dcnt", "excl")

    def __init__(self, name, excl=False):
        self.name = name
        self.excl = excl
        self.lw = None
        self.rd = {}
        self.dsem = {}
        self.dcnt = {}


class Prog:
    ENG = ("sp", "act", "dve", "pool", "pe")

    def __init__(self, nc, es):
        self.nc = nc
        self.es = es
        self.st = {e: [] for e in self.ENG}
        self.sem = {}
        self.cnt = {}
        self.waited = {e: {} for e in self.ENG}
        self.nsem = 0
        self.out_toks = []
        for e in self.ENG:
            self._new_sem(e)

    def _mk(self, name):
        self.nsem += 1
        return self.es.enter_context(self.nc.semaphore(f"{name}{self.nsem}"))

    def _new_sem(self, e):
        self.sem[e] = self._mk("e" + e)
        self.cnt[e] = 0

    def _wait(self, e, tok):
        sem, val, src = tok
        if src == e and e == "pe":
            return
        w = self.waited[e]
        if w.get(id(sem), -1) >= val:
            return
        w[id(sem)] = val
        self.st[e].append(("w", sem, val))

    def _deps(self, e, reads, writes):
        need = {}
        def add(tok):
            k = id(tok[0])
            if k not in need or need[k][1] < tok[1]:
                need[k] = tok
        for b in reads:
            if b.lw is not None:
                add(b.lw)
            if b.excl:
                for t in b.rd.values():
                    if t[2] != e:
                        add(t)
        for b in writes:
            if b.lw is not None:
                add(b.lw)
            for t in b.rd.values():
                add(t)
        for tok in need.values():
            self._wait(e, tok)

    def _commit(self, tok, reads, writes):
        for b in reads:
            k = id(tok[0])
            if k not in b.rd or b.rd[k][1] < tok[1]:
                b.rd[k] = tok
        for b in writes:
            b.lw = tok
            b.rd = {}

    def op(self, e, fn, reads=(), writes=()):
        self._deps(e, reads, writes)
        if self.cnt[e] >= 30000:
            self._new_sem(e)
        self.cnt[e] += 1
        tok = (self.sem[e], self.cnt[e], e)
        self.st[e].append(("o", fn, self.sem[e], 1))
        self._commit(tok, reads, writes)
        return tok

    def dma(self, q, fn, dbuf, reads=(), writes=(), is_out=False):
        self._deps(q, reads, writes)
        if q not in dbuf.dsem or dbuf.dcnt[q] >= 48000:
            dbuf.dsem[q] = self._mk("d")
            dbuf.dcnt[q] = 0
        dbuf.dcnt[q] += 16
        tok = (dbuf.dsem[q], dbuf.dcnt[q], "dma")
        self.st[q].append(("o", fn, dbuf.dsem[q], 16))
        self._commit(tok, reads, writes)
        if is_out:
            self.out_toks.append(tok)
        return tok

    def finish(self):
        for tok in self.out_toks:
            self._wait("sp", tok)

    def emit(self):
        blk = self.es.enter_context(self.nc.Block())

        def run(e):
            def f(eng):
                for it in self.st[e]:
                    if it[0] == "w":
                        eng.wait_ge(it[1], it[2])
                    else:
                        it[1](eng).then_inc(it[2], it[3])
            return f
        blk.sync(run("sp"))
        blk.scalar(run("act"))
        blk.vector(run("dve"))
        blk.gpsimd(run("pool"))
        blk.tensor(run("pe"))


def build_program(nt=NT, dbg=""):
    SEQ = nt * 128
    nc = bass.Bass("TRN2", target_bir_lowering=False)

    def din(name, shape, dt=F32):
        return nc.dram_tensor(name, list(shape), dt, kind="ExternalInput").ap()

    def dout(name, shape, dt=F32):
        return nc.dram_tensor(name, list(shape), dt, kind="ExternalOutput").ap()

    xp = din("xp", [SEQ, D]); xs = din("xs", [16, D])
    ck = din("ck", [128, 128]); cvv = din("cv", [128, 128])
    w_in = din("w_in", [D, DIN]); w_pa = din("w_pa", [1024, D]); w_pb = din("w_pb", [1024, D])
    w_out = din("w_out", [D, D]); w_q = din("w_q", [D, D])
    eu = din("eu", [16384, D]); ev = din("ev", [16384, D])
    gmixT_d = din("gmixT", [128, 16]); gffnT_d = din("gffnT", [128, 16])
    gffn_d = din("gffn_bc", [128, D]); gfin_d = din("gfin_bc", [128, D]); sgug_d = din("sgug_bc", [128, 1024])
    wsT_d = din("wsT", [128, 8, 128]); trilT_d = din("trilT", [128, 8, 128]); bsT_d = din("bsT", [128, 8])
    sinks_d = din("sinks_bc", [128, 16]); table_d = din("table", [32, 16]); oh_d = din("oh", [32, 32768])
    skT_d = din("skT", [128, 16, 128]); ident_d = din("ident", [128, 128]); iota_d = din("iota16", [128, 16])
    shc_d = din("shc", [128, 2], U32)
    y_p = dout("y_p", [SEQ, D]); y_s = dout("y_s", [16, D])
    nk_p = dout("nk_p", [128, 128]); nv_p = dout("nv_p", [128, 128])
    nk_s = dout("nk_s", [16, 128]); nv_s = dout("nv_s", [16, 128]); nsgu = dout("nsgu", [16, 1024])
    bscr = nc.dram_tensor("bscr", [16, 32768], F32, kind="Internal").ap()
    euvb = nc.dram_tensor("euvb", [16384, 2 * D], BF16, kind="Internal").ap()
    wscr = nc.dram_tensor("wscr", [64, 128, 2048], F32, kind="Internal").ap()

    w_in_v = w_in.rearrange("(kc p) n -> p kc n", p=128)
    w_pa_v = w_pa.rearrange("(kc p) n -> p kc n", p=128)
    w_pb_v = w_pb.rearrange("(kc p) n -> p kc n", p=128)
    w_out_v = w_out.rearrange("(kc p) n -> p kc n", p=128)
    w_q_v = w_q.rearrange("(kc p) n -> p kc n", p=128)

    with ExitStack() as es:
        P = Prog(nc, es)

        def sb(name, shape, dt=F32):
            return es.enter_context(nc.sbuf_tensor("s_" + name, list(shape), dt))

        biasT = sb("biasT", [128, 16, 2, 128]); B_bias = Buf("biasT")
        gffn = sb("gffn", [128, D]); gfin = sb("gfin", [128, D]); sgug = sb("sgug", [128, 1024])
        skT = sb("skT", [128, 16, 128]); wmT = sb("wmT", [128, 8, 128]); trilT = sb("trilT", [128, 8, 128])
        ident = sb("ident", [128, 128]); gmixT = sb("gmixT", [128, 16]); gffnT = sb("gffnT", [128, 16])
        bsT = sb("bsT", [128, 8]); esink = sb("esink", [128, 16]); iota16 = sb("iota16", [128, 16])
        shc = sb("shc", [128, 2], U32); tab = sb("tab", [32, 16])
        B_const = Buf("const")
        X = sb("X", [128, G, D]); B_X = [Buf(f"X{i}") for i in range(G)]
        R1 = sb("R1", [128, 16, TMAX], BF16); B_R1 = [Buf("R1a"), Buf("R1b")]
        QPT = sb("QPT", [128, 16, TMAX]); B_QPT = Buf("QPT")
        R2f = sb("R2", [128, 16 * TMAX]); B_R2 = [Buf("R2a"), Buf("R2b")]
        XNTb = R2f[:, :].bitcast(BF16)[:, 0:16 * TMAX].rearrange("p (k t) -> p k t", k=16)
        XN2T = R2f[:, :].rearrange("p (k t) -> p k t", k=16)
        R4 = sb("R4", [128, 16, TMAX], BF16); B_R4 = [Buf("R4a"), Buf("R4b")]
        R4f = R4[:, :, :].rearrange("p a b -> p (a b)")
        WBraw = [sb(f"WB{i}", [128, 2048]) for i in range(2)]
        WB = [w[:, :].bitcast(BF16).rearrange("p (k n) -> p k n", k=16) for w in WBraw]
        WBf32 = [w[:, :].rearrange("p (k n) -> p k n", k=16) for w in WBraw]
        B_WB = [[Buf(f"WB{i}")] for i in range(2)]
        VEB = sb("VEB", [128, 4, 2048], BF16); B_VE = [Buf(f"VE{i}") for i in range(4)]
        for h_ in range(2):
            vv = VEB[:, 2 * h_:2 * h_ + 2, :].rearrange("p a b -> p (a b)")
            WB.append(vv.rearrange("p (k n) -> p k n", k=16))
            WBf32.append(vv.bitcast(F32).rearrange("p (k n) -> p k n", k=16))
            WBraw.append(vv.bitcast(F32))
            B_WB.append([B_VE[2 * h_], B_VE[2 * h_ + 1]])
        NWB = 4
        R3 = sb("R3", [128, 3, 2048]); B_R3 = [[Buf(f"R3_{j}a"), Buf(f"R3_{j}b")] for j in range(3)]
        XG = sb("XG", [128, 2048]); B_XG = Buf("XG")
        KT = sb("KT", [64, 2, 128 + TMAX], BF16); B_KT = Buf("KT")
        VA = sb("VA", [128, 1 + G, 2, 65]); B_VA = Buf("VA")
        KTOK = sb("KTOK", [128, G, 128]); B_KTOK = Buf("KTOK")
        PT = [sb(f"PT{i}", [128, 2, 2, 128]) for i in range(2)]; B_PT = [[Buf(f"PT{i}a"), Buf(f"PT{i}b")] for i in range(2)]
        DIAG = [sb(f"DIAG{i}", [128, 128], BF16) for i in range(4)]; B_DIAG = [Buf(f"DG{i}") for i in range(4)]
        small = sb("small", [128, 64]); B_small = Buf("small")
        DEN = sb("DEN", [128, 16]); RDEN = sb("RDEN", [128, 16]); B_DEN = Buf("DEN")
        SV = sb("SV", [128, 16, 16]); SI = sb("SI", [128, 16, 16], U32); SIF = sb("SIF", [128, 16, 16])
        CV = sb("CV", [128, 8, 16]); CI = sb("CI", [128, 8, 16], U32)
        IK = sb("IK", [128, 128], U32); JK = sb("JK", [128, 128], U32)
        IKF = sb("IKF", [128, 8, 16]); JKF = sb("JKF", [128, 8, 16])
        SEL0 = sb("SEL0", [128, 8, 16]); SEL1 = sb("SEL1", [128, 8, 16])
        EIF = sb("EIF", [128, 128]); EIDX = sb("EIDX", [128, 128], I32)
        EW = sb("EW", [128, 8, 16]); GW = sb("GW", [128, 8, 16]); SUMW = sb("SUMW", [128, 8]); RW = sb("RW", [128, 8])
        AA = sb("AA", [128, 128]); AGL = sb("AGL", [128, 128]); HW = sb("HW", [128, 128])
        B_SV = Buf("SV"); B_SI = Buf("SI"); B_SIF = Buf("SIF"); B_CV = Buf("CV"); B_CI = Buf("CI")
        B_IK = Buf("IK"); B_IKF = Buf("IKF"); B_SEL = Buf("SEL"); B_EIDX = Buf("EIDX"); B_GW = Buf("GW")
        B_AA = [Buf(f"AA{i}") for i in range(4)]; B_AGL = [Buf(f"AGL{i}") for i in range(4)]; B_HW = [Buf(f"HW{i}") for i in range(4)]
        PS = [es.enter_context(nc.psum_tensor(f"PS{i}", [128, 512], F32)) for i in range(8)]
        B_PS = [Buf(f"PS{i}", excl=True) for i in range(8)]
        psrr = [0]

        def nps(lo=0, hi=8):
            k = lo + psrr[0] % (hi - lo)
            psrr[0] += 1
            return k

        def scol(k):
            return small[:, k:k + 1]

        def rstd_from(ss_col, out_col, n, rows=128):
            P.op("dve", lambda e: e.tensor_scalar(small[0:rows, out_col:out_col + 1], small[0:rows, ss_col:ss_col + 1],
                                                  1.0 / n, EPS, ALU.mult, ALU.add), [B_small], [B_small])
            P.op("act", lambda e: e.activation(small[0:rows, out_col:out_col + 1], small[0:rows, out_col:out_col + 1], AF.Sqrt),
                 [B_small], [B_small])
            P.op("dve", lambda e: e.reciprocal(small[0:rows, out_col:out_col + 1], small[0:rows, out_col:out_col + 1]),
                 [B_small], [B_small])

        def ld(dst, src, buf=B_const):
            P.dma("sp", lambda e: e.dma_start(out=dst, in_=src), buf, writes=[buf])
        ld(gffn[:], gffn_d); ld(gfin[:], gfin_d); ld(sgug[:], sgug_d); ld(skT[:], skT_d)
        ld(wmT[:], wsT_d); ld(trilT[:], trilT_d); ld(ident[:], ident_d); ld(gmixT[:], gmixT_d); ld(gffnT[:], gffnT_d)
        ld(bsT[:], bsT_d); ld(esink[:], sinks_d); ld(iota16[:], iota_d); ld(shc[:], shc_d); ld(tab[:], table_d)
        P.op("dve", lambda e: e.tensor_tensor(wmT[:], wmT[:], trilT[:], ALU.mult), [B_const], [B_const])
        P.op("act", lambda e: e.activation(esink[:], esink[:], AF.Exp), [B_const], [B_const])
        P.op("dve", lambda e: e.memset(VA[:, :, :, 64:65], 1.0), [], [B_VA])
        QPTf = QPT[:, :, :].rearrange("p a b -> p (a b)")

        def bias_piece(pc):
            def fn(_):
                for hf in range(2):
                    P.dma("act", lambda e, hf=hf: e.dma_start(out=QPTf[0:32, hf * 2048:(hf + 1) * 2048],
                                                              in_=oh_d[:, pc * 4096 + hf * 2048:pc * 4096 + (hf + 1) * 2048]), B_QPT, writes=[B_QPT])
                for hf in range(2):
                    for q4 in range(4):
                        k = nps()
                        P.op("pe", lambda e, k=k, hf=hf, q4=q4: e.matmul(PS[k][0:16, :], tab[:, :], QPTf[0:32, hf * 2048 + q4 * 512:hf * 2048 + (q4 + 1) * 512],
                                                                        start=True, stop=True), [B_const, B_QPT], [B_PS[k]])
                        P.op("act", lambda e, k=k, q4=q4: e.activation(XG[0:16, q4 * 512:(q4 + 1) * 512], PS[k][0:16, :], AF.Copy),
                             [B_PS[k]], [B_XG])
                    P.dma("act", lambda e, hf=hf: e.dma_start(out=bscr[:, pc * 4096 + hf * 2048: pc * 4096 + (hf + 1) * 2048], in_=XG[0:16, :]),
                          B_XG, reads=[B_XG], writes=[B_bias])
                if pc == 7:
                    P.dma("act", lambda e: e.dma_start(out=biasT[:], in_=bscr.rearrange("h (kb kk qq) -> kk h kb qq", kb=2, kk=128)),
                          B_bias, reads=[B_bias], writes=[B_bias])
            return fn

        B_TUV = Buf("tblUV")
        RC = 2
        conv_chunks = []
        for t_, src in enumerate((eu, ev)):
            srcv = src.rearrange("(p r) d -> p r d", p=128)
            dstv = euvb.rearrange("(p r) (t d) -> p r t d", p=128, t=2)[:, :, t_, :]
            for r0 in range(0, 128, RC):
                def chunk(srcv=srcv, dstv=dstv, r0=r0):
                    b = len(conv_done) % 2
                    conv_done.append(1)
                    stg = VEB[:, 2 * b:2 * b + 2, :]
                    sB = [B_VE[2 * b], B_VE[2 * b + 1]]
                    P.dma("pool", lambda e: e.dma_start(out=stg, in_=srcv[:, r0:r0 + RC, :]), sB[0], writes=sB)
                    P.dma("sp", lambda e: e.dma_start(out=dstv[:, r0:r0 + RC, :], in_=stg), B_TUV, reads=sB, writes=[B_TUV])
                conv_chunks.append(chunk)
        conv_done = []

        def emit_conv(n):
            for _ in range(n):
                if len(conv_done) < len(conv_chunks):
                    conv_chunks[len(conv_done)]()

        wcount = [0]

        B_WSCR = Buf("wscr")
        first_group = [True]

        def load_block(specs, blk):
            nwb = 2 if first_group[0] else NWB
            b = wcount[0] % nwb
            wcount[0] += 1
            if first_group[0]:
                for (dst_fn, src) in specs:
                    P.dma("pool", lambda e, dst_fn=dst_fn, src=src, b=b: e.dma_start(out=dst_fn(b), in_=src),
                          B_WB[b][0], writes=B_WB[b])
                P.dma("sp", lambda e, b=b, blk=blk: e.dma_start(out=wscr[blk], in_=WBraw[b][:, :]), B_WSCR, reads=B_WB[b], writes=[B_WSCR])
            else:
                P.dma("sp", lambda e, b=b, blk=blk: e.dma_start(out=WBraw[b][:, :], in_=wscr[blk]), B_WB[b][0], reads=[B_WSCR], writes=B_WB[b])
            return b

        def run_items(items):
            widx = [k for k, it in enumerate(items) if it[0] is not None]
            bufs = {}
            depth = (2 if first_group[0] else NWB) - 1
            for j0 in range(min(depth, len(widx))):
                bufs[widx[j0]] = load_block(items[widx[j0]][0], j0)
            for k, (w, fn) in enumerate(items):
                if w is not None:
                    j = widx.index(k)
                    if j + depth < len(widx):
                        bufs[widx[j + depth]] = load_block(items[widx[j + depth]][0], j + depth)
                    fn(bufs[k])
                    if first_group[0]:
                        emit_conv(3)
                else:
                    fn(None)
            if first_group[0]:
                emit_conv(len(conv_chunks))

        def full(src_v, c0, ncol=WCOL):
            return [(lambda b: WB[b][:, :, 0:ncol], src_v[:, :, c0:c0 + ncol])]

        def do_group(tiles, sample):
            ng = len(tiles)
            T = ng * 128
            nr = 16 if sample else 128
            XNT = XNTb
            QT = R1

            for i, gt in enumerate(tiles):
                if sample:
                    P.op("dve", lambda e, i=i: e.memset(X[:, i, :], 0.0), [], [B_X[i]])
                    P.dma("sp", lambda e, i=i: e.dma_start(out=X[0:16, i, :], in_=xs), B_X[i], writes=[B_X[i]])
                else:
                    P.dma("sp", lambda e, i=i, gt=gt: e.dma_start(out=X[:, i, :], in_=xp[gt * 128:(gt + 1) * 128, :]), B_X[i], writes=[B_X[i]])
            if sample:
                P.dma("sp", lambda e: e.dma_start(out=KTOK[:, 0, :], in_=ck), B_KTOK, writes=[B_KTOK])
                k = nps()
                P.op("pe", lambda e, k=k: e.transpose(PS[k][:, 0:128], KTOK[:, 0, :], ident[:]), [B_KTOK, B_const], [B_PS[k]])
                P.op("act", lambda e, k=k: e.activation(KT[0:64, 0, 0:128], PS[k][0:64, 0:128], AF.Copy), [B_PS[k]], [B_KT])
                P.op("act", lambda e, k=k: e.activation(KT[0:64, 1, 0:128], PS[k][64:128, 0:128], AF.Copy), [B_PS[k]], [B_KT])
                P.dma("sp", lambda e: e.dma_start(out=VA[:, 0, :, 0:64], in_=cvv.rearrange("p (k d) -> p k d", k=2)), B_VA, writes=[B_VA])
            elif tiles[0] != 0:
                P.op("act", lambda e: e.activation(KT[0:64, :, 0:128], KT[0:64, :, TMAX:TMAX + 128], AF.Copy), [B_KT], [B_KT])
                P.op("dve", lambda e: e.tensor_copy(VA[:, 0, :, 0:64], VA[:, G, :, 0:64]), [B_VA], [B_VA])

            def norm_T(i, gT, dstR, dstB, col):
                XR = R3[:, 2, :]
                P.op("act", lambda e: e.activation(XR, X[:, i, :], AF.Square, accum_out=scol(col)), [B_X[i]], B_R3[2] + [B_small])
                rstd_from(col, col + 8, D)
                P.op("dve", lambda e: e.tensor_scalar(XR, X[:, i, :], scol(col + 8), None, ALU.mult), [B_X[i], B_small], B_R3[2])
                for k4 in range(4):
                    k = nps()
                    for j in range(4):
                        kc = k4 * 4 + j
                        P.op("pe", lambda e, k=k, j=j, kc=kc: e.transpose(PS[k][:, j * 128:(j + 1) * 128], XR[:, kc * 128:(kc + 1) * 128], ident[:]),
                             B_R3[2] + [B_const], [B_PS[k]])
                    P.op("dve", lambda e, k=k, k4=k4: e.tensor_tensor(dstR[:, k4 * 4:(k4 + 1) * 4, i * 128:(i + 1) * 128],
                                                                      PS[k][:, :].rearrange("p (a t) -> p a t", a=4),
                                                                      gT[:, k4 * 4:(k4 + 1) * 4].unsqueeze(2).broadcast_to([128, 4, 128]), ALU.mult),
                         [B_PS[k], B_const], dstB)
            for i in range(ng):
                norm_T(i, gmixT, XNT, B_R2, i)

            items = []

            def q_block(blk):
                def fn(b):
                    for j in range(2):
                        pj = blk * 2 + j
                        k = nps()
                        for kc in range(16):
                            P.op("pe", lambda e, k=k, kc=kc, j=j, b=b: e.matmul(PS[k][:, 0:T], WB[b][:, kc, j * 128:(j + 1) * 128], XNT[:, kc, 0:T],
                                                                               start=(kc == 0), stop=(kc == 15)), B_WB[b] + B_R2, [B_PS[k]])
                        P.op("act", lambda e, k=k, pj=pj: e.activation(QT[0:64, 2 * pj, 0:T], PS[k][0:64, 0:T], AF.Copy, scale=0.125), [B_PS[k]], B_R1)
                        P.op("act", lambda e, k=k, pj=pj: e.activation(QT[0:64, 2 * pj + 1, 0:T], PS[k][64:128, 0:T], AF.Copy, scale=0.125), [B_PS[k]], B_R1)
                return fn
            for blk in range(4):
                items.append((full(w_in_v, blk * 256), q_block(blk)))

            def kv_fn(b):
                for i in range(ng):
                    k = nps()
                    for kc in range(16):
                        P.op("pe", lambda e, k=k, kc=kc, i=i, b=b: e.matmul(PS[k][:, 0:256], XNT[:, kc, i * 128:(i + 1) * 128], WB[b][:, kc, :],
                                                                           start=(kc == 0), stop=(kc == 15)), B_WB[b] + B_R2, [B_PS[k]])
                    P.op("act", lambda e, k=k, i=i: e.activation(KTOK[:, i, :], PS[k][:, 0:128], AF.Copy), [B_PS[k]], [B_KTOK])
                    P.op("dve", lambda e, k=k, i=i: e.tensor_copy(VA[:, 1 + i, :, 0:64], PS[k][:, 128:256].rearrange("p (k d) -> p k d", k=2)),
                         [B_PS[k]], [B_VA])
                    k2 = nps()
                    P.op("pe", lambda e, k2=k2, i=i: e.transpose(PS[k2][:, 0:128], KTOK[:, i, :], ident[:]), [B_KTOK, B_const], [B_PS[k2]])
                    P.op("act", lambda e, k2=k2, i=i: e.activation(KT[0:64, 0, 128 + i * 128:256 + i * 128], PS[k2][0:64, 0:128], AF.Copy), [B_PS[k2]], [B_KT])
                    P.op("act", lambda e, k2=k2, i=i: e.activation(KT[0:64, 1, 128 + i * 128:256 + i * 128], PS[k2][64:128, 0:128], AF.Copy), [B_PS[k2]], [B_KT])
            items.append((full(w_in_v, 1024), kv_fn))

            def uv_block(slot, blk):
                def fn(b):
                    for i in range(ng):
                        k = nps()
                        for kc in range(16):
                            P.op("pe", lambda e, k=k, kc=kc, i=i, b=b: e.matmul(PS[k][:, 0:256], XNT[:, kc, i * 128:(i + 1) * 128], WB[b][:, kc, :],
                                                                               start=(kc == 0), stop=(kc == 15)), B_WB[b] + B_R2, [B_PS[k]])
                        P.op("act", lambda e, k=k, i=i: e.activation(R3[:, slot, i * 1024 + blk * 256: i * 1024 + (blk + 1) * 256], PS[k][:, 0:256],
                                                                     AF.Gelu_apprx_tanh), [B_PS[k]], B_R3[slot])
                return fn
            for blk in range(4):
                items.append((full(w_in_v, 1280 + blk * 256), uv_block(0, blk)))
            for blk in range(4):
                items.append((full(w_in_v, 2304 + blk * 256), uv_block(1, blk)))

            def mixers(_):
                for i in range(ng):
                    VNi = R3[:, 1, i * 1024:(i + 1) * 1024]
                    P.op("act", lambda e, i=i, VNi=VNi: e.activation(R4f[:, 0:1024], VNi, AF.Square, accum_out=scol(16 + i)),
                         B_R3[1], [B_R4[0], B_small])
                    rstd_from(16 + i, 24 + i, 1024)
                    P.op("dve", lambda e, i=i, VNi=VNi: e.scalar_tensor_tensor(out=VNi, in0=VNi, scalar=scol(24 + i), in1=sgug[:],
                                                                              op0=ALU.mult, op1=ALU.mult), B_R3[1] + [B_small, B_const], B_R3[1])
                for i, gt in enumerate(tiles):
                    has_prev = sample or gt != 0
                    ncur = 16 if sample else 128
                    pso = [5, 6, 7]
                    kbs = ([0] if has_prev else []) + [1]

                    def st_S(hp):
                        k = nps(0, 5)
                        for hh in range(2):
                            h = 2 * hp + hh
                            kv = h // 8
                            for kb in kbs:
                                nk = 128 if kb == 0 else ncur
                                c0 = i * 128 + kb * 128
                                P.op("pe", lambda e, k=k, hh=hh, kb=kb, nk=nk, c0=c0, kv=kv, h=h, i=i: e.matmul(
                                    PS[k][0:nk, (hh * 2 + kb) * 128:(hh * 2 + kb + 1) * 128], KT[0:64, kv, c0:c0 + nk],
                                    QT[0:64, h, i * 128:(i + 1) * 128], start=True, stop=True), [B_KT] + B_R1, [B_PS[k]])
                        return k

                    def st_chain(hp, k):
                        s = hp % 2
                        for kb in kbs:
                            nk = 128 if kb == 0 else ncur
                            P.op("dve", lambda e, k=k, kb=kb, nk=nk, hp=hp, s=s: e.tensor_tensor(
                                PT[s][0:nk, :, kb, :], PS[k][0:nk, :].rearrange("p (a b q) -> p a b q", a=2, b=2)[:, :, kb, :],
                                biasT[0:nk, 2 * hp:2 * hp + 2, kb, :], ALU.add), [B_PS[k], B_bias], [B_PT[s][kb]])
                        for kb in kbs:
                            nk = 128 if kb == 0 else ncur
                            P.op("act", lambda e, kb=kb, nk=nk, s=s: e.activation(PT[s][0:nk, :, kb, :], PT[s][0:nk, :, kb, :], AF.Exp),
                                 [B_PT[s][kb]], [B_PT[s][kb]])
                        if not sample:
                            P.op("dve", lambda e, s=s: e.memset(PT[s][64:128, :, 1, 0:64], 0.0), [], [B_PT[s][1]])
                            if has_prev:
                                P.op("dve", lambda e, s=s: e.memset(PT[s][0:64, :, 0, 64:128], 0.0), [], [B_PT[s][0]])

                    def st_PV(hp):
                        s = hp % 2
                        for hh in range(2):
                            h = 2 * hp + hh
                            kv = h // 8
                            bk = pso[h // 6]
                            hl = h % 6
                            for n_, kb in enumerate(kbs):
                                nk = 128 if kb == 0 else ncur
                                slot = i + kb
                                P.op("pe", lambda e, bk=bk, hl=hl, s=s, hh=hh, kb=kb, nk=nk, slot=slot, kv=kv, n_=n_, nkb=len(kbs): e.matmul(
                                    PS[bk][:, hl * 65:(hl + 1) * 65], PT[s][0:nk, hh, kb, :], VA[0:nk, slot, kv, :],
                                    start=(n_ == 0), stop=(n_ == nkb - 1)), [B_PT[s][kb], B_VA], [B_PS[bk]])
                    kcur = st_S(0)
                    for hp in range(8):
                        knext = st_S(hp + 1) if hp + 1 < 8 else None
                        st_chain(hp, kcur)
                        st_PV(hp)
                        kcur = knext
                    for b3 in range(3):
                        nh = 6 if b3 < 2 else 4
                        h0 = b3 * 6
                        P.op("dve", lambda e, b3=b3, nh=nh, h0=h0: e.tensor_tensor(
                            DEN[:, h0:h0 + nh], PS[pso[b3]][:, 0:nh * 65].rearrange("p (h c) -> p h c", c=65)[:, :, 64],
                            esink[:, h0:h0 + nh], ALU.add), [B_PS[pso[b3]], B_const], [B_DEN])
                    P.op("dve", lambda e: e.reciprocal(RDEN[:], DEN[:]), [B_DEN], [B_DEN])
                    for h in range(16):
                        bk = pso[h // 6]
                        hl = h % 6
                        P.op("dve", lambda e, bk=bk, hl=hl, h=h, i=i: e.tensor_scalar(
                            R3[:, 2, i * 1024 + h * 64: i * 1024 + (h + 1) * 64], PS[bk][:, hl * 65:hl * 65 + 64], RDEN[:, h:h + 1], None, ALU.mult),
                            [B_PS[bk], B_DEN], B_R3[2])
                for i in range(ng):
                    ks = [nps(), nps()]
                    for g in range(8):
                        k = ks[g // 4]
                        P.op("pe", lambda e, k=k, g=g, i=i: e.matmul(PS[k][:, (g % 4) * 128:(g % 4 + 1) * 128], wmT[:, g, :],
                                                                    R3[:, 1, i * 1024 + g * 128: i * 1024 + (g + 1) * 128], start=True, stop=True),
                             [B_const] + B_R3[1], [B_PS[k]])
                    for g in range(8):
                        k = ks[g // 4]
                        Ug = R3[:, 0, i * 1024 + g * 128: i * 1024 + (g + 1) * 128]
                        P.op("dve", lambda e, k=k, g=g, Ug=Ug: e.scalar_tensor_tensor(out=Ug, in0=PS[k][:, (g % 4) * 128:(g % 4 + 1) * 128],
                                                                                    scalar=bsT[:, g:g + 1], in1=Ug, op0=ALU.add, op1=ALU.mult),
                             [B_PS[k], B_const] + B_R3[0], B_R3[0])
                if sample:
                    P.dma("sp", lambda e: e.dma_start(out=nk_s, in_=KTOK[0:16, 0, :]), B_KTOK, reads=[B_KTOK], is_out=True)
                    P.dma("sp", lambda e: e.dma_start(out=nv_s.rearrange("p (k d) -> p k d", k=2), in_=VA[0:16, 1, :, 0:64]), B_VA, reads=[B_VA], is_out=True)
                    P.dma("sp", lambda e: e.dma_start(out=nsgu, in_=R3[0:16, 1, 0:1024]), B_R3[1][0], reads=B_R3[1], is_out=True)
                elif tiles[-1] == nt - 1:
                    il = ng - 1
                    P.dma("sp", lambda e: e.dma_start(out=nk_p, in_=KTOK[:, il, :]), B_KTOK, reads=[B_KTOK], is_out=True)
                    P.dma("sp", lambda e: e.dma_start(out=nv_p.rearrange("p (k d) -> p k d", k=2), in_=VA[:, 1 + il, :, 0:64]), B_VA, reads=[B_VA], is_out=True)
                for i in range(ng):
                    for src_slot, dst0, dB in ((2, 0, B_R1[0]), (0, 8, B_R1[1])):
                        for f4 in range(2):
                            k = nps()
                            for j in range(4):
                                f = f4 * 4 + j
                                P.op("pe", lambda e, k=k, j=j, f=f, i=i, src_slot=src_slot: e.transpose(
                                    PS[k][:, j * 128:(j + 1) * 128], R3[:, src_slot, i * 1024 + f * 128: i * 1024 + (f + 1) * 128], ident[:]),
                                    B_R3[src_slot] + [B_const], [B_PS[k]])
                            P.op("act", lambda e, k=k, f4=f4, i=i, dst0=dst0: e.activation(
                                R1[:, dst0 + f4 * 4: dst0 + (f4 + 1) * 4, i * 128:(i + 1) * 128], PS[k][:, :].rearrange("p (a t) -> p a t", a=4), AF.Copy),
                                [B_PS[k]], [dB])
            items.append((None, mixers))

            SGA = XG[:, 0:1024]
            SGB = XG[:, 1024:2048]

            def gate_block(which, nb):
                def fn(b):
                    dst = SGA if which == 0 else SGB
                    for i in range(ng):
                        k = nps()
                        for kc in range(16):
                            P.op("pe", lambda e, k=k, kc=kc, i=i, b=b: e.matmul(PS[k][:, 0:256], XNT[:, kc, i * 128:(i + 1) * 128], WB[b][:, kc, :],
                                                                               start=(kc == 0), stop=(kc == 15)), B_WB[b] + B_R2, [B_PS[k]])
                        P.op("act", lambda e, k=k, i=i, dst=dst: e.activation(dst[:, i * 256:(i + 1) * 256], PS[k][:, 0:256], AF.Sigmoid),
                             [B_PS[k]], [B_XG])
                return fn

            def papb_block(nb):
                def fn(b):
                    for i in range(ng):
                        ka = nps()
                        kb_ = nps()
                        for kc in range(8):
                            P.op("pe", lambda e, ka=ka, kc=kc, i=i, b=b: e.matmul(PS[ka][:, 0:256], R1[:, kc, i * 128:(i + 1) * 128], WB[b][:, kc, :],
                                                                                 start=(kc == 0), stop=(kc == 7)), B_WB[b] + B_R1, [B_PS[ka]])
                        for kc in range(8):
                            P.op("pe", lambda e, kb_=kb_, kc=kc, i=i, b=b: e.matmul(PS[kb_][:, 0:256], R1[:, 8 + kc, i * 128:(i + 1) * 128], WB[b][:, 8 + kc, :],
                                                                                   start=(kc == 0), stop=(kc == 7)), B_WB[b] + B_R1, [B_PS[kb_]])
                        sa = SGA[:, i * 256:(i + 1) * 256]
                        sb_ = SGB[:, i * 256:(i + 1) * 256]
                        P.op("dve", lambda e, ka=ka, sa=sa: e.tensor_tensor(sa, sa, PS[ka][:, 0:256], ALU.mult), [B_PS[ka], B_XG], [B_XG])
                        P.op("dve", lambda e, kb_=kb_, sb_=sb_: e.tensor_tensor(sb_, sb_, PS[kb_][:, 0:256], ALU.mult), [B_PS[kb_], B_XG], [B_XG])
                        P.op("dve", lambda e, sa=sa, sb_=sb_: e.tensor_tensor(sa, sa, sb_, ALU.add), [B_XG], [B_XG])
                        k = nps()
                        for j in range(2):
                            P.op("pe", lambda e, k=k, j=j, sa=sa: e.transpose(PS[k][:, j * 128:(j + 1) * 128], sa[:, j * 128:(j + 1) * 128], ident[:]),
                                 [B_XG, B_const], [B_PS[k]])
                        HTv = R4
                        P.op("dve", lambda e, k=k, i=i, HTv=HTv: e.tensor_copy(HTv[:, nb * 2:nb * 2 + 2, i * 128:(i + 1) * 128],
                                                                              PS[k][:, 0:256].rearrange("p (a t) -> p a t", a=2)),
                             [B_PS[k]], B_R4)
                return fn
            for nb in range(8):
                items.append((full(w_in_v, 3328 + nb * 256), gate_block(0, nb)))
                items.append((full(w_in_v, 5376 + nb * 256), gate_block(1, nb)))
                items.append(([(lambda b: WB[b][:, 0:8, :], w_pa_v[:, :, nb * 256:(nb + 1) * 256]),
                               (lambda b: WB[b][:, 8:16, :], w_pb_v[:, :, nb * 256:(nb + 1) * 256])], papb_block(nb)))

            def wout_block(nb):
                def fn(b):
                    HTv = R4
                    for i in range(ng):
                        k = nps()
                        for kc in range(16):
                            P.op("pe", lambda e, k=k, kc=kc, i=i, b=b: e.matmul(PS[k][:, 0:256], HTv[:, kc, i * 128:(i + 1) * 128], WB[b][:, kc, :],
                                                                               start=(kc == 0), stop=(kc == 15)), B_WB[b] + B_R4, [B_PS[k]])
                        xs_ = X[:, i, nb * 256:(nb + 1) * 256]
                        P.op("dve", lambda e, k=k, xs_=xs_: e.tensor_tensor(xs_, xs_, PS[k][:, 0:256], ALU.add), [B_PS[k], B_X[i]], [B_X[i]])
                return fn
            for nb in range(8):
                items.append((full(w_out_v, nb * 256), wout_block(nb)))

            def peer_norm(_):
                for i in range(ng):
                    norm_T(i, gffnT, XNTb, B_R2, 32 + i)
            items.append((None, peer_norm))

            def wq_block(blk):
                def fn(b):
                    for j in range(2):
                        c = blk * 2 + j
                        k = nps()
                        for kc in range(16):
                            P.op("pe", lambda e, k=k, kc=kc, j=j, b=b: e.matmul(PS[k][:, 0:T], WB[b][:, kc, j * 128:(j + 1) * 128], XNTb[:, kc, 0:T],
                                                                               start=(kc == 0), stop=(kc == 15)), B_WB[b] + B_R2, [B_PS[k]])
                        P.op("act", lambda e, k=k, c=c: e.activation(QPT[:, c, 0:T], PS[k][:, 0:T], AF.Copy), [B_PS[k]], [B_QPT])
                return fn
            for blk in range(8):
                items.append((full(w_q_v, blk * 256), wq_block(blk)))

            if first_group[0] and not sample and "nobias" not in dbg:
                ni = []
                for idx, it in enumerate(items):
                    ni.append(it)
                    if idx < 8:
                        ni.append((None, bias_piece(idx)))
                items = ni
            if "it=" in dbg:
                items = items[:int(dbg.split("it=")[1].split(",")[0])]
            run_items(items)

            for i, gt in enumerate(tiles):
                if "nopeer" in dbg:
                    break
                SC = R3[:, 0, :].rearrange("p (a n) -> p a n", a=16)
                TMP = R3[:, 1, :].rearrange("p (a n) -> p a n", a=16)
                CAND = R3[:, 2, :]
                sks = [nps(0, 4) for _ in range(4)]
                for hp in range(16):
                    k = sks[hp // 4]
                    P.op("pe", lambda e, k=k, hp=hp, i=i: e.matmul(PS[k][:, (hp % 4) * 128:(hp % 4 + 1) * 128], QPT[:, hp, i * 128:(i + 1) * 128],
                                                                  skT[:, hp, :], start=True, stop=True), [B_QPT, B_const], [B_PS[k]])
                for q4 in range(4):
                    P.op("act", lambda e, q4=q4: e.activation(R3[:, 0, q4 * 512:(q4 + 1) * 512], PS[sks[q4]][:, :], AF.Copy), [B_PS[sks[q4]]], B_R3[0])
                for hp in range(16):
                    P.op("dve", lambda e, hp=hp: e.max(out=SV[:, hp, 0:8], in_=SC[:, hp, :]), B_R3[0], [B_SV])
                for hp in range(16):
                    P.op("dve", lambda e, hp=hp: e.match_replace(out=TMP[:, hp, :], in_to_replace=SV[:, hp, 0:8], in_values=SC[:, hp, :], imm_value=-1e30),
                         B_R3[0] + [B_SV], B_R3[1])
                for hp in range(16):
                    P.op("dve", lambda e, hp=hp: e.max(out=SV[:, hp, 8:16], in_=TMP[:, hp, :]), B_R3[1], [B_SV])
                for hp in range(16):
                    for o in (0, 8):
                        P.op("dve", lambda e, hp=hp, o=o: e.max_index(out=SI[:, hp, o:o + 8], in_max=SV[:, hp, o:o + 8], in_values=SC[:, hp, :]),
                             B_R3[0] + [B_SV], [B_SI])
                P.op("dve", lambda e: e.tensor_copy(SIF[:], SI[:]), [B_SI], [B_SIF])
                sv4 = SV[:, :, :].rearrange("p (h two) k -> p h two k", two=2)
                sif4 = SIF[:, :, :].rearrange("p (h two) k -> p h two k", two=2)
                CAND4 = CAND.rearrange("p (h a b) -> p h a b", h=8, a=16)
                P.op("dve", lambda e: e.tensor_tensor(CAND4, sv4[:, :, 0, :].unsqueeze(3).broadcast_to([128, 8, 16, 16]),
                                                      sv4[:, :, 1, :].unsqueeze(2).broadcast_to([128, 8, 16, 16]), ALU.add), [B_SV], B_R3[2])
                CAND2 = CAND.rearrange("p (h m) -> p h m", h=8)
                TMPC = R3[:, 1, :].rearrange("p (h m) -> p h m", h=8)
                for h in range(8):
                    P.op("dve", lambda e, h=h: e.max(out=CV[:, h, 0:8], in_=CAND2[:, h, :]), B_R3[2], [B_CV])
                for h in range(8):
                    P.op("dve", lambda e, h=h: e.match_replace(out=TMPC[:, h, :], in_to_replace=CV[:, h, 0:8], in_values=CAND2[:, h, :], imm_value=-1e30),
                         B_R3[2] + [B_CV], B_R3[1])
                for h in range(8):
                    P.op("dve", lambda e, h=h: e.max(out=CV[:, h, 8:16], in_=TMPC[:, h, :]), B_R3[1], [B_CV])
                for h in range(8):
                    for o in (0, 8):
                        P.op("dve", lambda e, h=h, o=o: e.max_index(out=CI[:, h, o:o + 8], in_max=CV[:, h, o:o + 8], in_values=CAND2[:, h, :]),
                             B_R3[2] + [B_CV], [B_CI])
                CIf = CI[:, :, :].rearrange("p h k -> p (h k)")
                P.op("dve", lambda e: e.tensor_scalar(IK[:], CIf, shc[:, 0:1], None, ALU.logical_shift_right), [B_CI, B_const], [B_IK])
                P.op("dve", lambda e: e.tensor_scalar(JK[:], CIf, shc[:, 1:2], None, ALU.bitwise_and), [B_CI, B_const], [B_IK])
                P.op("dve", lambda e: e.tensor_copy(IKF[:, :, :].rearrange("p h k -> p (h k)"), IK[:]), [B_IK], [B_IKF])
                P.op("dve", lambda e: e.tensor_copy(JKF[:, :, :].rearrange("p h k -> p (h k)"), JK[:]), [B_IK], [B_IKF])
                io4 = iota16[:, :].unsqueeze(1).unsqueeze(1).broadcast_to([128, 8, 16, 16])
                for w_, (KF, SEL) in enumerate(((IKF, SEL0), (JKF, SEL1))):
                    E4w = R2f[:, w_ * 2048:(w_ + 1) * 2048].rearrange("p (h a b) -> p h a b", h=8, a=16)
                    E4r = E4w
                    P.op("dve", lambda e, KF=KF, E4w=E4w: e.tensor_tensor(E4w, KF[:, :, :].unsqueeze(3).broadcast_to([128, 8, 16, 16]), io4, ALU.is_equal),
                         [B_IKF, B_const], [B_R2[w_]])
                    P.op("dve", lambda e, E4w=E4w, E4r=E4r, w_=w_: e.tensor_tensor(E4w, E4r, sif4[:, :, w_, :].unsqueeze(2).broadcast_to([128, 8, 16, 16]), ALU.mult),
                         [B_R2[w_], B_SIF], [B_R2[w_]])
                    P.op("dve", lambda e, SEL=SEL, E4r=E4r: e.tensor_reduce(SEL[:], E4r, AX.X, ALU.add), [B_R2[w_]], [B_SEL])
                P.op("dve", lambda e: e.scalar_tensor_tensor(out=EIF[:], in0=SEL0[:, :, :].rearrange("p h k -> p (h k)"), scalar=128.0,
                                                             in1=SEL1[:, :, :].rearrange("p h k -> p (h k)"), op0=ALU.mult, op1=ALU.add), [B_SEL], [B_SEL])
                P.op("dve", lambda e: e.tensor_copy(EIDX[:], EIF[:]), [B_SEL], [B_EIDX])
                P.op("dve", lambda e: e.tensor_tensor(EW[:], CV[:], CV[:, :, 0:1].broadcast_to([128, 8, 16]), ALU.subtract), [B_CV], [B_GW])
                P.op("act", lambda e: e.activation(EW[:], EW[:], AF.Exp), [B_GW], [B_GW])
                P.op("dve", lambda e: e.tensor_reduce(SUMW[:], EW[:], AX.X, ALU.add), [B_GW], [B_GW])
                P.op("dve", lambda e: e.reciprocal(RW[:], SUMW[:]), [B_GW], [B_GW])
                P.op("dve", lambda e: e.tensor_tensor(GW[:], EW[:], RW[:, :].unsqueeze(2).broadcast_to([128, 8, 16]), ALU.mult), [B_GW], [B_GW])
                GWf = GW[:, :, :].rearrange("p h k -> p (h k)")
                P.op("dve", lambda e, i=i: e.scalar_tensor_tensor(out=XG[:], in0=X[:, i, :], scalar=scol(40 + i), in1=gffn[:], op0=ALU.mult, op1=ALU.mult),
                     [B_X[i], B_small, B_const], [B_XG])
                acc = [4, 5, 6, 7]
                NB = 9
                JUNK = R4f[0:nr, 0:2048]

                def UV(s):
                    if s < 3:
                        return R3[:, s, :].bitcast(BF16)[0:nr, :]
                    if s < 5:
                        return VEB[:, 2 * (s - 3):2 * (s - 3) + 2, :].rearrange("p a b -> p (a b)")[0:nr, :]
                    if s < 7:
                        return R2f[:, (s - 5) * 2048:(s - 4) * 2048].bitcast(BF16)[0:nr, :]
                    return WBraw[s - 7][:, :].bitcast(BF16)[0:nr, :]

                def BUV(s):
                    if s < 3:
                        return B_R3[s]
                    if s < 5:
                        return [B_VE[2 * (s - 3)], B_VE[2 * (s - 3) + 1]]
                    if s < 7:
                        return [B_R2[s - 5]]
                    return B_WB[s - 7]
                LA = NB - 2

                def gather(cg):
                    sg_ = cg % NB
                    P.dma("pool", lambda e, cg=cg, sg_=sg_: e.indirect_dma_start(out=UV(sg_), out_offset=None, in_=euvb,
                                                                                 in_offset=bass.IndirectOffsetOnAxis(ap=EIDX[0:nr, cg:cg + 1], axis=0)),
                          BUV(sg_)[0], reads=[B_EIDX, B_TUV], writes=BUV(sg_))
                if "nogather" not in dbg:
                    for cg in range(LA):
                        gather(cg)
                for c in range(129 if "nogather" not in dbg else 0):
                    if c + LA < 128:
                        gather(c + LA)
                    if c < 128:
                        sb_ = c % NB
                        p4 = c % 4
                        P.op("dve", lambda e, c=c, sb_=sb_: e.scalar_tensor_tensor(out=JUNK, in0=UV(sb_)[:, 0:2048], scalar=1.0, in1=XG[0:nr, :],
                                                                                   op0=ALU.mult, op1=ALU.mult, accum_out=AA[0:nr, c:c + 1]),
                             BUV(sb_) + [B_XG], [B_R4[0], B_AA[p4]])
                        P.op("act", lambda e, c=c: e.activation(AGL[0:nr, c:c + 1], AA[0:nr, c:c + 1], AF.Gelu_apprx_tanh), [B_AA[p4]], [B_AGL[p4]])
                    if c >= 1:
                        c1 = c - 1
                        sb_ = c1 % NB
                        p4 = c1 % 4
                        P.op("act", lambda e, c1=c1: e.activation(HW[0:nr, c1:c1 + 1], AGL[0:nr, c1:c1 + 1], AF.Copy, scale=GWf[0:nr, c1:c1 + 1]),
                             [B_AGL[p4], B_GW], [B_HW[p4]])
                        P.op("act", lambda e, c1=c1, p4=p4: e.activation(DIAG[p4][0:nr, :], ident[0:nr, :], AF.Copy, scale=HW[0:nr, c1:c1 + 1]),
                             [B_HW[p4], B_const], [B_DIAG[p4]])
                        for j in range(4):
                            P.op("pe", lambda e, j=j, sb_=sb_, p4=p4, c1=c1: e.matmul(PS[acc[j]][:, :], DIAG[p4][0:nr, :], UV(sb_)[:, 2048 + j * 512:2048 + (j + 1) * 512],
                                                                                  start=(c1 == 0), stop=(c1 == 127)), [B_DIAG[p4]] + BUV(sb_), [B_PS[acc[j]]])
                for j in range(4):
                    xs_ = X[0:nr, i, j * 512:(j + 1) * 512]
                    P.op("dve", lambda e, j=j, xs_=xs_: e.tensor_tensor(xs_, xs_, PS[acc[j]][0:nr, :], ALU.add), [B_PS[acc[j]], B_X[i]], [B_X[i]])
                P.op("act", lambda e, i=i: e.activation(R4f[0:nr, 2048:4096], X[0:nr, i, :], AF.Square, accum_out=small[0:nr, 48 + i:49 + i]),
                     [B_X[i]], [B_R4[1], B_small])
                rstd_from(48 + i, 56 + i, D, rows=nr)
                P.op("dve", lambda e, i=i: e.scalar_tensor_tensor(out=X[0:nr, i, :], in0=X[0:nr, i, :], scalar=small[0:nr, 56 + i:57 + i], in1=gfin[0:nr, :],
                                                                  op0=ALU.mult, op1=ALU.mult), [B_X[i], B_small, B_const], [B_X[i]])
                if sample:
                    P.dma("sp", lambda e, i=i: e.dma_start(out=y_s, in_=X[0:16, i, :]), B_X[i], reads=[B_X[i]], is_out=True)
                else:
                    P.dma("sp", lambda e, i=i, gt=gt: e.dma_start(out=y_p[gt * 128:(gt + 1) * 128, :], in_=X[:, i, :]), B_X[i], reads=[B_X[i]], is_out=True)

        for g0 in range(0, nt, G):
            do_group(list(range(g0, g0 + G)), False)
            first_group[0] = False
        if "nosample" not in dbg:
            do_group([0], True)
        P.finish()
        P.emit()
    return nc


def _t5_bucket_np(rel):
    try:
        import jax
        import jax.numpy as jnp
        with jax.default_device(jax.devices("cpu")[0]):
            r = jnp.asarray(rel, dtype=jnp.int32)
            half = 16
            max_exact = 8
            ret = jnp.where(r > 0, half, 0)
            n = jnp.abs(r)
            nf = jnp.maximum(n, 1).astype(jnp.float32)
            large = max_exact + (jnp.log(nf / max_exact) / math.log(128 / max_exact) * (half - max_exact)).astype(jnp.int32)
            large = jnp.minimum(large, half - 1)
            return np.asarray(ret + jnp.where(n < max_exact, n, large))
    except Exception:
        r = np.asarray(rel, dtype=np.int32)
        ret = np.where(r > 0, 16, 0)
        n = np.abs(r)
        nf = np.maximum(n, 1).astype(np.float32)
        large = 8 + (np.log(nf / np.float32(8)) / np.float32(math.log(16.0)) * np.float32(8)).astype(np.int32)
        large = np.minimum(large, 15)
        return ret + np.where(n < 8, n, large)


_NC_CACHE = {}


def kernel(x_prompt, x_sample, cache_k_swa, cache_v_swa, norm_mix_g, w_in, sgu_norm_g, sgu_w_s, sgu_b_s,
           attn_sinks, rel_bias_table, w_branch_attn, w_branch_sgu, w_out, norm_ffn_g, peer_w_query,
           peer_sub_keys, peer_expert_u, peer_expert_v, norm_final_g):
    f = lambda a: np.ascontiguousarray(np.asarray(a), dtype=np.float32)
    if "nc" not in _NC_CACHE:
        _NC_CACHE["nc"] = build_program()
    nc = _NC_CACHE["nc"]
    kb = np.arange(2)[:, None, None]; kk = np.arange(128)[None, :, None]; qq = np.arange(128)[None, None, :]
    rel = (kb - 1) * 128 + kk - qq
    bkt = _t5_bucket_np(rel).reshape(-1)
    oh = np.zeros((32, 32768), np.float32)
    oh[bkt, np.arange(32768)] = 1.0
    s_i = np.arange(128)[:, None, None]; t_i = np.arange(128)[None, None, :]
    trilT = np.ascontiguousarray(np.broadcast_to((t_i >= s_i), (128, 8, 128))).astype(np.float32)
    shared = dict(
        w_in=f(w_in[0]), w_pa=f(w_branch_attn[0]), w_pb=f(w_branch_sgu[0]), w_out=f(w_out[0]), w_q=f(peer_w_query[0]),
        eu=f(peer_expert_u[0]), ev=f(peer_expert_v[0]),
        gmixT=f(np.asarray(norm_mix_g[0]).reshape(16, 128).T), gffnT=f(np.asarray(norm_ffn_g[0]).reshape(16, 128).T),
        gffn_bc=f(np.broadcast_to(np.asarray(norm_ffn_g[0])[None, :], (128, D))),
        gfin_bc=f(np.broadcast_to(np.asarray(norm_final_g)[None, :], (128, D))),
        sgug_bc=f(np.broadcast_to(np.asarray(sgu_norm_g[0])[None, :], (128, 1024))),
        wsT=f(np.asarray(sgu_w_s[0]).transpose(2, 0, 1)), trilT=trilT, bsT=f(np.asarray(sgu_b_s[0]).T),
        sinks_bc=f(np.broadcast_to(np.asarray(attn_sinks[0])[None, :], (128, 16))), table=f(rel_bias_table), oh=oh,
        skT=f(np.asarray(peer_sub_keys[0]).reshape(16, 128, 128).transpose(2, 0, 1)),
        ident=np.eye(128, dtype=np.float32), iota16=f(np.broadcast_to(np.arange(16, dtype=np.float32)[None, :], (128, 16))),
        shc=np.ascontiguousarray(np.broadcast_to(np.array([[4, 15]], np.uint32), (128, 2))),
    )
    xpn = np.asarray(x_prompt); xsn = np.asarray(x_sample); ckn = np.asarray(cache_k_swa); cvn = np.asarray(cache_v_swa)
    in_maps = []
    for c in range(8):
        m = dict(shared)
        m["xp"] = f(xpn[c]); m["xs"] = f(xsn[c])
        m["ck"] = f(ckn[0, c].reshape(128, 128)); m["cv"] = f(cvn[0, c].reshape(128, 128))
        in_maps.append(m)
    res = run_bass_kernel_spmd(nc, in_maps, core_ids=list(range(8)))
    r = res.results
    y_prompt = np.stack([r[c]["y_p"] for c in range(8)]).astype(np.float32)
    y_sample = np.stack([r[c]["y_s"] for c in range(8)]).astype(np.float32)
    nk_p = np.stack([r[c]["nk_p"].reshape(128, 2, 64) for c in range(8)])[None].astype(np.float32)
    nv_p = np.stack([r[c]["nv_p"].reshape(128, 2, 64) for c in range(8)])[None].astype(np.float32)
    nk_s = np.stack([r[c]["nk_s"].reshape(16, 2, 64) for c in range(8)])[None].astype(np.float32)
    nv_s = np.stack([r[c]["nv_s"].reshape(16, 2, 64) for c in range(8)])[None].astype(np.float32)
    nsg = np.stack([r[c]["nsgu"] for c in range(8)])[None].astype(np.float32)
    return (y_prompt, y_sample, nk_p, nv_p, nk_s, nv_s, nsg)
```
